# Optimizing a Trainium2 kernel written in Bass

```python
import math
import jax, jax.numpy as jnp
from jax import lax
import numpy as np

D_MODEL = 1024
BATCH = 16
SEQ = 2048
DEPTH = 2

DA_HEADS = 8
DA_HEAD_DIM = 64
DA_WIDTH = DA_HEADS * 2 * DA_HEAD_DIM
SSD_WIDTH = D_MODEL
SSD_HEAD_DIM = 64
SSD_HEADS = SSD_WIDTH // SSD_HEAD_DIM
SSD_GROUPS = 2
SSD_STATE = 128
SSD_CONV = 4
SSD_CHUNK = 128
SSD_CONV_DIM = SSD_WIDTH + 2 * SSD_GROUPS * SSD_STATE
SW_HEADS = 16
SW_KV_HEADS = 2
SW_GQ = SW_HEADS // SW_KV_HEADS
SW_HEAD_DIM = 64
SW_WIDTH = SW_HEADS * SW_HEAD_DIM
SW_KV_WIDTH = SW_KV_HEADS * SW_HEAD_DIM
WINDOW = 128
Q_BLOCK = 128
EPS = 1e-5

N_EVEN = (DEPTH + 1) // 2
N_ODD = DEPTH // 2
EVEN_IN = 4 * DA_WIDTH + SSD_WIDTH + SSD_CONV_DIM + SSD_HEADS
EVEN_SPLITS = [DA_WIDTH, 2 * DA_WIDTH, 3 * DA_WIDTH, 4 * DA_WIDTH,
               4 * DA_WIDTH + SSD_WIDTH, 4 * DA_WIDTH + SSD_WIDTH + SSD_CONV_DIM]
ODD_IN = 2 * SW_WIDTH + 2 * SW_KV_WIDTH
ODD_SPLITS = [SW_WIDTH, SW_WIDTH + SW_KV_WIDTH, SW_WIDTH + 2 * SW_KV_WIDTH]

kernel_name = "hybrid_diffattn_ssd_swa_block"


def rmsnorm(x, w):
    xf = x.astype(jnp.float32)
    y = xf * lax.rsqrt(jnp.mean(xf * xf, axis=-1, keepdims=True) + EPS)
    return (y * w.astype(jnp.float32)).astype(x.dtype)


def alibi_slopes(n):
    return jnp.exp2(-8.0 * jnp.arange(1, n + 1, dtype=jnp.float32) / n)


def diff_attention(q, k, v, lam, slopes):
    b, s = q.shape[:2]
    nb = s // Q_BLOCK
    scale = DA_HEAD_DIM ** -0.5
    qb = q.reshape(b, nb, Q_BLOCK, DA_HEADS, 2, DA_HEAD_DIM).transpose(1, 0, 2, 3, 4, 5)
    kpos = jnp.arange(s)

    def block(args):
        qi, n = args
        qpos = n * Q_BLOCK + jnp.arange(Q_BLOCK)
        dist = qpos[:, None] - kpos[None, :]
        logits = jnp.einsum('bqhmd,bshmd->bhmqs', qi, k).astype(jnp.float32) * scale
        logits = logits - slopes[None, :, None, None, None] * dist.astype(jnp.float32)[None, None, None]
        logits = jnp.where((dist >= 0)[None, None, None], logits, -jnp.inf)
        p = jax.nn.softmax(logits, axis=-1)
        attn = p[:, :, 0] - lam * p[:, :, 1]
        return jnp.einsum('bhqs,bshe->bqhe', attn.astype(v.dtype), v)

    out = lax.map(block, (qb, jnp.arange(nb)))
    return out.transpose(1, 0, 2, 3, 4).reshape(b, s, DA_HEADS, 2 * DA_HEAD_DIM)


def causal_dwconv(x, w, bias):
    c = x.shape[-1]
    y = lax.conv_general_dilated(x, w[:, None, :].astype(x.dtype), window_strides=(1,),
                                 padding=[(SSD_CONV - 1, 0)],
                                 dimension_numbers=('NWC', 'WIO', 'NWC'),
                                 feature_group_count=c)
    return y + bias.astype(x.dtype)


def segsum(a):
    cs = jnp.cumsum(a, axis=-1)
    diff = cs[..., :, None] - cs[..., None, :]
    n = a.shape[-1]
    mask = jnp.tril(jnp.ones((n, n), dtype=bool))
    return jnp.where(mask, diff, -jnp.inf)


def ssd_scan(xh, dt, A, bm, cm):
    b, s = xh.shape[:2]
    nc = s // SSD_CHUNK
    hpg = SSD_HEADS // SSD_GROUPS
    dtype = xh.dtype
    xdt = (xh * dt[..., None].astype(dtype)).reshape(b, nc, SSD_CHUNK, SSD_GROUPS, hpg, SSD_HEAD_DIM)
    bc = bm.reshape(b, nc, SSD_CHUNK, SSD_GROUPS, SSD_STATE)
    cc = cm.reshape(b, nc, SSD_CHUNK, SSD_GROUPS, SSD_STATE)
    dA = (dt * A).reshape(b, nc, SSD_CHUNK, SSD_GROUPS, hpg).transpose(0, 1, 3, 4, 2)
    cs = jnp.cumsum(dA, axis=-1)
    lmat = jnp.exp(segsum(dA)).astype(dtype)
    cb = jnp.einsum('bclgn,bcsgn->bcgls', cc, bc)
    y_diag = jnp.einsum('bcgjls,bcsgjp->bclgjp', cb[:, :, :, None] * lmat, xdt)
    decay_states = jnp.exp(cs[..., -1:] - cs).astype(dtype)
    states = jnp.einsum('bclgn,bcgjl,bclgjp->bcgjpn', bc, decay_states, xdt)
    chunk_decay = jnp.exp(cs[..., -1]).astype(dtype)

    def step(h, inp):
        st, dec = inp
        return h * dec[..., None, None] + st, h

    h0 = jnp.zeros((b, SSD_GROUPS, hpg, SSD_HEAD_DIM, SSD_STATE), dtype)
    _, prev = lax.scan(step, h0, (states.transpose(1, 0, 2, 3, 4, 5), chunk_decay.transpose(1, 0, 2, 3)))
    prev = prev.transpose(1, 0, 2, 3, 4, 5)
    y_off = jnp.einsum('bclgn,bcgjpn,bcgjl->bclgjp', cc, prev, jnp.exp(cs).astype(dtype))
    return (y_diag + y_off).reshape(b, s, SSD_HEADS, SSD_HEAD_DIM)


def sliding_window_attention(q, k, v, sinks, slopes):
    b, s = q.shape[:2]
    nb = s // WINDOW
    scale = SW_HEAD_DIM ** -0.5
    qb = q.reshape(b, nb, WINDOW, SW_KV_HEADS, SW_GQ, SW_HEAD_DIM)
    kb = k.reshape(b, nb, WINDOW, SW_KV_HEADS, SW_HEAD_DIM)
    vb = v.reshape(b, nb, WINDOW, SW_KV_HEADS, SW_HEAD_DIM)
    pad = ((0, 0), (1, 0), (0, 0), (0, 0), (0, 0))
    kk = jnp.concatenate([jnp.pad(kb, pad)[:, :-1], kb], axis=2)
    vv = jnp.concatenate([jnp.pad(vb, pad)[:, :-1], vb], axis=2)
    logits = jnp.einsum('bnqkgd,bnskd->bnkgqs', qb, kk).astype(jnp.float32) * scale
    qrel = jnp.arange(WINDOW) + WINDOW
    krel = jnp.arange(2 * WINDOW)
    dist = qrel[:, None] - krel[None, :]
    key_abs = (jnp.arange(nb) * WINDOW - WINDOW)[:, None] + krel[None, :]
    valid = ((dist >= 0) & (dist < WINDOW))[None] & (key_abs >= 0)[:, None, :]
    sl = slopes.reshape(SW_KV_HEADS, SW_GQ)[None, None, :, :, None, None]
    logits = logits - sl * dist.astype(jnp.float32)
    logits = jnp.where(valid[None, :, None, None], logits, -jnp.inf)
    sink = jnp.broadcast_to(sinks.astype(jnp.float32).reshape(SW_KV_HEADS, SW_GQ)[None, None, :, :, None, None],
                            logits.shape[:-1] + (1,))
    p = jax.nn.softmax(jnp.concatenate([logits, sink], axis=-1), axis=-1)[..., :-1]
    out = jnp.einsum('bnkgqs,bnskd->bnqkgd', p.astype(v.dtype), vv)
    return out.reshape(b, s, SW_WIDTH)


def even_layer(h, w_in, conv_w, conv_b, dt_bias, a_log, d_skip, ssd_norm_w,
               lq1, lk1, lq2, lk2, subln_w, w_out, lambda_init):
    b, s, _ = h.shape
    proj = h @ w_in
    q, k, v, g_a, z, xbc, dt_raw = jnp.split(proj, EVEN_SPLITS, axis=-1)
    q = q.reshape(b, s, DA_HEADS, 2, DA_HEAD_DIM)
    k = k.reshape(b, s, DA_HEADS, 2, DA_HEAD_DIM)
    v = v.reshape(b, s, DA_HEADS, 2 * DA_HEAD_DIM)
    f32 = jnp.float32
    lam = (jnp.exp(jnp.sum(lq1.astype(f32) * lk1.astype(f32)))
           - jnp.exp(jnp.sum(lq2.astype(f32) * lk2.astype(f32))) + lambda_init)
    attn = diff_attention(q, k, v, lam, alibi_slopes(DA_HEADS))
    attn = rmsnorm(attn, subln_w) * (1.0 - lambda_init)
    y_a = attn.reshape(b, s, DA_WIDTH) * jax.nn.silu(g_a)
    xbc = jax.nn.silu(causal_dwconv(xbc, conv_w, conv_b))
    xs, bm, cm = jnp.split(xbc, [SSD_WIDTH, SSD_WIDTH + SSD_GROUPS * SSD_STATE], axis=-1)
    xh = xs.reshape(b, s, SSD_HEADS, SSD_HEAD_DIM)
    dt = jax.nn.softplus(dt_raw.astype(f32) + dt_bias.astype(f32))
    A = -jnp.exp(a_log.astype(f32))
    y = ssd_scan(xh, dt, A, bm.reshape(b, s, SSD_GROUPS, SSD_STATE), cm.reshape(b, s, SSD_GROUPS, SSD_STATE))
    y = y + xh * d_skip[:, None].astype(xh.dtype)
    y = y.reshape(b, s, SSD_WIDTH) * jax.nn.silu(z)
    y_b = rmsnorm(y.reshape(b, s, SSD_GROUPS, SSD_WIDTH // SSD_GROUPS),
                  ssd_norm_w.reshape(SSD_GROUPS, SSD_WIDTH // SSD_GROUPS)).reshape(b, s, SSD_WIDTH)
    return jnp.concatenate([y_a, y_b], axis=-1) @ w_out


def odd_layer(h, w_in, b_in, sinks, w_out):
    b, s, _ = h.shape
    proj = h @ w_in + b_in
    q, k, v, g = jnp.split(proj, ODD_SPLITS, axis=-1)
    q = q.reshape(b, s, SW_KV_HEADS, SW_GQ, SW_HEAD_DIM)
    k = k.reshape(b, s, SW_KV_HEADS, SW_HEAD_DIM)
    v = v.reshape(b, s, SW_KV_HEADS, SW_HEAD_DIM)
    o = sliding_window_attention(q, k, v, sinks, alibi_slopes(SW_HEADS))
    return (o * jax.nn.silu(g)) @ w_out


def setup_inputs(seed: int = 0) -> dict:
    key = jax.random.key(seed)
    ks = jax.random.split(key, 24)
    f32 = jnp.float32

    def nrm(k, shape, scale):
        return jax.random.normal(k, shape, f32) * scale

    dt0 = jnp.exp(jax.random.uniform(ks[5], (N_EVEN, SSD_HEADS), f32, math.log(1e-3), math.log(1e-1)))
    return {
        "x": nrm(ks[0], (BATCH, SEQ, D_MODEL), 1.0),
        "norm_a": 1.0 + nrm(ks[1], (N_EVEN, D_MODEL), 0.02),
        "w_in_a": nrm(ks[2], (N_EVEN, D_MODEL, EVEN_IN), D_MODEL ** -0.5),
        "conv_w_a": nrm(ks[3], (N_EVEN, SSD_CONV, SSD_CONV_DIM), SSD_CONV ** -0.5),
        "conv_b_a": nrm(ks[4], (N_EVEN, SSD_CONV_DIM), 0.02),
        "dt_bias_a": dt0 + jnp.log(-jnp.expm1(-dt0)),
        "a_log_a": jnp.log(jax.random.uniform(ks[6], (N_EVEN, SSD_HEADS), f32, 1.0, 16.0)),
        "d_skip_a": 1.0 + nrm(ks[7], (N_EVEN, SSD_HEADS), 0.1),
        "ssd_norm_a": 1.0 + nrm(ks[8], (N_EVEN, SSD_WIDTH), 0.02),
        "lambda_q1_a": nrm(ks[9], (N_EVEN, DA_HEAD_DIM), 0.1),
        "lambda_k1_a": nrm(ks[10], (N_EVEN, DA_HEAD_DIM), 0.1),
        "lambda_q2_a": nrm(ks[11], (N_EVEN, DA_HEAD_DIM), 0.1),
        "lambda_k2_a": nrm(ks[12], (N_EVEN, DA_HEAD_DIM), 0.1),
        "subln_a": 1.0 + nrm(ks[13], (N_EVEN, 2 * DA_HEAD_DIM), 0.02),
        "w_out_a": nrm(ks[14], (N_EVEN, DA_WIDTH + SSD_WIDTH, D_MODEL), (DA_WIDTH + SSD_WIDTH) ** -0.5),
        "norm_c": 1.0 + nrm(ks[15], (N_ODD, D_MODEL), 0.02),
        "w_in_c": nrm(ks[16], (N_ODD, D_MODEL, ODD_IN), D_MODEL ** -0.5),
        "b_in_c": nrm(ks[17], (N_ODD, ODD_IN), 0.02),
        "sinks_c": nrm(ks[18], (N_ODD, SW_HEADS), 1.0),
        "w_out_c": nrm(ks[19], (N_ODD, SW_WIDTH, D_MODEL), SW_WIDTH ** -0.5),
        "final_norm": 1.0 + nrm(ks[20], (D_MODEL,), 0.02),
    }


def reference(x, norm_a, w_in_a, conv_w_a, conv_b_a, dt_bias_a, a_log_a, d_skip_a, ssd_norm_a,
              lambda_q1_a, lambda_k1_a, lambda_q2_a, lambda_k2_a, subln_a, w_out_a,
              norm_c, w_in_c, b_in_c, sinks_c, w_out_c, final_norm):
    for i in range(DEPTH):
        j = i // 2
        if i % 2 == 0:
            lambda_init = 0.8 - 0.6 * math.exp(-0.3 * i)
            x = x + even_layer(rmsnorm(x, norm_a[j]), w_in_a[j], conv_w_a[j], conv_b_a[j],
                               dt_bias_a[j], a_log_a[j], d_skip_a[j], ssd_norm_a[j],
                               lambda_q1_a[j], lambda_k1_a[j], lambda_q2_a[j], lambda_k2_a[j],
                               subln_a[j], w_out_a[j], lambda_init)
        else:
            x = x + odd_layer(rmsnorm(x, norm_c[j]), w_in_c[j], b_in_c[j], sinks_c[j], w_out_c[j])
    return rmsnorm(x, final_norm)
```

```python
import contextlib
import math
import numpy as np
import ml_dtypes
import concourse.bass as bass
import concourse.mybir as mybir
from concourse.bass_utils import run_bass_kernel_spmd

F32 = mybir.dt.float32
BF16 = mybir.dt.bfloat16
AF = mybir.ActivationFunctionType
ALU = mybir.AluOpType
AX = mybir.AxisListType

D = 1024
EVEN_IN = 6672
ODD_IN = 2304
EPS = 1e-5
NEGBIG = -30000.0
LAMBDA_INIT = 0.8 - 0.6 * math.exp(-0.3 * 0)
NM = 19


ENGS = ("pe", "act", "dve", "pool", "sp")


class Prog:
    def __init__(self, nc, same_engine_sync=True):
        self.nc = nc
        self.ops = {e: [] for e in ENGS}
        self.last_write = {}
        self.reads_since = {}
        self.known = {e: {} for e in ENGS}
        self.dma_count = {}
        self.same_engine_sync = same_engine_sync
        self.latest = {}

    def _add(self, eng, fn, reads, writes, dma_key=None, extra_deps=()):
        writes = tuple(writes) + tuple(r for r in reads if isinstance(r, str) and (r.startswith("bank") or (r[0] == "O" and r[1:].isdigit())))
        lst = self.ops[eng]
        idx = len(lst)
        deps = {}

        def need(tok):
            if tok is None:
                return
            k, i, clk = tok
            if deps.get(k, (-1, None))[0] < i:
                deps[k] = (i, clk)

        for r in reads:
            need(self.last_write.get(r))
        for w in writes:
            need(self.last_write.get(w))
            for tok in self.reads_since.get(w, {}).values():
                need(tok)
        for tok in extra_deps:
            need(tok)
        known = self.known[eng]
        waits = []
        for k, (i, clk) in deps.items():
            if k == eng and not (self.same_engine_sync and eng != "pe" and eng != "sp"):
                continue
            if known.get(k, -1) >= i:
                continue
            waits.append((k, i))
        for k, (i, clk) in deps.items():
            if k == eng and (k, i) not in waits:
                continue
            if known.get(k, -1) < i:
                known[k] = i
            for kk, vv in clk.items():
                if known.get(kk, -1) < vv:
                    known[kk] = vv
        op = dict(fn=fn, waits=waits, flag=False, dma_key=dma_key)
        lst.append(op)
        if dma_key is not None:
            n = self.dma_count.get(dma_key, 0) + 1
            self.dma_count[dma_key] = n
            tok = (("D", dma_key), n, dict(known))
        else:
            clk = dict(known)
            tok = (eng, idx, clk)
        self.latest[tok[0]] = tok
        for r in reads:
            self.reads_since.setdefault(r, {})[tok[0]] = tok
        for w in writes:
            self.last_write[w] = tok
            self.reads_since[w] = {}
        return tok


    def barrier(self):
        toks = list(self.latest.values())
        for e in ENGS:
            self._add(e, (lambda eng: eng.nop()), (), (), extra_deps=toks)

    def op(self, eng, fn, reads=(), writes=(), extra_deps=()):
        return self._add(eng, fn, tuple(reads), tuple(writes), extra_deps=extra_deps)

    def dma(self, eng, fn, key, reads=(), writes=(), extra_deps=()):
        return self._add(eng, fn, tuple(reads), tuple(writes), dma_key=key, extra_deps=extra_deps)

    def emit(self, final_tokens):
        nc = self.nc
        for e in ENGS:
            for op in self.ops[e]:
                for k, i in op["waits"]:
                    if not isinstance(k, tuple):
                        self.ops[k][i]["flag"] = True
        final_waits = []
        for k, i, _ in final_tokens:
            if isinstance(k, tuple):
                final_waits.append((k, i))
            else:
                self.ops[k][i]["flag"] = True
                final_waits.append((k, i))
        rank = {}
        for e in ENGS:
            c = 0
            r = {}
            for i, op in enumerate(self.ops[e]):
                if op["flag"]:
                    c += 1
                    r[i] = c
            rank[e] = r
        import contextlib
        with contextlib.ExitStack() as st:
            esem = {e: st.enter_context(nc.semaphore("s_" + e)) for e in ENGS}
            dsem = {}
            for key in self.dma_count:
                dsem[key] = st.enter_context(nc.semaphore("d_%s" % (str(key).replace(" ", ""))))
            block = st.enter_context(nc.Block())

            def run(e, engine):
                for i, op in enumerate(self.ops[e]):
                    for k, v in op["waits"]:
                        if isinstance(k, tuple):
                            engine.wait_ge(dsem[k[1]], 16 * v)
                        else:
                            engine.wait_ge(esem[k], rank[k][v])
                    ins = op["fn"](engine)
                    if op["dma_key"] is not None:
                        ins.then_inc(dsem[op["dma_key"]], 16)
                    elif op["flag"]:
                        ins.then_inc(esem[e], 1)
                if e == "sp":
                    for k, v in final_waits:
                        if isinstance(k, tuple):
                            engine.wait_ge(dsem[k[1]], 16 * v)
                        else:
                            engine.wait_ge(esem[k], rank[k][v])

            @block.tensor
            def _(eng):
                run("pe", eng)

            @block.scalar
            def _(eng):
                run("act", eng)

            @block.vector
            def _(eng):
                run("dve", eng)

            @block.gpsimd
            def _(eng):
                run("pool", eng)

            @block.sync
            def _(eng):
                run("sp", eng)
        return {e: len(self.ops[e]) for e in ENGS}


def make_consts():
    s = np.arange(128)
    ident = np.eye(128, dtype=np.float32)
    negT = np.where(s[None, :] < s[:, None], NEGBIG, 0.0).astype(np.float32)
    c_bf = np.concatenate([ident] + [negT] * 4, axis=1).astype(ml_dtypes.bfloat16)
    U = (s[:, None] > s[None, :]).astype(np.float32)
    Tri = (s[:, None] <= s[None, :]).astype(np.float32)
    Sel = np.zeros((128, 128), np.float32); Sel[127, :] = 1.0
    slopes = np.exp2(-8.0 * np.arange(1, 9, dtype=np.float32) / 8).astype(np.float32)
    ab = np.zeros((128, 8, NM), np.float32)
    for h in range(8):
        for mi in range(NM):
            m = mi - 17
            ab[:, h, mi] = slopes[h] * (128.0 * m + s)
    c_f32 = np.concatenate([U, Tri, Sel, ident, ab.reshape(128, 8 * NM)], axis=1).astype(np.float32)
    sl16 = np.exp2(-8.0 * np.arange(1, 17, dtype=np.float32) / 16).astype(np.float32)
    sw = np.zeros((128, 16, 2, 2, 128), np.float32)
    q = np.arange(128)
    for kt in range(2):
        srel = s - 128 if kt == 0 else s
        dist = q[None, :] - srel[:, None]
        valid = (dist >= 0) & (dist < 128)
        for h in range(16):
            val = np.where(valid, -sl16[h] * dist.astype(np.float32) * 8.0, NEGBIG * 8.0).astype(np.float32)
            hi = val.astype(ml_dtypes.bfloat16).astype(np.float32)
            lo = (val - hi).astype(ml_dtypes.bfloat16).astype(np.float32)
            sw[:, h, kt, 0, :] = hi
            sw[:, h, kt, 1, :] = lo
    c_swa = sw.reshape(128, -1).astype(ml_dtypes.bfloat16)
    return c_bf, c_f32, c_swa


class Rot:
    def __init__(self, items):
        self.items = list(items)
        self.i = 0

    def next(self):
        it = self.items[self.i % len(self.items)]
        self.i += 1
        return it


def build(S, NSEQ, dbg=()):
    NT = S // 128
    NQ = S // 512
    nc = bass.Bass("TRN2", target_bir_lowering=False)

    def din(name, shape, dt=F32):
        return nc.dram_tensor(name, list(shape), dt, kind="ExternalInput").ap()

    x = din("x", [NSEQ * S, D])
    norm_a = din("norm_a", [D]); w_in_a = din("w_in_a", [D, EVEN_IN])
    conv_w = din("conv_w_a", [4, 1536]); conv_b = din("conv_b_a", [1536])
    dt_bias = din("dt_bias_a", [16]); a_log = din("a_log_a", [16]); d_skip = din("d_skip_a", [16])
    ssd_norm = din("ssd_norm_a", [D])
    lq1 = din("lambda_q1_a", [64]); lk1 = din("lambda_k1_a", [64]); lq2 = din("lambda_q2_a", [64]); lk2 = din("lambda_k2_a", [64])
    subln = din("subln_a", [128]); w_out_a = din("w_out_a", [2048, D])
    norm_c = din("norm_c", [D]); w_in_c = din("w_in_c", [D, ODD_IN]); b_in_c = din("b_in_c", [ODD_IN])
    sinks = din("sinks_c", [16]); w_out_c = din("w_out_c", [D, D]); final_norm = din("final_norm", [D])
    c_bf = din("c_bf", [128, 640], BF16); c_f32 = din("c_f32", [128, 512 + 8 * NM], F32)
    c_swa = din("c_swa", [128, 16 * 2 * 2 * 128], BF16)
    out = nc.dram_tensor("out", [NSEQ * S, D], F32, kind="ExternalOutput").ap()
    import os
    _kw = {"kind": "ExternalOutput"} if os.environ.get("KDBG") else {}
    x1d = nc.dram_tensor("x1_scr", [NSEQ * S, D], F32, **_kw).ap()
    yd = nc.dram_tensor("y_scr", [NSEQ * S, 2048], BF16, **_kw).ap()
    y2d = nc.dram_tensor("y2_scr", [NSEQ * S, D], BF16, **_kw).ap()
    dbg_out = {}
    for name, shape in dbg:
        dbg_out[name] = nc.dram_tensor("dbg_" + name, list(shape), F32, kind="ExternalOutput").ap()

    st = contextlib.ExitStack()
    with st:
        def sb(name, shape, dt):
            return st.enter_context(nc.sbuf_tensor(name, list(shape), dt))

        P = Prog(nc, same_engine_sync=bool(int(os.environ.get("KSES", "1"))))
        final = []
        banks = [st.enter_context(nc.psum_tensor("bank%d" % i, [128, 512], F32)) for i in range(8)]

        def bk(i):
            return banks[i][:]

        def bkbf(i):
            return banks[i][:].bitcast(BF16)

        cbf = sb("cbf", [128, 640], BF16)
        cf = sb("cf", [128, 512 + 8 * NM], F32)
        ident = cbf[:, 0:128]
        negT = cbf[:, 128:256]
        NEG4 = cbf[:, 128:640]
        U_ = cf[:, 0:128]; Tri = cf[:, 128:256]; Sel = cf[:, 256:384]; identf = cf[:, 384:512]
        abias = cf[:, 512:512 + 8 * NM].rearrange("p (h m) -> p h m", m=NM)
        hT = sb("hT", [128, 8, S], BF16)
        normw = sb("normw", [128, D], F32)
        stg = [sb("stg%d" % i, [128, 8, 128], F32) for i in range(3)]
        stgR = Rot([(stg[i], "stg%d" % i) for i in range(3)])
        junk = sb("junk", [128, D], BF16)
        smallf = sb("smallf", [128, 512], F32)
        ARENA = 70000
        arena_t = sb("arena", [128, ARENA], BF16)

        class Arena:
            def __init__(self):
                self.off = 0

            def reset(self):
                self.off = 0

            def alloc(self, shape, dt):
                n = int(np.prod(shape))
                el = n * (2 if dt == F32 else 1)
                el = (el + 1) // 2 * 2
                assert self.off + el <= ARENA, ("arena overflow", self.off, el)
                v = arena_t[:, self.off:self.off + el]
                self.off += el
                if dt == F32:
                    v = v.bitcast(F32)
                if el != n * (2 if dt == F32 else 1):
                    v = v[:, 0:n]
                if len(shape) == 2:
                    v = v.rearrange("p (a b) -> p a b", b=shape[1])
                elif len(shape) == 3:
                    v = v.rearrange("p (a b c) -> p a b c", b=shape[1], c=shape[2])
                return v

        AR = Arena()

        def dma(out_ap, in_ap, key, reads=(), writes=(), eng="sp"):
            return P.dma(eng, lambda e: e.dma_start(out=out_ap, in_=in_ap), key, reads, writes)

        def mm(out_ap, lhsT, rhs, start, stop, reads, writes):
            return P.op("pe", lambda e: e.matmul(out_ap, lhsT=lhsT, rhs=rhs, start=start, stop=stop), reads, writes)

        def tr(out_ap, in_ap, reads, writes):
            return P.op("pe", lambda e: e.transpose(out=out_ap, in_=in_ap, identity=ident), tuple(reads) + ("cbf",), writes)

        def act(out_ap, in_ap, func, reads, writes, bias=None, scale=None, accum=None):
            kw = {}
            if bias is not None:
                kw["bias"] = bias
            if scale is not None:
                kw["scale"] = scale
            if accum is not None:
                kw["accum_out"] = accum
            return P.op("act", lambda e: e.activation(out=out_ap, in_=in_ap, func=func, **kw), reads, writes)

        def tt(eng, out_ap, in0, in1, op, reads, writes):
            return P.op(eng, lambda e: e.tensor_tensor(out=out_ap, in0=in0, in1=in1, op=op), reads, writes)

        def ts(eng, out_ap, in0, s1, s2, op0, op1, reads, writes):
            if op1 is None:
                return P.op(eng, lambda e: e.tensor_scalar(out=out_ap, in0=in0, scalar1=s1, scalar2=None, op0=op0), reads, writes)
            return P.op(eng, lambda e: e.tensor_scalar(out=out_ap, in0=in0, scalar1=s1, scalar2=s2, op0=op0, op1=op1), reads, writes)

        def stt(eng, out_ap, in0, scalar, in1, op0, op1, reads, writes):
            return P.op(eng, lambda e: e.scalar_tensor_tensor(out=out_ap, in0=in0, scalar=scalar, in1=in1, op0=op0, op1=op1), reads, writes)

        def cp(eng, out_ap, in_ap, reads, writes):
            if eng == "act":
                return act(out_ap, in_ap, AF.Copy, reads, writes)
            return P.op(eng, lambda e: e.tensor_copy(out=out_ap, in_=in_ap), reads, writes)

        def recip(out_ap, in_ap, reads, writes):
            return P.op("dve", lambda e: e.reciprocal(out=out_ap, in_=in_ap), reads, writes)

        def memset(eng, ap, val, writes):
            return P.op(eng, lambda e: e.memset(ap, val), (), writes)

        def bc(ap, shape):
            return ap.to_broadcast(list(shape))

        def load_w(dst_ap, dst_res, src_ap):
            s_t, s_res = stgR.next()
            dma(s_t[:], src_ap.rearrange("(c p) n -> p c n", p=128), s_res, writes=[s_res])
            cp("pool", dst_ap, s_t[:], [s_res], [dst_res])

        evac_rr = Rot(["act", "dve"])

        dma(cbf[:], c_bf, "setup", writes=["cbf"])
        dma(cf[:], c_f32, "setup", writes=["cf"])
        lamt = AR.alloc([4, 64], F32)
        for i, v in enumerate((lq1, lk1, lq2, lk2)):
            dma(lamt[:, i, :], v.partition_broadcast(128), "setup", writes=["lamt"])
        sublnw = sb("sublnw", [128, 128], F32)
        dma(sublnw[:], subln.partition_broadcast(128), "setup", writes=["sublnw"])
        sp16 = sb("sp16", [128, 4, 16], F32)
        dma(sp16[:, 0, :], dt_bias.partition_broadcast(128), "setup", writes=["sp16"])
        dma(sp16[:, 1, :], a_log.partition_broadcast(128), "setup", writes=["sp16"])
        dma(sp16[:, 2, :], d_skip.partition_broadcast(128), "setup", writes=["sp16"])
        dma(sp16[:, 3, :], sinks.partition_broadcast(128), "setup", writes=["sp16"])
        convw = sb("convw", [128, 12, 4], F32)
        convb = sb("convb", [128, 12], F32)
        for k4 in range(4):
            P.dma("sp", lambda e, k4=k4: e.dma_start(out=convw[:, :, k4], in_=conv_w[k4].rearrange("(c p) -> p c", p=128), allow_slow_non_contiguous=True), "setup", (), ["convw"])
        P.dma("sp", lambda e: e.dma_start(out=convb[:], in_=conv_b.rearrange("(c p) -> p c", p=128), allow_slow_non_contiguous=True), "setup", (), ["convb"])
        bcol = sb("bcol", [128, 12], F32)
        for j in range(8):
            P.dma("sp", lambda e, j=j: e.dma_start(out=bcol[:, j:j + 1], in_=b_in_c[j * 128:(j + 1) * 128].rearrange("(p o) -> p o", o=1), allow_slow_non_contiguous=True), "setup", (), ["bcol"])
        for kv in range(2):
            for hf in range(2):
                P.dma("sp", lambda e, kv=kv, hf=hf: e.dma_start(out=bcol[hf * 64:(hf + 1) * 64, 8 + kv:9 + kv],
                      in_=b_in_c[1024 + kv * 64:1024 + (kv + 1) * 64].rearrange("(p o) -> p o", o=1), allow_slow_non_contiguous=True),
                      "setup", (), ["bcol"])
        P.barrier()
        tt("dve", lamt[:, 0, :], lamt[:, 0, :], lamt[:, 1, :], ALU.mult, ["lamt"], ["lamt"])
        tt("dve", lamt[:, 2, :], lamt[:, 2, :], lamt[:, 3, :], ALU.mult, ["lamt"], ["lamt"])
        P.op("dve", lambda e: e.tensor_reduce(out=smallf[:, 1:2], in_=lamt[:, 0, :], axis=AX.X, op=ALU.add), ["lamt"], ["sm1"])
        P.op("dve", lambda e: e.tensor_reduce(out=smallf[:, 2:3], in_=lamt[:, 2, :], axis=AX.X, op=ALU.add), ["lamt"], ["sm2"])
        act(smallf[:, 1:3], smallf[:, 1:3], AF.Exp, ["sm1", "sm2"], ["sm1", "sm2"])
        stt("dve", smallf[:, 0:1], smallf[:, 2:3], -LAMBDA_INIT, smallf[:, 1:2], ALU.add, ALU.subtract, ["sm1", "sm2"], ["neglam"])
        neg_lam = smallf[:, 0:1]
        ts("dve", sublnw[:], sublnw[:], 1.0 - LAMBDA_INIT, None, ALU.mult, None, ["sublnw"], ["sublnw"])
        act(sp16[:, 1, :], sp16[:, 1, :], AF.Exp, ["sp16"], ["sp16"])
        ts("dve", sp16[:, 1, :], sp16[:, 1, :], -1.0, None, ALU.mult, None, ["sp16"], ["sp16"])
        act(sp16[:, 3, :], sp16[:, 3, :], AF.Exp, ["sp16"], ["sp16"])
        dtb_bc = sp16[:, 0, :]; negA_bc = sp16[:, 1, :]; dsk_bc = sp16[:, 2, :]; esink_bc = sp16[:, 3, :]
        diagf = AR.alloc([16, 128], F32)
        diagh = sb("diagh", [128, 16, 128], BF16)
        diagl = sb("diagl", [128, 16, 128], BF16)
        tt("dve", diagf, bc(identf.unsqueeze(1), [128, 16, 128]), bc(dsk_bc.unsqueeze(2), [128, 16, 128]), ALU.mult, ["cf", "sp16"], ["diagf"])
        cp("dve", diagh[:], diagf, ["diagf"], ["diagh"])
        tt("dve", diagf, diagf, diagh[:], ALU.subtract, ["diagf", "diagh"], ["diagf"])
        cp("dve", diagl[:], diagf, ["diagf"], ["diagl"])
        P.barrier()
        def phase_norm(b, src, nw_dram, tag):
            AR.reset()
            xts = [AR.alloc([D], F32) for _ in range(2)]
            xtR = Rot([(xts[i], "xt%d" % i) for i in range(2)])
            hbs = [AR.alloc([D], BF16) for _ in range(2)]
            hbR = Rot([(hbs[i], "hb%d" % i) for i in range(2)])
            dma(normw[:], nw_dram.partition_broadcast(128), "normw", writes=["normw"])
            sscol = smallf[:, 16:16 + 3 * NT]
            memset("pool", sscol, 0.0, ["sscol"])
            trR = Rot([0, 1])
            for t in range(NT):
                xt, xres = xtR.next()
                hb, hres = hbR.next()
                dma(xt, src[b * S + t * 128: b * S + (t + 1) * 128, :], xres, writes=[xres])
                c0 = 16 + 3 * t
                act(junk[:], xt, AF.Square, [xres, "sscol"], ["junk", "sscol"], accum=smallf[:, c0:c0 + 1])
                act(smallf[:, c0 + 1:c0 + 2], smallf[:, c0:c0 + 1], AF.Ln, ["sscol"], ["sscol"], bias=EPS, scale=1.0 / D)
                act(smallf[:, c0 + 2:c0 + 3], smallf[:, c0 + 1:c0 + 2], AF.Exp, ["sscol"], ["sscol"], scale=-0.5)
                stt("dve", hb, xt, smallf[:, c0 + 2:c0 + 3], normw[:], ALU.mult, ALU.mult, [xres, "sscol", "normw"], [hres])
                bi = trR.next()
                pv = bkbf(bi).rearrange("p (c n) -> p c n", n=128)
                for c in range(8):
                    tr(pv[:, c, :], hb[:, c * 128:(c + 1) * 128], [hres], ["bank%d" % bi])
                cp(evac_rr.next(), hT[:, :, t * 128:(t + 1) * 128], pv, ["bank%d" % bi], [("hTw", t)])

        def proj_fm(dst, dst_res, wt, wres, pjR, bias_col=None):
            for n in range(S // 512):
                bi = pjR.next()
                for c in range(8):
                    mm(bk(bi), wt[:, c, :], hT[:, c, n * 512:(n + 1) * 512], c == 0, c == 7, [wres, "hT"], ["bank%d" % bi])
                if bias_col is None:
                    cp(evac_rr.next(), dst[:, n * 512:(n + 1) * 512], bk(bi), ["bank%d" % bi], [(dst_res, n)])
                else:
                    act(dst[:, n * 512:(n + 1) * 512], bk(bi), AF.Identity, ["bank%d" % bi, "bcol"], [(dst_res, n)], bias=bias_col)

        def proj_tm4(t0, wt, wres, bi):
            pv = bk(bi).rearrange("p (j n) -> p j n", n=128)
            for j in range(4):
                for c in range(8):
                    mm(pv[:, j, :], hT[:, c, (t0 + j) * 128:(t0 + j + 1) * 128], wt[:, c, :], c == 0, c == 7, [wres, "hT"], ["bank%d" % bi])
            return pv

        def phase_attn(b):
            AR.reset()
            wq = [AR.alloc([8, 128], BF16) for _ in range(8)]
            qT2 = [AR.alloc([S], BF16) for _ in range(2)]
            kTz2 = [[AR.alloc([S], BF16) for _ in range(2)] for _ in range(2)]
            vaug2 = [AR.alloc([NT, 129], BF16) for _ in range(2)]
            sg2 = [AR.alloc([NT, 128], BF16) for _ in range(2)]
            for s_ in range(2):
                memset("pool", kTz2[s_][0], 0.0, ["kTzero"])
                memset("pool", kTz2[s_][1], 0.0, ["kTzero"])
                memset("pool", vaug2[s_][:, :, 128:129], 1.0, [("vaug", s_)])
            PTs = [AR.alloc([512], BF16) for _ in range(3)]
            PTR = Rot([(PTs[i], "PT%d" % i) for i in range(3)])
            a0 = AR.alloc([4, 128], F32); t1 = AR.alloc([4, 128], F32); attn = AR.alloc([4, 128], F32)
            sq = AR.alloc([4, 128], F32)
            ya = AR.alloc([4, 128], BF16)
            rec = smallf[:, 400:408]; ssq = smallf[:, 408:420]
            pjR = Rot([6, 7])
            LR = Rot([4, 5])
            OR = Rot([(0, 1), (2, 3)])
            col0 = [0, 1024, 2048, 3072]
            def proj_units(h):
                ws = h % 2
                qT = qT2[ws]; kTz = kTz2[ws]; vaug = vaug2[ws]; sg = sg2[ws]
                units = []

                def u_load():
                    for i in range(4):
                        load_w(wq[ws * 4 + i], "wq%d_%d" % (ws, i), w_in_a[:, col0[i] + h * 128: col0[i] + (h + 1) * 128])
                units.append(u_load)
                for n in range(S // 512):
                    def u_q(n=n):
                        bi = pjR.next()
                        for c in range(8):
                            mm(bk(bi), wq[ws * 4 + 0][:, c, :], hT[:, c, n * 512:(n + 1) * 512], c == 0, c == 7, ["wq%d_0" % ws, "hT"], ["bank%d" % bi])
                        cp("dve", qT[:, n * 512:(n + 1) * 512], bk(bi), ["bank%d" % bi], [("qT", ws, n)])
                    units.append(u_q)

                    def u_k(n=n):
                        bi = pjR.next()
                        for c in range(8):
                            mm(bk(bi), wq[ws * 4 + 1][:, c, :], hT[:, c, n * 512:(n + 1) * 512], c == 0, c == 7, ["wq%d_1" % ws, "hT"], ["bank%d" % bi])
                        cp("dve", kTz[0][0:64, n * 512:(n + 1) * 512], bk(bi)[0:64, :], ["bank%d" % bi, "kTzero"], [("kT", ws, n)])
                        cp("dve", kTz[1][64:128, n * 512:(n + 1) * 512], bk(bi)[64:128, :], ["bank%d" % bi, "kTzero"], [("kT", ws, n)])
                    units.append(u_k)
                for t0 in range(0, NT, 4):
                    def u_v(t0=t0):
                        bi = pjR.next()
                        pv = proj_tm4(t0, wq[ws * 4 + 2], "wq%d_2" % ws, bi)
                        cp("dve", vaug[:, t0:t0 + 4, 0:128], pv, ["bank%d" % bi], [("vaug", ws)])
                    units.append(u_v)

                    def u_g(t0=t0):
                        bi = pjR.next()
                        pv = proj_tm4(t0, wq[ws * 4 + 3], "wq%d_3" % ws, bi)
                        act(sg[:, t0:t0 + 4, :], pv, AF.Silu, ["bank%d" % bi], [("sg", ws)])
                    units.append(u_g)
                return units

            for u in proj_units(0):
                u()
            for h in range(8):
                ws = h % 2
                qT = qT2[ws]; kTz = kTz2[ws]; vaug = vaug2[ws]; sg = sg2[ws]
                nxt = proj_units(h + 1) if h < 7 else []
                WA = 256 if h == 0 else 512
                blocks = [(qt, m, kt) for qt in range(NQ) for m in range(2) for kt in range(4 * qt + 4)]
                binfo = {}
                ostate = {}

                def emit_qk(i):
                    qt, m, kt = blocks[i]
                    rows = slice(0, 128)
                    kT = kTz[m]
                    j = kt - 4 * qt
                    c0 = 128 * j if j > 0 else 0
                    li = LR.next()
                    Lres = "bank%d" % li
                    kq = [("kT", ws, kt // 4), ("qT", ws, qt)]
                    if j < 0:
                        mm(bk(li)[:, 0:512], kT[rows, kt * 128:(kt + 1) * 128], qT[rows, qt * 512:(qt + 1) * 512], True, True, kq, [Lres])
                    else:
                        mm(bk(li)[:, c0:c0 + 128], kT[rows, kt * 128:(kt + 1) * 128], qT[rows, qt * 512 + c0:qt * 512 + c0 + 128], True, False, kq, [Lres])
                        mm(bk(li)[:, c0:c0 + 128], ident, negT, False, True, ["cbf"], [Lres])
                        if c0 + 128 < 512:
                            mm(bk(li)[:, c0 + 128:512], kT[rows, kt * 128:(kt + 1) * 128], qT[rows, qt * 512 + c0 + 128:(qt + 1) * 512], True, True, kq, [Lres])
                    pt, ptres = PTR.next()
                    for a in range(512 // WA):
                        lo = max(c0, a * WA); hi = (a + 1) * WA
                        if lo >= hi:
                            continue
                        mval = kt - 4 * qt - (WA // 128) * a - (WA // 256)
                        assert -17 <= mval <= 1
                        act(pt[:, lo:hi], bk(li)[:, lo:hi], AF.Exp, [Lres, "cf"], [ptres], bias=abias[:, h, mval + 17:mval + 18], scale=0.125)
                    binfo[i] = (pt, ptres)

                def emit_pv(i):
                    qt, m, kt = blocks[i]
                    j = kt - 4 * qt
                    if kt == 0:
                        ob = OR.next()
                        ostate[(qt, m)] = (ob, "O%d" % ob[0])
                    ob, Ores = ostate[(qt, m)]
                    pt, ptres = binfo.pop(i)
                    for jj in range(max(j, 0), 4):
                        osub = banks[ob[jj // 2]][:, (jj % 2) * 129:(jj % 2) * 129 + 129]
                        mm(osub, pt[:, jj * 128:(jj + 1) * 128], vaug[:, kt, :], (kt == 0 and jj % 2 == 0), kt == 4 * qt + jj, [ptres, ("vaug", ws)], [Ores])
                    if kt == 4 * qt + 3 and m == 1:
                        evac(qt)

                def evac(qt):
                    def O4(ob, lo, hi):
                        return [banks[ob[k]][:, 0:258].rearrange("p (i e) -> p i e", e=129)[:, :, lo:hi] for k in range(2)]
                    (oa, ra), (ob_, rb) = ostate[(qt, 0)], ostate[(qt, 1)]
                    for k in range(2):
                        recip(rec[:, 2 * k:2 * k + 2], O4(oa, 128, 129)[k].rearrange("p i e -> p (i e)"), [ra], ["rec"])
                        recip(rec[:, 4 + 2 * k:4 + 2 * k + 2], O4(ob_, 128, 129)[k].rearrange("p i e -> p (i e)"), [rb], ["rec"])
                    ts("dve", rec[:, 4:8], rec[:, 4:8], neg_lam, None, ALU.mult, None, ["rec", "neglam"], ["rec"])
                    for k in range(2):
                        tt("dve", a0[:, 2 * k:2 * k + 2, :], O4(oa, 0, 128)[k], bc(rec[:, 2 * k:2 * k + 2].unsqueeze(2), [128, 2, 128]), ALU.mult, [ra, "rec"], ["a0"])
                        tt("dve", t1[:, 2 * k:2 * k + 2, :], O4(ob_, 0, 128)[k], bc(rec[:, 4 + 2 * k:4 + 2 * k + 2].unsqueeze(2), [128, 2, 128]), ALU.mult, [rb, "rec"], ["t1"])
                    tt("pool", attn, a0, t1, ALU.add, ["a0", "t1"], ["attn"])
                    tt("pool", sq, attn, attn, ALU.mult, ["attn"], ["sq"])
                    P.op("dve", lambda e: e.tensor_reduce(out=ssq[:, 0:4], in_=sq, axis=AX.X, op=ALU.add), ["sq"], ["ssq"])
                    act(ssq[:, 4:8], ssq[:, 0:4], AF.Ln, ["ssq"], ["ssq"], bias=EPS, scale=1.0 / 128)
                    act(ssq[:, 8:12], ssq[:, 4:8], AF.Exp, ["ssq"], ["ssq"], scale=-0.5)
                    tt("pool", attn, attn, bc(ssq[:, 8:12].unsqueeze(2), [128, 4, 128]), ALU.mult, ["attn", "ssq"], ["attn"])
                    tt("pool", attn, attn, bc(sublnw[:].unsqueeze(1), [128, 4, 128]), ALU.mult, ["attn", "sublnw"], ["attn"])
                    tt("pool", ya, attn, sg[:, 4 * qt:4 * qt + 4, :], ALU.mult, ["attn", ("sg", ws)], ["ya"])
                    dma(yd[b * S + qt * 512: b * S + (qt + 1) * 512, h * 128:(h + 1) * 128].rearrange("(j p) e -> p j e", p=128), ya,
                        "ya", reads=["ya"], writes=[("yd", b)])

                every = max(1, len(blocks) // (len(nxt) + 1)) if nxt else 0
                for i in range(len(blocks) + 1):
                    if i < len(blocks):
                        emit_qk(i)
                    if i >= 1:
                        emit_pv(i - 1)
                    if nxt and i % every == every - 1:
                        nxt.pop(0)()
                while nxt:
                    nxt.pop(0)()

        def phase_ssd(b):
            AR.reset()
            wz = AR.alloc([8, 8, 128], BF16)
            xs_tok = AR.alloc([NT, 1024], BF16)
            bmT = AR.alloc([2, S], BF16); cmT = AR.alloc([2, S], BF16); bm_tok = AR.alloc([NT, 256], BF16)
            xpre = AR.alloc([NT, 16], F32); dtt = AR.alloc([NT, 16], F32); dA = AR.alloc([NT, 16], F32)
            cs = AR.alloc([NT, 16], F32); expcs = AR.alloc([NT, 16], F32); cdec = AR.alloc([NT, 16], F32)
            wds = AR.alloc([NT, 16], F32); tmp16 = AR.alloc([NT, 16], F32)
            hst = [AR.alloc([8, 64], F32) for _ in range(2)]
            prevT = [AR.alloc([8, 64], BF16) for _ in range(2)]
            off_tmp = AR.off + 2 * D
            AR.off = off_tmp
            wx = [AR.alloc([8, 128], BF16) for _ in range(2)]
            wdt_f = AR.alloc([8, 16], F32); wdt = AR.alloc([8, 16], BF16)
            raw = AR.alloc([3 + S], F32); acc = AR.alloc([S], F32); xtmp = AR.alloc([S], BF16)
            AR.off = off_tmp - 2 * D
            ssdnw = AR.alloc([D], F32)
            dma(ssdnw, ssd_norm.partition_broadcast(128), "ssdnw", writes=["ssdnw"])
            sscol = smallf[:, 16:16 + 3 * 2 * NT]
            memset("pool", sscol, 0.0, ["sscol"])
            memset("pool", raw[:, 0:3], 0.0, ["raw"])
            pjR = Rot([6, 7])
            import os
            KSSD = int(os.environ.get("KSSD", "9"))
            if KSSD < -3:
                return
            P.dma("sp", lambda e: e.dma_start(out=wdt_f, in_=w_in_a[:, 6656:6672].rearrange("(c p) n -> p c n", p=128)), "wdt", (), ["wdt_f"])
            cp("pool", wdt, wdt_f, ["wdt_f"], ["wdt"])
            dtp = bk(0)[:, 0:NT * 16].rearrange("p (t n) -> p t n", n=16)
            for t in range(NT):
                for c in range(8):
                    mm(dtp[:, t, :], hT[:, c, t * 128:(t + 1) * 128], wdt[:, c, :], c == 0, c == 7, ["wdt", "hT"], ["bank0"])
            tt("dve", xpre, dtp, bc(dtb_bc.unsqueeze(1), [128, NT, 16]), ALU.add, ["bank0", "sp16"], ["xpre"])
            act(tmp16, xpre, AF.Abs, ["xpre"], ["tmp16"])
            act(tmp16, tmp16, AF.Exp, ["tmp16"], ["tmp16"], scale=-1.0)
            act(tmp16, tmp16, AF.Ln, ["tmp16"], ["tmp16"], bias=1.0)
            stt("dve", dtt, xpre, 0.0, tmp16, ALU.max, ALU.add, ["xpre", "tmp16"], ["dtt"])
            tt("dve", dA, dtt, bc(negA_bc.unsqueeze(1), [128, NT, 16]), ALU.mult, ["dtt", "sp16"], ["dA"])
            if KSSD < -2:
                return
            csp = bk(1)[:, 0:NT * 16].rearrange("p (t n) -> p t n", n=16)
            for t in range(NT):
                mm(csp[:, t, :], Tri, dA[:, t, :], True, True, ["cf", "dA"], ["bank1"])
            cp("dve", cs, csp, ["bank1"], ["cs"])
            act(expcs, cs, AF.Exp, ["cs"], ["expcs"])
            if KSSD < -1:
                return
            clp = bk(2)[:, 0:NT * 16].rearrange("p (t n) -> p t n", n=16)
            KVAR = os.environ.get("KVAR", "")
            if KVAR == "V3":
                mm(bk(2)[:, 0:NT * 16], Tri, cs.rearrange("p t n -> p (t n)"), True, True, ["cf", "cs"], ["bank2"])
            elif KVAR == "V4":
                clp = bk(1)[:, 256:256 + NT * 16].rearrange("p (t n) -> p t n", n=16)
                mm(bk(1)[:, 256:256 + NT * 16], Sel, cs.rearrange("p t n -> p (t n)"), True, True, ["cf", "cs"], ["bank2"])
            else:
                mm(bk(2)[:, 0:NT * 16], Sel, cs.rearrange("p t n -> p (t n)"), True, True, ["cf", "cs"], ["bank2"])
            act(cdec, clp, AF.Exp, ["bank2"], ["cdec"])
            if KVAR in ("V2", "V3", "V4"):
                return
            tt("dve", tmp16, clp, cs, ALU.subtract, ["bank2", "cs"], ["tmp16"])
            act(tmp16, tmp16, AF.Exp, ["tmp16"], ["tmp16"])
            tt("dve", wds, tmp16, dtt, ALU.mult, ["tmp16", "dtt"], ["wds"])
            if KSSD < 1:
                return
            for i in range(8):
                load_w(wz[:, i, :, :], "wz", w_in_a[:, 4096 + i * 128: 4096 + (i + 1) * 128])
            trR = Rot([3, 4])
            for i in range(12):
                wt = wx[i % 2]; wres = "wx%d" % (i % 2)
                load_w(wt, wres, w_in_a[:, 5120 + i * 128: 5120 + (i + 1) * 128])
                for n in range(S // 512):
                    bi = pjR.next()
                    for c in range(8):
                        mm(bk(bi), wt[:, c, :], hT[:, c, n * 512:(n + 1) * 512], c == 0, c == 7, [wres, "hT"], ["bank%d" % bi])
                    cp(evac_rr.next(), raw[:, 3 + n * 512:3 + (n + 1) * 512], bk(bi), ["bank%d" % bi], ["raw"])
                ts("dve", acc, raw[:, 3:3 + S], convw[:, i, 3:4], convb[:, i:i + 1], ALU.mult, ALU.add, ["raw", "convw", "convb"], ["acc0"])
                for k in (2, 1, 0):
                    stt("dve", acc, raw[:, k:k + S], convw[:, i, k:k + 1], acc, ALU.mult, ALU.add, ["raw", "convw", "acc0"], ["acc0"])
                if i < 8:
                    dst, dres = xtmp, "xtmp"
                elif i < 10:
                    dst, dres = bmT[:, i - 8, :], "bmT"
                else:
                    dst, dres = cmT[:, i - 10, :], "cmT"
                act(dst, acc, AF.Silu, ["acc0"], [dres])
                if i < 10:
                    for t0 in range(0, NT, 4):
                        bi = trR.next()
                        pv = bkbf(bi)[:, 0:512].rearrange("p (j n) -> p j n", n=128)
                        for j in range(4):
                            tr(pv[:, j, :], dst[:, (t0 + j) * 128:(t0 + j + 1) * 128], [dres], ["bank%d" % bi])
                        if i < 8:
                            cp("dve", xs_tok[:, t0:t0 + 4, i * 128:(i + 1) * 128], pv, ["bank%d" % bi], ["xs_tok"])
                        else:
                            cp("dve", bm_tok[:, t0:t0 + 4, (i - 8) * 128:(i - 7) * 128], pv, ["bank%d" % bi], ["bm_tok"])
            if KSSD < 2:
                return
            P.barrier()
            AR.off = off_tmp
            dATri2 = [AR.alloc([8, 128], F32) for _ in range(2)]
            LmT2 = [AR.alloc([8, 128], F32) for _ in range(2)]
            MT2 = [AR.alloc([8, 128], BF16) for _ in range(2)]
            sz2 = [AR.alloc([512], F32) for _ in range(2)]
            u1 = AR.alloc([8, 64], F32); u2 = AR.alloc([8, 64], F32); u3 = AR.alloc([512], F32)
            xdtp2 = [AR.alloc([8, 64], BF16) for _ in range(2)]
            yb2 = [AR.alloc([512], BF16) for _ in range(2)]
            zR = Rot([0, 7])
            its = [(c, g) for c in range(NT if KSSD >= 4 else 1) for g in range(2)]

            def front(i):
                c, g = its[i]
                sl = i % 2
                tok = slice(c * 128, (c + 1) * 128)
                hs = slice(g * 8, (g + 1) * 8)
                zi = zR.next()
                zp = bk(zi).rearrange("p (j n) -> p j n", n=128)
                for j in range(4):
                    for kc in range(8):
                        mm(zp[:, j, :], hT[:, kc, tok], wz[:, g * 4 + j, kc, :], kc == 0, kc == 7, ["wz", "hT"], ["bank%d" % zi])
                act(sz2[sl], bk(zi), AF.Silu, ["bank%d" % zi], ["sz%d" % sl])
                CB = bk(1)[:, 256:384]
                mm(CB, bmT[:, g, tok], cmT[:, g, tok], True, True, ["bmT", "cmT"], ["bank1"])

            def front_b(i):
                c, g = its[i]
                sl = i % 2
                tok = slice(c * 128, (c + 1) * 128)
                hs = slice(g * 8, (g + 1) * 8)
                CB = bk(1)[:, 256:384]
                dATri = dATri2[sl]; LmT = LmT2[sl]; MT = MT2[sl]
                tt("pool", dATri, bc(Tri.unsqueeze(1), [128, 8, 128]), bc(dA[:, c, hs].unsqueeze(2), [128, 8, 128]), ALU.mult, ["cf", "dA"], ["dATri%d" % sl])
                for hf in range(2):
                    mm(bk(2 + hf), U_, dATri[:, hf * 4:(hf + 1) * 4, :].rearrange("p a b -> p (a b)"), True, False, ["cf", "dATri%d" % sl], ["bank%d" % (2 + hf)])
                    mm(bk(2 + hf), ident, NEG4, False, True, ["cbf"], ["bank%d" % (2 + hf)])
                    act(LmT[:, hf * 4:(hf + 1) * 4, :].rearrange("p a b -> p (a b)"), bk(2 + hf), AF.Exp, ["bank%d" % (2 + hf)], ["LmT%d" % sl])
                for hh in range(8):
                    stt("dve", MT[:, hh, :], LmT[:, hh, :], dtt[:, c, g * 8 + hh:g * 8 + hh + 1], CB, ALU.mult, ALU.mult, ["LmT%d" % sl, "dtt", "bank1"], ["MT%d" % sl])

            def back(i):
                c, g = its[i]
                sl = i % 2
                tok = slice(c * 128, (c + 1) * 128)
                hs = slice(g * 8, (g + 1) * 8)
                MT = MT2[sl]; sz = sz2[sl]; yb = yb2[sl]; xdtp = xdtp2[sl]
                Yd = bk(4).rearrange("p (a b) -> p a b", b=64)
                for hh in range(8):
                    H = g * 8 + hh
                    xs_h = xs_tok[:, c, H * 64:(H + 1) * 64]
                    mm(Yd[:, hh, :], MT[:, hh, :], xs_h, True, False, ["MT%d" % sl, "xs_tok"], ["bank4"])
                    mm(Yd[:, hh, :], diagh[:, H, :], xs_h, False, False, ["diagh", "xs_tok"], ["bank4"])
                    mm(Yd[:, hh, :], diagl[:, H, :], xs_h, False, True, ["diagl", "xs_tok"], ["bank4"])
                if c > 0:
                    Yo = bk(5).rearrange("p (a b) -> p a b", b=64)
                    mm(bk(5), cmT[:, g, tok], prevT[g].rearrange("p a b -> p (a b)"), True, True, ["cmT", "prevT%d" % g], ["bank5"])
                    tt("dve", u1, Yo, bc(expcs[:, c, hs].unsqueeze(2), [128, 8, 64]), ALU.mult, ["bank5", "expcs"], ["u1"])
                    tt("dve", u2, Yd, u1, ALU.add, ["bank4", "u1"], ["u2"])
                else:
                    cp("dve", u2, Yd, ["bank4"], ["u2"])
                tt("pool", u3, u2.rearrange("p a b -> p (a b)"), sz, ALU.mult, ["u2", "sz%d" % sl], ["u3"])
                c0 = 16 + 3 * (2 * c + g)
                act(junk[:, 0:512], u3, AF.Square, ["u3", "sscol"], ["junk", "sscol"], accum=smallf[:, c0:c0 + 1])
                act(smallf[:, c0 + 1:c0 + 2], smallf[:, c0:c0 + 1], AF.Ln, ["sscol"], ["sscol"], bias=EPS, scale=1.0 / 512)
                act(smallf[:, c0 + 2:c0 + 3], smallf[:, c0 + 1:c0 + 2], AF.Exp, ["sscol"], ["sscol"], scale=-0.5)
                stt("dve", yb, u3, smallf[:, c0 + 2:c0 + 3], ssdnw[:, g * 512:(g + 1) * 512], ALU.mult, ALU.mult, ["u3", "sscol", "ssdnw"], ["yb%d" % sl])
                dma(yd[b * S + c * 128: b * S + (c + 1) * 128, 1024 + g * 512: 1024 + (g + 1) * 512], yb, "yb%d" % sl, reads=["yb%d" % sl], writes=[("yd", b)])
                if c < NT - 1:
                    tt("pool", xdtp, xs_tok[:, c, g * 512:(g + 1) * 512].rearrange("p (a b) -> p a b", b=64),
                       bc(wds[:, c, hs].unsqueeze(2), [128, 8, 64]), ALU.mult, ["xs_tok", "wds"], ["xdtp%d" % sl])
                    mm(bk(6), bm_tok[:, c, g * 128:(g + 1) * 128], xdtp.rearrange("p a b -> p (a b)"), True, True, ["bm_tok", "xdtp%d" % sl], ["bank6"])
                    STv = bk(6).rearrange("p (a b) -> p a b", b=64)
                    if c == 0:
                        cp("dve", hst[g], STv, ["bank6"], ["hst%d" % g])
                    else:
                        tt("pool", hst[g], hst[g], bc(cdec[:, c, hs].unsqueeze(2), [128, 8, 64]), ALU.mult, ["hst%d" % g, "cdec"], ["hst%d" % g])
                        tt("dve", hst[g], hst[g], STv, ALU.add, ["hst%d" % g, "bank6"], ["hst%d" % g])
                    cp("pool", prevT[g], hst[g], ["hst%d" % g], ["prevT%d" % g])

            KPIPE = os.environ.get("KPIPE", "a")
            if KPIPE == "none":
                for i in range(len(its)):
                    front(i); front_b(i); back(i)
            elif KPIPE == "a":
                front(0)
                for i in range(len(its)):
                    front_b(i)
                    if i + 1 < len(its):
                        front(i + 1)
                    back(i)
            elif KPIPE == "b":
                front(0); front_b(0)
                for i in range(len(its)):
                    if i + 1 < len(its):
                        front(i + 1)
                    back(i)
                    if i + 1 < len(its):
                        front_b(i + 1)
            else:
                front(0); front_b(0)
                for i in range(len(its)):
                    if i + 1 < len(its):
                        front(i + 1); front_b(i + 1)
                    back(i)

        def phase_out(b, layer):
            AR.reset()
            KC = 16 if layer == 0 else 8
            wdram = w_out_a if layer == 0 else w_out_c
            ysrc = yd if layer == 0 else y2d
            xsrc = x if layer == 0 else x1d
            wout = AR.alloc([KC, D], BF16)
            for kh in range(KC // 8):
                for csl in range(8):
                    load_w(wout[:, kh * 8:(kh + 1) * 8, csl * 128:(csl + 1) * 128], "wout", wdram[kh * 1024:(kh + 1) * 1024, csl * 128:(csl + 1) * 128])
            yts = [AR.alloc([KC * 128], BF16) for _ in range(2)]
            yTts = [AR.alloc([KC, 128], BF16) for _ in range(2)]
            x1ts = [AR.alloc([D], F32) for _ in range(2)]
            ots = [AR.alloc([D], F32) for _ in range(2)]
            xts = [AR.alloc([D], F32) for _ in range(2)]
            xtR = Rot([(xts[i], "xt%d" % i) for i in range(2)])
            if layer == 1:
                dma(normw[:], final_norm.partition_broadcast(128), "normw", writes=["normw"])
                sscol = smallf[:, 16:16 + 3 * NT]
                memset("pool", sscol, 0.0, ["sscol"])
            trR = Rot([0, 1, 2, 3])
            mmR = Rot([4, 5, 6, 7])
            for t in range(NT):
                s2 = t % 2
                rows = slice(b * S + t * 128, b * S + (t + 1) * 128)
                dma(yts[s2], ysrc[rows, :], "yt%d" % s2, reads=[("yd", b)], writes=["yt%d" % s2])
                xt, xres = xtR.next()
                dma(xt, xsrc[rows, :], xres, reads=[("x1d", b)] if layer == 1 else [], writes=[xres])
                for k0 in range(0, KC, 8):
                    bi = trR.next()
                    pv = bkbf(bi).rearrange("p (c n) -> p c n", n=128)
                    for k in range(8):
                        tr(pv[:, k, :], yts[s2][:, (k0 + k) * 128:(k0 + k + 1) * 128], ["yt%d" % s2], ["bank%d" % bi])
                    cp(evac_rr.next(), yTts[s2][:, k0:k0 + 8, :], pv, ["bank%d" % bi], ["yTt%d" % s2])
                for hf in range(2):
                    bi = mmR.next()
                    for k in range(KC):
                        mm(bk(bi), yTts[s2][:, k, :], wout[:, k, hf * 512:(hf + 1) * 512], k == 0, k == KC - 1, ["yTt%d" % s2, "wout"], ["bank%d" % bi])
                    tt("dve", x1ts[s2][:, hf * 512:(hf + 1) * 512], xt[:, hf * 512:(hf + 1) * 512], bk(bi), ALU.add, [xres, "bank%d" % bi], ["x1t%d" % s2])
                if layer == 0:
                    dma(x1d[rows, :], x1ts[s2], "x1w%d" % s2, reads=["x1t%d" % s2], writes=[("x1d", b)])
                else:
                    c0 = 16 + 3 * t
                    act(junk[:], x1ts[s2], AF.Square, ["x1t%d" % s2, "sscol"], ["junk", "sscol"], accum=smallf[:, c0:c0 + 1])
                    act(smallf[:, c0 + 1:c0 + 2], smallf[:, c0:c0 + 1], AF.Ln, ["sscol"], ["sscol"], bias=EPS, scale=1.0 / D)
                    act(smallf[:, c0 + 2:c0 + 3], smallf[:, c0 + 1:c0 + 2], AF.Exp, ["sscol"], ["sscol"], scale=-0.5)
                    stt("dve", ots[s2], x1ts[s2], smallf[:, c0 + 2:c0 + 3], normw[:], ALU.mult, ALU.mult, ["x1t%d" % s2, "sscol", "normw"], ["ot%d" % s2])
                    final.append(dma(out[rows, :], ots[s2], "ow%d" % s2, reads=["ot%d" % s2]))

        def phase_swa(b):
            AR.reset()
            swab = AR.alloc([16, 4, 128], BF16)
            dma(swab.rearrange("p a b c -> p (a b c)"), c_swa, "swab", writes=["swab"])
            wk = [AR.alloc([8, 128], BF16) for _ in range(2)]
            wqg = [AR.alloc([8, 128], BF16) for _ in range(4)]
            wv = AR.alloc([8, 128], BF16)
            kz = [[AR.alloc([S], BF16) for _ in range(2)] for _ in range(2)]
            for kv_ in range(2):
                memset("pool", kz[kv_][0], 0.0, ["kzero"])
                memset("pool", kz[kv_][1], 0.0, ["kzero"])
            vaug = AR.alloc([NT, 2, 65], BF16)
            qT = AR.alloc([S], BF16); sg = AR.alloc([NT, 128], BF16); gtmp = AR.alloc([4, 128], F32)
            PTs = [AR.alloc([2, 2, 128], BF16) for _ in range(2)]
            PTR = Rot([(PTs[i], "PT%d" % i) for i in range(2)])
            den = smallf[:, 400:404]
            of = AR.alloc([2, 64], F32); obf = [AR.alloc([128], BF16) for _ in range(2)]
            memset("pool", vaug[:, :, :, 64:65], 1.0, ["vaug"])
            bvg = AR.alloc([128 + D], F32)
            dma(bvg[:, 0:128], b_in_c[1152:1280].partition_broadcast(128), "bvg0", writes=["bvg0"])
            dma(bvg[:, 128:128 + D], b_in_c[1280:2304].partition_broadcast(128), "bvg1", writes=["bvg1"])
            pjR = Rot([6, 7])
            for kv in range(2):
                s_t, s_res = stgR.next()
                for hf in range(2):
                    dma(s_t[:, :, hf * 64:(hf + 1) * 64], w_in_c[:, 1024 + kv * 64:1024 + (kv + 1) * 64].rearrange("(c p) n -> p c n", p=128),
                        s_res, reads=([s_res] if hf == 1 else []), writes=[s_res])
                cp("pool", wk[kv], s_t[:], [s_res], ["wk%d" % kv])
                for n in range(S // 512):
                    bi = pjR.next()
                    for c in range(8):
                        mm(bk(bi), wk[kv][:, c, :], hT[:, c, n * 512:(n + 1) * 512], c == 0, c == 7, ["wk%d" % kv, "hT"], ["bank%d" % bi])
                    for i2 in range(2):
                        hs2 = slice(i2 * 64, (i2 + 1) * 64)
                        act(kz[kv][i2][hs2, n * 512:(n + 1) * 512], bk(bi)[hs2, :], AF.Identity, ["bank%d" % bi, "bcol", "kzero"], [("kT2_%d" % kv, n)],
                            bias=bcol[hs2, 8 + kv:9 + kv])
            load_w(wv, "wv", w_in_c[:, 1152:1280])
            for t0 in range(0, NT, 4):
                bi = pjR.next()
                pv = proj_tm4(t0, wv, "wv", bi)
                tt("dve", vaug[:, t0:t0 + 4, :, 0:64], pv.rearrange("p j (k e) -> p j k e", e=64),
                   bc(bvg[:, 0:128].rearrange("p (k e) -> p k e", e=64).unsqueeze(1), [128, 4, 2, 64]), ALU.add, ["bank%d" % bi, "bvg0"], ["vaug"])
            LR = Rot([0, 1, 2])
            OR = Rot([3, 4])
            trR = Rot([5])
            for j in range(8):
                kv = j // 4
                ws = j % 2
                load_w(wqg[ws * 2], "wqg%d" % (ws * 2), w_in_c[:, j * 128:(j + 1) * 128])
                load_w(wqg[ws * 2 + 1], "wqg%d" % (ws * 2 + 1), w_in_c[:, 1280 + j * 128:1280 + (j + 1) * 128])
                proj_fm(qT, "qT", wqg[ws * 2], "wqg%d" % (ws * 2), pjR, bias_col=bcol[:, j:j + 1])
                for t0 in range(0, NT, 4):
                    bi = pjR.next()
                    pv = proj_tm4(t0, wqg[ws * 2 + 1], "wqg%d" % (ws * 2 + 1), bi)
                    tt("dve", gtmp, pv, bc(bvg[:, 128 + j * 128:128 + (j + 1) * 128].unsqueeze(1), [128, 4, 128]), ALU.add, ["bank%d" % bi, "bvg1"], ["gtmp"])
                    act(sg[:, t0:t0 + 4, :], gtmp, AF.Silu, ["gtmp"], ["sg"])
                for n in range(NT):
                    li = LR.next(); Lres = "bank%d" % li
                    Lv = bk(li).rearrange("p (i k q) -> p i k q", k=2, q=128)
                    kts = (1,) if n == 0 else (0, 1)
                    for i in range(2):
                        hq = 2 * j + i
                        rows = slice(i * 64, (i + 1) * 64)
                        for kt in kts:
                            tk = n - 1 + kt
                            mm(Lv[:, i, kt, :], kz[kv][i][:, tk * 128:(tk + 1) * 128], qT[:, n * 128:(n + 1) * 128], True, False, [("kT2_%d" % kv, tk // 4), ("qT", n // 4)], [Lres])
                            mm(Lv[:, i, kt, :], ident, swab[:, hq, kt * 2 + 0, :], False, False, ["cbf", "swab"], [Lres])
                            mm(Lv[:, i, kt, :], ident, swab[:, hq, kt * 2 + 1, :], False, True, ["cbf", "swab"], [Lres])
                    pt, ptres = PTR.next()
                    if n == 0:
                        act(pt[:, :, 1, :], Lv[:, :, 1, :], AF.Exp, [Lres], [ptres], scale=0.125)
                    else:
                        act(pt, Lv, AF.Exp, [Lres], [ptres], scale=0.125)
                    oi = OR.next(); Ores = "bank%d" % oi
                    Ov = bk(oi)[:, 0:130].rearrange("p (i e) -> p i e", e=65)
                    for i in range(2):
                        for kt in kts:
                            tk = n - 1 + kt
                            mm(Ov[:, i, :], pt[:, i, kt, :], vaug[:, tk, kv, :], kt == kts[0], kt == kts[-1], [ptres, "vaug"], [Ores])
                    tt("dve", den[:, 0:2], Ov[:, :, 64:65].rearrange("p i e -> p (i e)"), esink_bc[:, 2 * j:2 * j + 2], ALU.add, [Ores, "sp16"], ["den"])
                    P.op("dve", lambda e: e.reciprocal(out=den[:, 2:4], in_=den[:, 0:2]), ["den"], ["den"])
                    tt("dve", of, Ov[:, :, 0:64], bc(den[:, 2:4].unsqueeze(2), [128, 2, 64]), ALU.mult, [Ores, "den"], ["of"])
                    ob2 = obf[n % 2]; obres = "obf%d" % (n % 2)
                    tt("pool", ob2, of.rearrange("p a b -> p (a b)"), sg[:, n, :], ALU.mult, ["of", "sg"], [obres])
                    dma(y2d[b * S + n * 128: b * S + (n + 1) * 128, j * 128:(j + 1) * 128], ob2, obres, reads=[obres], writes=[("yd", b)])

        import os
        KSTOP = os.environ.get("KSTOP", "all")
        order = ["norm0", "attn", "ssd", "out0", "norm1", "swa", "out1"]
        nphase = len(order) if KSTOP == "all" else (0 if KSTOP == "setup" else order.index(KSTOP) + 1)
        for b in range(NSEQ):
            for ph in order[:nphase]:
                if ph == "norm0":
                    phase_norm(b, x, norm_a, "a")
                elif ph == "attn":
                    phase_attn(b)
                elif ph == "ssd":
                    phase_ssd(b)
                elif ph == "out0":
                    phase_out(b, 0)
                elif ph == "norm1":
                    phase_norm(b, x1d, norm_c, "c")
                elif ph == "swa":
                    phase_swa(b)
                elif ph == "out1":
                    phase_out(b, 1)
                P.barrier()
        counts = P.emit(final)
        print("ops:", counts, "sbuf left", nc.sbuf_bytes_remaining)
    return nc


PARAMS = ["norm_a", "w_in_a", "conv_w_a", "conv_b_a", "dt_bias_a", "a_log_a", "d_skip_a", "ssd_norm_a",
          "lambda_q1_a", "lambda_k1_a", "lambda_q2_a", "lambda_k2_a", "subln_a", "w_out_a",
          "norm_c", "w_in_c", "b_in_c", "sinks_c", "w_out_c", "final_norm"]

_CACHE = {}


def run(inputs, n_cores=8):
    x = np.ascontiguousarray(np.asarray(inputs["x"], dtype=np.float32))
    B, S, _ = x.shape
    assert B % n_cores == 0
    NSEQ = B // n_cores
    key = (S, NSEQ)
    if key not in _CACHE:
        _CACHE[key] = build(S, NSEQ)
    nc = _CACHE[key]
    c_bf, c_f32, c_swa = make_consts()
    base = {"c_bf": c_bf, "c_f32": c_f32, "c_swa": c_swa}
    for k in PARAMS:
        a = np.asarray(inputs[k], dtype=np.float32)
        if k != "final_norm":
            a = a[0]
        base[k] = np.ascontiguousarray(a)
    in_maps = []
    for i in range(n_cores):
        m = dict(base)
        m["x"] = x[i * NSEQ:(i + 1) * NSEQ].reshape(NSEQ * S, D)
        in_maps.append(m)
    res = run_bass_kernel_spmd(nc, in_maps, core_ids=list(range(n_cores)))
    import os
    if os.environ.get("KDBG"):
        global DBG
        DBG = [{k: np.asarray(v) for k, v in r.items()} for r in res.results]
    outs = [np.asarray(r["out"]).reshape(NSEQ, S, D) for r in res.results]
    return np.concatenate(outs, axis=0).astype(np.float32)


def kernel(**inputs):
    return run(inputs, 8)
```

```python
import contextlib
import math
import numpy as np
import ml_dtypes
import concourse.bass as bass
import concourse.mybir as mybir
from concourse.bass_utils import run_bass_kernel_spmd

F32 = mybir.dt.float32
BF16 = mybir.dt.bfloat16
AF = mybir.ActivationFunctionType
ALU = mybir.AluOpType
AX = mybir.AxisListType

D = 1024
EVEN_IN = 6672
ODD_IN = 2304
EPS = 1e-5
NEGBIG = -30000.0
LAMBDA_INIT = 0.8 - 0.6 * math.exp(-0.3 * 0)
NM = 19


ENGS = ("pe", "act", "dve", "pool", "sp")


class Prog:
    def __init__(self, nc, same_engine_sync=True):
        self.nc = nc
        self.ops = {e: [] for e in ENGS}
        self.last_write = {}
        self.reads_since = {}
        self.known = {e: {} for e in ENGS}
        self.dma_count = {}
        self.same_engine_sync = same_engine_sync
        self.latest = {}

    def _add(self, eng, fn, reads, writes, dma_key=None, extra_deps=()):
        writes = tuple(writes) + tuple(r for r in reads if isinstance(r, str) and (r.startswith("bank") or (r[0] == "O" and r[1:].isdigit())))
        lst = self.ops[eng]
        idx = len(lst)
        deps = {}

        def need(tok):
            if tok is None:
                return
            k, i, clk = tok
            if deps.get(k, (-1, None))[0] < i:
                deps[k] = (i, clk)

        for r in reads:
            need(self.last_write.get(r))
        for w in writes:
            need(self.last_write.get(w))
            for tok in self.reads_since.get(w, {}).values():
                need(tok)
        for tok in extra_deps:
            need(tok)
        known = self.known[eng]
        waits = []
        for k, (i, clk) in deps.items():
            if k == eng and not (self.same_engine_sync and eng != "pe" and eng != "sp"):
                continue
            if known.get(k, -1) >= i:
                continue
            waits.append((k, i))
        for k, (i, clk) in deps.items():
            if k == eng and (k, i) not in waits:
                continue
            if known.get(k, -1) < i:
                known[k] = i
            for kk, vv in clk.items():
                if known.get(kk, -1) < vv:
                    known[kk] = vv
        op = dict(fn=fn, waits=waits, flag=False, dma_key=dma_key)
        lst.append(op)
        if dma_key is not None:
            n = self.dma_count.get(dma_key, 0) + 1
            self.dma_count[dma_key] = n
            tok = (("D", dma_key), n, dict(known))
        else:
            clk = dict(known)
            tok = (eng, idx, clk)
        self.latest[tok[0]] = tok
        for r in reads:
            self.reads_since.setdefault(r, {})[tok[0]] = tok
        for w in writes:
            self.last_write[w] = tok
            self.reads_since[w] = {}
        return tok


    def barrier(self):
        toks = list(self.latest.values())
        for e in ENGS:
            self._add(e, (lambda eng: eng.nop()), (), (), extra_deps=toks)

    def op(self, eng, fn, reads=(), writes=(), extra_deps=()):
        return self._add(eng, fn, tuple(reads), tuple(writes), extra_deps=extra_deps)

    def dma(self, eng, fn, key, reads=(), writes=(), extra_deps=()):
        return self._add(eng, fn, tuple(reads), tuple(writes), dma_key=key, extra_deps=extra_deps)

    def emit(self, final_tokens):
        nc = self.nc
        for e in ENGS:
            for op in self.ops[e]:
                for k, i in op["waits"]:
                    if not isinstance(k, tuple):
                        self.ops[k][i]["flag"] = True
        final_waits = []
        for k, i, _ in final_tokens:
            if isinstance(k, tuple):
                final_waits.append((k, i))
            else:
                self.ops[k][i]["flag"] = True
                final_waits.append((k, i))
        rank = {}
        for e in ENGS:
            c = 0
            r = {}
            for i, op in enumerate(self.ops[e]):
                if op["flag"]:
                    c += 1
                    r[i] = c
            rank[e] = r
        import contextlib
        with contextlib.ExitStack() as st:
            esem = {e: st.enter_context(nc.semaphore("s_" + e)) for e in ENGS}
            dsem = {}
            for key in self.dma_count:
                dsem[key] = st.enter_context(nc.semaphore("d_%s" % (str(key).replace(" ", ""))))
            block = st.enter_context(nc.Block())

            def run(e, engine):
                for i, op in enumerate(self.ops[e]):
                    for k, v in op["waits"]:
                        if isinstance(k, tuple):
                            engine.wait_ge(dsem[k[1]], 16 * v)
                        else:
                            engine.wait_ge(esem[k], rank[k][v])
                    ins = op["fn"](engine)
                    if op["dma_key"] is not None:
                        ins.then_inc(dsem[op["dma_key"]], 16)
                    elif op["flag"]:
                        ins.then_inc(esem[e], 1)
                if e == "sp":
                    for k, v in final_waits:
                        if isinstance(k, tuple):
                            engine.wait_ge(dsem[k[1]], 16 * v)
                        else:
                            engine.wait_ge(esem[k], rank[k][v])

            @block.tensor
            def _(eng):
                run("pe", eng)

            @block.scalar
            def _(eng):
                run("act", eng)

            @block.vector
            def _(eng):
                run("dve", eng)

            @block.gpsimd
            def _(eng):
                run("pool", eng)

            @block.sync
            def _(eng):
                run("sp", eng)
        return {e: len(self.ops[e]) for e in ENGS}


def make_consts():
    s = np.arange(128)
    ident = np.eye(128, dtype=np.float32)
    negT = np.where(s[None, :] < s[:, None], NEGBIG, 0.0).astype(np.float32)
    c_bf = np.concatenate([ident] + [negT] * 4, axis=1).astype(ml_dtypes.bfloat16)
    U = (s[:, None] > s[None, :]).astype(np.float32)
    Tri = (s[:, None] <= s[None, :]).astype(np.float32)
    Sel = np.zeros((128, 128), np.float32); Sel[127, :] = 1.0
    slopes = np.exp2(-8.0 * np.arange(1, 9, dtype=np.float32) / 8).astype(np.float32)
    ab = np.zeros((128, 8, NM), np.float32)
    for h in range(8):
        for mi in range(NM):
            m = mi - 17
            ab[:, h, mi] = slopes[h] * (128.0 * m + s)
    c_f32 = np.concatenate([U, Tri, Sel, ident, ab.reshape(128, 8 * NM)], axis=1).astype(np.float32)
    sl16 = np.exp2(-8.0 * np.arange(1, 17, dtype=np.float32) / 16).astype(np.float32)
    sw = np.zeros((128, 16, 2, 2, 128), np.float32)
    q = np.arange(128)
    for kt in range(2):
        srel = s - 128 if kt == 0 else s
        dist = q[None, :] - srel[:, None]
        valid = (dist >= 0) & (dist < 128)
        for h in range(16):
            val = np.where(valid, -sl16[h] * dist.astype(np.float32) * 8.0, NEGBIG * 8.0).astype(np.float32)
            hi = val.astype(ml_dtypes.bfloat16).astype(np.float32)
            lo = (val - hi).astype(ml_dtypes.bfloat16).astype(np.float32)
            sw[:, h, kt, 0, :] = hi
            sw[:, h, kt, 1, :] = lo
    c_swa = sw.reshape(128, -1).astype(ml_dtypes.bfloat16)
    return c_bf, c_f32, c_swa


class Rot:
    def __init__(self, items):
        self.items = list(items)
        self.i = 0

    def next(self):
        it = self.items[self.i % len(self.items)]
        self.i += 1
        return it


def build(S, NSEQ, dbg=()):
    NT = S // 128
    NQ = S // 512
    nc = bass.Bass("TRN2", target_bir_lowering=False)

    def din(name, shape, dt=F32):
        return nc.dram_tensor(name, list(shape), dt, kind="ExternalInput").ap()

    x = din("x", [NSEQ * S, D])
    norm_a = din("norm_a", [D]); w_in_a = din("w_in_a", [D, EVEN_IN])
    conv_w = din("conv_w_a", [4, 1536]); conv_b = din("conv_b_a", [1536])
    dt_bias = din("dt_bias_a", [16]); a_log = din("a_log_a", [16]); d_skip = din("d_skip_a", [16])
    ssd_norm = din("ssd_norm_a", [D])
    lq1 = din("lambda_q1_a", [64]); lk1 = din("lambda_k1_a", [64]); lq2 = din("lambda_q2_a", [64]); lk2 = din("lambda_k2_a", [64])
    subln = din("subln_a", [128]); w_out_a = din("w_out_a", [2048, D])
    norm_c = din("norm_c", [D]); w_in_c = din("w_in_c", [D, ODD_IN]); b_in_c = din("b_in_c", [ODD_IN])
    sinks = din("sinks_c", [16]); w_out_c = din("w_out_c", [D, D]); final_norm = din("final_norm", [D])
    c_bf = din("c_bf", [128, 640], BF16); c_f32 = din("c_f32", [128, 512 + 8 * NM], F32)
    c_swa = din("c_swa", [128, 16 * 2 * 2 * 128], BF16)
    out = nc.dram_tensor("out", [NSEQ * S, D], F32, kind="ExternalOutput").ap()
    import os
    _kw = {"kind": "ExternalOutput"} if os.environ.get("KDBG") else {}
    x1d = nc.dram_tensor("x1_scr", [NSEQ * S, D], F32, **_kw).ap()
    yd = nc.dram_tensor("y_scr", [NSEQ * S, 2048], BF16, **_kw).ap()
    y2d = nc.dram_tensor("y2_scr", [NSEQ * S, D], BF16, **_kw).ap()
    dbg_out = {}
    for name, shape in dbg:
        dbg_out[name] = nc.dram_tensor("dbg_" + name, list(shape), F32, kind="ExternalOutput").ap()

    st = contextlib.ExitStack()
    with st:
        def sb(name, shape, dt):
            return st.enter_context(nc.sbuf_tensor(name, list(shape), dt))

        P = Prog(nc, same_engine_sync=bool(int(os.environ.get("KSES", "1"))))
        final = []
        banks = [st.enter_context(nc.psum_tensor("bank%d" % i, [128, 512], F32)) for i in range(8)]

        def bk(i):
            return banks[i][:]

        def bkbf(i):
            return banks[i][:].bitcast(BF16)

        cbf = sb("cbf", [128, 640], BF16)
        cf = sb("cf", [128, 512 + 8 * NM], F32)
        ident = cbf[:, 0:128]
        negT = cbf[:, 128:256]
        NEG4 = cbf[:, 128:640]
        U_ = cf[:, 0:128]; Tri = cf[:, 128:256]; Sel = cf[:, 256:384]; identf = cf[:, 384:512]
        abias = cf[:, 512:512 + 8 * NM].rearrange("p (h m) -> p h m", m=NM)
        hT = sb("hT", [128, 8, S], BF16)
        normw = sb("normw", [128, D], F32)
        stg = [sb("stg%d" % i, [128, 8, 128], F32) for i in range(3)]
        stgR = Rot([(stg[i], "stg%d" % i) for i in range(3)])
        junk = sb("junk", [128, D], BF16)
        smallf = sb("smallf", [128, 512], F32)
        ARENA = 70000
        arena_t = sb("arena", [128, ARENA], BF16)

        class Arena:
            def __init__(self):
                self.off = 0

            def reset(self):
                self.off = 0

            def alloc(self, shape, dt):
                n = int(np.prod(shape))
                el = n * (2 if dt == F32 else 1)
                el = (el + 1) // 2 * 2
                assert self.off + el <= ARENA, ("arena overflow", self.off, el)
                v = arena_t[:, self.off:self.off + el]
                self.off += el
                if dt == F32:
                    v = v.bitcast(F32)
                if el != n * (2 if dt == F32 else 1):
                    v = v[:, 0:n]
                if len(shape) == 2:
                    v = v.rearrange("p (a b) -> p a b", b=shape[1])
                elif len(shape) == 3:
                    v = v.rearrange("p (a b c) -> p a b c", b=shape[1], c=shape[2])
                return v

        AR = Arena()

        def dma(out_ap, in_ap, key, reads=(), writes=(), eng="sp"):
            return P.dma(eng, lambda e: e.dma_start(out=out_ap, in_=in_ap), key, reads, writes)

        def mm(out_ap, lhsT, rhs, start, stop, reads, writes):
            return P.op("pe", lambda e: e.matmul(out_ap, lhsT=lhsT, rhs=rhs, start=start, stop=stop), reads, writes)

        def tr(out_ap, in_ap, reads, writes):
            return P.op("pe", lambda e: e.transpose(out=out_ap, in_=in_ap, identity=ident), tuple(reads) + ("cbf",), writes)

        def act(out_ap, in_ap, func, reads, writes, bias=None, scale=None, accum=None):
            kw = {}
            if bias is not None:
                kw["bias"] = bias
            if scale is not None:
                kw["scale"] = scale
            if accum is not None:
                kw["accum_out"] = accum
            return P.op("act", lambda e: e.activation(out=out_ap, in_=in_ap, func=func, **kw), reads, writes)

        def tt(eng, out_ap, in0, in1, op, reads, writes):
            return P.op(eng, lambda e: e.tensor_tensor(out=out_ap, in0=in0, in1=in1, op=op), reads, writes)

        def ts(eng, out_ap, in0, s1, s2, op0, op1, reads, writes):
            if op1 is None:
                return P.op(eng, lambda e: e.tensor_scalar(out=out_ap, in0=in0, scalar1=s1, scalar2=None, op0=op0), reads, writes)
            return P.op(eng, lambda e: e.tensor_scalar(out=out_ap, in0=in0, scalar1=s1, scalar2=s2, op0=op0, op1=op1), reads, writes)

        def stt(eng, out_ap, in0, scalar, in1, op0, op1, reads, writes):
            return P.op(eng, lambda e: e.scalar_tensor_tensor(out=out_ap, in0=in0, scalar=scalar, in1=in1, op0=op0, op1=op1), reads, writes)

        def cp(eng, out_ap, in_ap, reads, writes):
            if eng == "act":
                return act(out_ap, in_ap, AF.Copy, reads, writes)
            return P.op(eng, lambda e: e.tensor_copy(out=out_ap, in_=in_ap), reads, writes)

        def recip(out_ap, in_ap, reads, writes):
            return P.op("dve", lambda e: e.reciprocal(out=out_ap, in_=in_ap), reads, writes)

        def memset(eng, ap, val, writes):
            return P.op(eng, lambda e: e.memset(ap, val), (), writes)

        def bc(ap, shape):
            return ap.to_broadcast(list(shape))

        def load_w(dst_ap, dst_res, src_ap):
            s_t, s_res = stgR.next()
            dma(s_t[:], src_ap.rearrange("(c p) n -> p c n", p=128), s_res, writes=[s_res])
            cp("pool", dst_ap, s_t[:], [s_res], [dst_res])

        evac_rr = Rot(["act", "dve"])

        dma(cbf[:], c_bf, "setup", writes=["cbf"])
        dma(cf[:], c_f32, "setup", writes=["cf"])
        lamt = AR.alloc([4, 64], F32)
        for i, v in enumerate((lq1, lk1, lq2, lk2)):
            dma(lamt[:, i, :], v.partition_broadcast(128), "setup", writes=["lamt"])
        sublnw = sb("sublnw", [128, 128], F32)
        dma(sublnw[:], subln.partition_broadcast(128), "setup", writes=["sublnw"])
        sp16 = sb("sp16", [128, 4, 16], F32)
        dma(sp16[:, 0, :], dt_bias.partition_broadcast(128), "setup", writes=["sp16"])
        dma(sp16[:, 1, :], a_log.partition_broadcast(128), "setup", writes=["sp16"])
        dma(sp16[:, 2, :], d_skip.partition_broadcast(128), "setup", writes=["sp16"])
        dma(sp16[:, 3, :], sinks.partition_broadcast(128), "setup", writes=["sp16"])
        convw = sb("convw", [128, 12, 4], F32)
        convb = sb("convb", [128, 12], F32)
        for k4 in range(4):
            P.dma("sp", lambda e, k4=k4: e.dma_start(out=convw[:, :, k4], in_=conv_w[k4].rearrange("(c p) -> p c", p=128), allow_slow_non_contiguous=True), "setup", (), ["convw"])
        P.dma("sp", lambda e: e.dma_start(out=convb[:], in_=conv_b.rearrange("(c p) -> p c", p=128), allow_slow_non_contiguous=True), "setup", (), ["convb"])
        bcol = sb("bcol", [128, 12], F32)
        for j in range(8):
            P.dma("sp", lambda e, j=j: e.dma_start(out=bcol[:, j:j + 1], in_=b_in_c[j * 128:(j + 1) * 128].rearrange("(p o) -> p o", o=1), allow_slow_non_contiguous=True), "setup", (), ["bcol"])
        for kv in range(2):
            for hf in range(2):
                P.dma("sp", lambda e, kv=kv, hf=hf: e.dma_start(out=bcol[hf * 64:(hf + 1) * 64, 8 + kv:9 + kv],
                      in_=b_in_c[1024 + kv * 64:1024 + (kv + 1) * 64].rearrange("(p o) -> p o", o=1), allow_slow_non_contiguous=True),
                      "setup", (), ["bcol"])
        P.barrier()
        tt("dve", lamt[:, 0, :], lamt[:, 0, :], lamt[:, 1, :], ALU.mult, ["lamt"], ["lamt"])
        tt("dve", lamt[:, 2, :], lamt[:, 2, :], lamt[:, 3, :], ALU.mult, ["lamt"], ["lamt"])
        P.op("dve", lambda e: e.tensor_reduce(out=smallf[:, 1:2], in_=lamt[:, 0, :], axis=AX.X, op=ALU.add), ["lamt"], ["sm1"])
        P.op("dve", lambda e: e.tensor_reduce(out=smallf[:, 2:3], in_=lamt[:, 2, :], axis=AX.X, op=ALU.add), ["lamt"], ["sm2"])
        act(smallf[:, 1:3], smallf[:, 1:3], AF.Exp, ["sm1", "sm2"], ["sm1", "sm2"])
        stt("dve", smallf[:, 0:1], smallf[:, 2:3], -LAMBDA_INIT, smallf[:, 1:2], ALU.add, ALU.subtract, ["sm1", "sm2"], ["neglam"])
        neg_lam = smallf[:, 0:1]
        ts("dve", sublnw[:], sublnw[:], 1.0 - LAMBDA_INIT, None, ALU.mult, None, ["sublnw"], ["sublnw"])
        act(sp16[:, 1, :], sp16[:, 1, :], AF.Exp, ["sp16"], ["sp16"])
        ts("dve", sp16[:, 1, :], sp16[:, 1, :], -1.0, None, ALU.mult, None, ["sp16"], ["sp16"])
        act(sp16[:, 3, :], sp16[:, 3, :], AF.Exp, ["sp16"], ["sp16"])
        dtb_bc = sp16[:, 0, :]; negA_bc = sp16[:, 1, :]; dsk_bc = sp16[:, 2, :]; esink_bc = sp16[:, 3, :]
        diagf = AR.alloc([16, 128], F32)
        diagh = sb("diagh", [128, 16, 128], BF16)
        diagl = sb("diagl", [128, 16, 128], BF16)
        tt("dve", diagf, bc(identf.unsqueeze(1), [128, 16, 128]), bc(dsk_bc.unsqueeze(2), [128, 16, 128]), ALU.mult, ["cf", "sp16"], ["diagf"])
        cp("dve", diagh[:], diagf, ["diagf"], ["diagh"])
        tt("dve", diagf, diagf, diagh[:], ALU.subtract, ["diagf", "diagh"], ["diagf"])
        cp("dve", diagl[:], diagf, ["diagf"], ["diagl"])
        P.barrier()
        def phase_norm(b, src, nw_dram, tag):
            AR.reset()
            xts = [AR.alloc([D], F32) for _ in range(2)]
            xtR = Rot([(xts[i], "xt%d" % i) for i in range(2)])
            hbs = [AR.alloc([D], BF16) for _ in range(2)]
            hbR = Rot([(hbs[i], "hb%d" % i) for i in range(2)])
            dma(normw[:], nw_dram.partition_broadcast(128), "normw", writes=["normw"])
            sscol = smallf[:, 16:16 + 3 * NT]
            memset("pool", sscol, 0.0, ["sscol"])
            trR = Rot([0, 1])
            for t in range(NT):
                xt, xres = xtR.next()
                hb, hres = hbR.next()
                dma(xt, src[b * S + t * 128: b * S + (t + 1) * 128, :], xres, writes=[xres])
                c0 = 16 + 3 * t
                act(junk[:], xt, AF.Square, [xres, "sscol"], ["junk", "sscol"], accum=smallf[:, c0:c0 + 1])
                act(smallf[:, c0 + 1:c0 + 2], smallf[:, c0:c0 + 1], AF.Ln, ["sscol"], ["sscol"], bias=EPS, scale=1.0 / D)
                act(smallf[:, c0 + 2:c0 + 3], smallf[:, c0 + 1:c0 + 2], AF.Exp, ["sscol"], ["sscol"], scale=-0.5)
                stt("dve", hb, xt, smallf[:, c0 + 2:c0 + 3], normw[:], ALU.mult, ALU.mult, [xres, "sscol", "normw"], [hres])
                bi = trR.next()
                pv = bkbf(bi).rearrange("p (c n) -> p c n", n=128)
                for c in range(8):
                    tr(pv[:, c, :], hb[:, c * 128:(c + 1) * 128], [hres], ["bank%d" % bi])
                cp(evac_rr.next(), hT[:, :, t * 128:(t + 1) * 128], pv, ["bank%d" % bi], [("hTw", t)])

        def proj_fm(dst, dst_res, wt, wres, pjR, bias_col=None):
            for n in range(S // 512):
                bi = pjR.next()
                for c in range(8):
                    mm(bk(bi), wt[:, c, :], hT[:, c, n * 512:(n + 1) * 512], c == 0, c == 7, [wres, "hT"], ["bank%d" % bi])
                if bias_col is None:
                    cp(evac_rr.next(), dst[:, n * 512:(n + 1) * 512], bk(bi), ["bank%d" % bi], [(dst_res, n)])
                else:
                    act(dst[:, n * 512:(n + 1) * 512], bk(bi), AF.Identity, ["bank%d" % bi, "bcol"], [(dst_res, n)], bias=bias_col)

        def proj_tm4(t0, wt, wres, bi):
            pv = bk(bi).rearrange("p (j n) -> p j n", n=128)
            for j in range(4):
                for c in range(8):
                    mm(pv[:, j, :], hT[:, c, (t0 + j) * 128:(t0 + j + 1) * 128], wt[:, c, :], c == 0, c == 7, [wres, "hT"], ["bank%d" % bi])
            return pv

        def phase_attn(b):
            AR.reset()
            wq = [AR.alloc([8, 128], BF16) for _ in range(8)]
            qT = AR.alloc([S], BF16); kTz = [AR.alloc([S], BF16) for _ in range(2)]
            vaug = AR.alloc([NT, 129], BF16); sg = AR.alloc([NT, 128], BF16)
            memset("pool", kTz[0], 0.0, ["kTzero"])
            memset("pool", kTz[1], 0.0, ["kTzero"])
            PTs = [AR.alloc([512], BF16) for _ in range(3)]
            PTR = Rot([(PTs[i], "PT%d" % i) for i in range(3)])
            a0 = AR.alloc([4, 128], F32); t1 = AR.alloc([4, 128], F32); attn = AR.alloc([4, 128], F32)
            sq = AR.alloc([4, 128], F32)
            ya = AR.alloc([4, 128], BF16)
            rec = smallf[:, 400:408]; ssq = smallf[:, 408:420]
            memset("pool", vaug[:, :, 128:129], 1.0, ["vaug"])
            pjR = Rot([6, 7])
            LR = Rot([4, 5])
            OR = Rot([(0, 1), (2, 3)])
            col0 = [0, 1024, 2048, 3072]
            for h in range(8):
                ws = h % 2
                for i in range(4):
                    load_w(wq[ws * 4 + i], "wq%d_%d" % (ws, i), w_in_a[:, col0[i] + h * 128: col0[i] + (h + 1) * 128])
                proj_fm(qT, "qT", wq[ws * 4 + 0], "wq%d_0" % ws, pjR)
                for n in range(S // 512):
                    bi = pjR.next()
                    for c in range(8):
                        mm(bk(bi), wq[ws * 4 + 1][:, c, :], hT[:, c, n * 512:(n + 1) * 512], c == 0, c == 7, ["wq%d_1" % ws, "hT"], ["bank%d" % bi])
                    cp("dve", kTz[0][0:64, n * 512:(n + 1) * 512], bk(bi)[0:64, :], ["bank%d" % bi, "kTzero"], [("kT", n)])
                    cp("dve", kTz[1][64:128, n * 512:(n + 1) * 512], bk(bi)[64:128, :], ["bank%d" % bi, "kTzero"], [("kT", n)])
                for t0 in range(0, NT, 4):
                    bi = pjR.next()
                    pv = proj_tm4(t0, wq[ws * 4 + 2], "wq%d_2" % ws, bi)
                    cp("dve", vaug[:, t0:t0 + 4, 0:128], pv, ["bank%d" % bi], ["vaug"])
                    bi = pjR.next()
                    pv = proj_tm4(t0, wq[ws * 4 + 3], "wq%d_3" % ws, bi)
                    act(sg[:, t0:t0 + 4, :], pv, AF.Silu, ["bank%d" % bi], ["sg"])
                WA = 256 if h == 0 else 512
                blocks = [(qt, m, kt) for qt in range(NQ) for m in range(2) for kt in range(4 * qt + 4)]
                binfo = {}
                ostate = {}

                def emit_qk(i):
                    qt, m, kt = blocks[i]
                    rows = slice(0, 128)
                    kT = kTz[m]
                    j = kt - 4 * qt
                    c0 = 128 * j if j > 0 else 0
                    li = LR.next()
                    Lres = "bank%d" % li
                    kq = [("kT", kt // 4), ("qT", qt)]
                    if j < 0:
                        mm(bk(li)[:, 0:512], kT[rows, kt * 128:(kt + 1) * 128], qT[rows, qt * 512:(qt + 1) * 512], True, True, kq, [Lres])
                    else:
                        mm(bk(li)[:, c0:c0 + 128], kT[rows, kt * 128:(kt + 1) * 128], qT[rows, qt * 512 + c0:qt * 512 + c0 + 128], True, False, kq, [Lres])
                        mm(bk(li)[:, c0:c0 + 128], ident, negT, False, True, ["cbf"], [Lres])
                        if c0 + 128 < 512:
                            mm(bk(li)[:, c0 + 128:512], kT[rows, kt * 128:(kt + 1) * 128], qT[rows, qt * 512 + c0 + 128:(qt + 1) * 512], True, True, kq, [Lres])
                    pt, ptres = PTR.next()
                    for a in range(512 // WA):
                        lo = max(c0, a * WA); hi = (a + 1) * WA
                        if lo >= hi:
                            continue
                        mval = kt - 4 * qt - (WA // 128) * a - (WA // 256)
                        assert -17 <= mval <= 1
                        act(pt[:, lo:hi], bk(li)[:, lo:hi], AF.Exp, [Lres, "cf"], [ptres], bias=abias[:, h, mval + 17:mval + 18], scale=0.125)
                    binfo[i] = (pt, ptres)

                def emit_pv(i):
                    qt, m, kt = blocks[i]
                    j = kt - 4 * qt
                    if kt == 0:
                        ob = OR.next()
                        ostate[(qt, m)] = (ob, "O%d" % ob[0])
                    ob, Ores = ostate[(qt, m)]
                    pt, ptres = binfo.pop(i)
                    for jj in range(max(j, 0), 4):
                        osub = banks[ob[jj // 2]][:, (jj % 2) * 129:(jj % 2) * 129 + 129]
                        mm(osub, pt[:, jj * 128:(jj + 1) * 128], vaug[:, kt, :], (kt == 0 and jj % 2 == 0), kt == 4 * qt + jj, [ptres, "vaug"], [Ores])
                    if kt == 4 * qt + 3 and m == 1:
                        evac(qt)

                def evac(qt):
                    def O4(ob, lo, hi):
                        return [banks[ob[k]][:, 0:258].rearrange("p (i e) -> p i e", e=129)[:, :, lo:hi] for k in range(2)]
                    (oa, ra), (ob_, rb) = ostate[(qt, 0)], ostate[(qt, 1)]
                    for k in range(2):
                        recip(rec[:, 2 * k:2 * k + 2], O4(oa, 128, 129)[k].rearrange("p i e -> p (i e)"), [ra], ["rec"])
                        recip(rec[:, 4 + 2 * k:4 + 2 * k + 2], O4(ob_, 128, 129)[k].rearrange("p i e -> p (i e)"), [rb], ["rec"])
                    ts("dve", rec[:, 4:8], rec[:, 4:8], neg_lam, None, ALU.mult, None, ["rec", "neglam"], ["rec"])
                    for k in range(2):
                        tt("dve", a0[:, 2 * k:2 * k + 2, :], O4(oa, 0, 128)[k], bc(rec[:, 2 * k:2 * k + 2].unsqueeze(2), [128, 2, 128]), ALU.mult, [ra, "rec"], ["a0"])
                        tt("dve", t1[:, 2 * k:2 * k + 2, :], O4(ob_, 0, 128)[k], bc(rec[:, 4 + 2 * k:4 + 2 * k + 2].unsqueeze(2), [128, 2, 128]), ALU.mult, [rb, "rec"], ["t1"])
                    tt("pool", attn, a0, t1, ALU.add, ["a0", "t1"], ["attn"])
                    tt("pool", sq, attn, attn, ALU.mult, ["attn"], ["sq"])
                    P.op("dve", lambda e: e.tensor_reduce(out=ssq[:, 0:4], in_=sq, axis=AX.X, op=ALU.add), ["sq"], ["ssq"])
                    act(ssq[:, 4:8], ssq[:, 0:4], AF.Ln, ["ssq"], ["ssq"], bias=EPS, scale=1.0 / 128)
                    act(ssq[:, 8:12], ssq[:, 4:8], AF.Exp, ["ssq"], ["ssq"], scale=-0.5)
                    tt("pool", attn, attn, bc(ssq[:, 8:12].unsqueeze(2), [128, 4, 128]), ALU.mult, ["attn", "ssq"], ["attn"])
                    tt("pool", attn, attn, bc(sublnw[:].unsqueeze(1), [128, 4, 128]), ALU.mult, ["attn", "sublnw"], ["attn"])
                    tt("pool", ya, attn, sg[:, 4 * qt:4 * qt + 4, :], ALU.mult, ["attn", "sg"], ["ya"])
                    dma(yd[b * S + qt * 512: b * S + (qt + 1) * 512, h * 128:(h + 1) * 128].rearrange("(j p) e -> p j e", p=128), ya,
                        "ya", reads=["ya"], writes=[("yd", b)])

                for i in range(len(blocks) + 1):
                    if i < len(blocks):
                        emit_qk(i)
                    if i >= 1:
                        emit_pv(i - 1)

        def phase_ssd(b):
            AR.reset()
            wx = [AR.alloc([8, 128], BF16) for _ in range(2)]
            wz = AR.alloc([8, 8, 128], BF16)
            wdt_f = AR.alloc([8, 16], F32); wdt = AR.alloc([8, 16], BF16)
            raw = AR.alloc([3 + S], F32); acc = AR.alloc([S], F32); xtmp = AR.alloc([S], BF16)
            xs_tok = AR.alloc([NT, 1024], BF16)
            bmT = AR.alloc([2, S], BF16); cmT = AR.alloc([2, S], BF16); bm_tok = AR.alloc([NT, 256], BF16)
            xpre = AR.alloc([NT, 16], F32); dtt = AR.alloc([NT, 16], F32); dA = AR.alloc([NT, 16], F32)
            cs = AR.alloc([NT, 16], F32); expcs = AR.alloc([NT, 16], F32); cdec = AR.alloc([NT, 16], F32)
            wds = AR.alloc([NT, 16], F32); tmp16 = AR.alloc([NT, 16], F32)
            dATri = AR.alloc([8, 128], F32); LmT = AR.alloc([8, 128], F32); MT = AR.alloc([8, 128], BF16)
            u1 = AR.alloc([8, 64], F32); u2 = AR.alloc([8, 64], F32); u3 = AR.alloc([512], F32)
            sz = AR.alloc([512], F32); xdtp = AR.alloc([8, 64], BF16); yb = AR.alloc([512], BF16)
            hst = [AR.alloc([8, 64], F32) for _ in range(2)]
            prevT = [AR.alloc([8, 64], BF16) for _ in range(2)]
            ssdnw = AR.alloc([D], F32)
            dma(ssdnw, ssd_norm.partition_broadcast(128), "ssdnw", writes=["ssdnw"])
            sscol = smallf[:, 16:16 + 3 * 2 * NT]
            memset("pool", sscol, 0.0, ["sscol"])
            memset("pool", raw[:, 0:3], 0.0, ["raw"])
            pjR = Rot([6, 7])
            import os
            KSSD = int(os.environ.get("KSSD", "9"))
            if KSSD < -3:
                return
            P.dma("sp", lambda e: e.dma_start(out=wdt_f, in_=w_in_a[:, 6656:6672].rearrange("(c p) n -> p c n", p=128)), "wdt", (), ["wdt_f"])
            cp("pool", wdt, wdt_f, ["wdt_f"], ["wdt"])
            dtp = bk(0)[:, 0:NT * 16].rearrange("p (t n) -> p t n", n=16)
            for t in range(NT):
                for c in range(8):
                    mm(dtp[:, t, :], hT[:, c, t * 128:(t + 1) * 128], wdt[:, c, :], c == 0, c == 7, ["wdt", "hT"], ["bank0"])
            tt("dve", xpre, dtp, bc(dtb_bc.unsqueeze(1), [128, NT, 16]), ALU.add, ["bank0", "sp16"], ["xpre"])
            act(tmp16, xpre, AF.Abs, ["xpre"], ["tmp16"])
            act(tmp16, tmp16, AF.Exp, ["tmp16"], ["tmp16"], scale=-1.0)
            act(tmp16, tmp16, AF.Ln, ["tmp16"], ["tmp16"], bias=1.0)
            stt("dve", dtt, xpre, 0.0, tmp16, ALU.max, ALU.add, ["xpre", "tmp16"], ["dtt"])
            tt("dve", dA, dtt, bc(negA_bc.unsqueeze(1), [128, NT, 16]), ALU.mult, ["dtt", "sp16"], ["dA"])
            if KSSD < -2:
                return
            csp = bk(1)[:, 0:NT * 16].rearrange("p (t n) -> p t n", n=16)
            for t in range(NT):
                mm(csp[:, t, :], Tri, dA[:, t, :], True, True, ["cf", "dA"], ["bank1"])
            cp("dve", cs, csp, ["bank1"], ["cs"])
            act(expcs, cs, AF.Exp, ["cs"], ["expcs"])
            if KSSD < -1:
                return
            clp = bk(2)[:, 0:NT * 16].rearrange("p (t n) -> p t n", n=16)
            KVAR = os.environ.get("KVAR", "")
            if KVAR == "V3":
                mm(bk(2)[:, 0:NT * 16], Tri, cs.rearrange("p t n -> p (t n)"), True, True, ["cf", "cs"], ["bank2"])
            elif KVAR == "V4":
                clp = bk(1)[:, 256:256 + NT * 16].rearrange("p (t n) -> p t n", n=16)
                mm(bk(1)[:, 256:256 + NT * 16], Sel, cs.rearrange("p t n -> p (t n)"), True, True, ["cf", "cs"], ["bank2"])
            else:
                mm(bk(2)[:, 0:NT * 16], Sel, cs.rearrange("p t n -> p (t n)"), True, True, ["cf", "cs"], ["bank2"])
            act(cdec, clp, AF.Exp, ["bank2"], ["cdec"])
            if KVAR in ("V2", "V3", "V4"):
                return
            tt("dve", tmp16, clp, cs, ALU.subtract, ["bank2", "cs"], ["tmp16"])
            act(tmp16, tmp16, AF.Exp, ["tmp16"], ["tmp16"])
            tt("dve", wds, tmp16, dtt, ALU.mult, ["tmp16", "dtt"], ["wds"])
            if KSSD < 1:
                return
            for i in range(8):
                load_w(wz[:, i, :, :], "wz", w_in_a[:, 4096 + i * 128: 4096 + (i + 1) * 128])
            trR = Rot([3, 4])
            for i in range(12):
                wt = wx[i % 2]; wres = "wx%d" % (i % 2)
                load_w(wt, wres, w_in_a[:, 5120 + i * 128: 5120 + (i + 1) * 128])
                for n in range(S // 512):
                    bi = pjR.next()
                    for c in range(8):
                        mm(bk(bi), wt[:, c, :], hT[:, c, n * 512:(n + 1) * 512], c == 0, c == 7, [wres, "hT"], ["bank%d" % bi])
                    cp(evac_rr.next(), raw[:, 3 + n * 512:3 + (n + 1) * 512], bk(bi), ["bank%d" % bi], ["raw"])
                ts("dve", acc, raw[:, 3:3 + S], convw[:, i, 3:4], convb[:, i:i + 1], ALU.mult, ALU.add, ["raw", "convw", "convb"], ["acc0"])
                for k in (2, 1, 0):
                    stt("dve", acc, raw[:, k:k + S], convw[:, i, k:k + 1], acc, ALU.mult, ALU.add, ["raw", "convw", "acc0"], ["acc0"])
                if i < 8:
                    dst, dres = xtmp, "xtmp"
                elif i < 10:
                    dst, dres = bmT[:, i - 8, :], "bmT"
                else:
                    dst, dres = cmT[:, i - 10, :], "cmT"
                act(dst, acc, AF.Silu, ["acc0"], [dres])
                if i < 10:
                    for t0 in range(0, NT, 4):
                        bi = trR.next()
                        pv = bkbf(bi)[:, 0:512].rearrange("p (j n) -> p j n", n=128)
                        for j in range(4):
                            tr(pv[:, j, :], dst[:, (t0 + j) * 128:(t0 + j + 1) * 128], [dres], ["bank%d" % bi])
                        if i < 8:
                            cp("dve", xs_tok[:, t0:t0 + 4, i * 128:(i + 1) * 128], pv, ["bank%d" % bi], ["xs_tok"])
                        else:
                            cp("dve", bm_tok[:, t0:t0 + 4, (i - 8) * 128:(i - 7) * 128], pv, ["bank%d" % bi], ["bm_tok"])
            if KSSD < 2:
                return
            zR = Rot([0, 7])
            for c in range(NT if KSSD >= 4 else 1):
                tok = slice(c * 128, (c + 1) * 128)
                for g in range(2):
                    hs = slice(g * 8, (g + 1) * 8)
                    zi = zR.next()
                    zp = bk(zi).rearrange("p (j n) -> p j n", n=128)
                    for j in range(4):
                        for kc in range(8):
                            mm(zp[:, j, :], hT[:, kc, tok], wz[:, g * 4 + j, kc, :], kc == 0, kc == 7, ["wz", "hT"], ["bank%d" % zi])
                    act(sz, bk(zi), AF.Silu, ["bank%d" % zi], ["sz"])
                    CB = bk(1)[:, 256:384]
                    mm(CB, bmT[:, g, tok], cmT[:, g, tok], True, True, ["bmT", "cmT"], ["bank1"])
                    tt("pool", dATri, bc(Tri.unsqueeze(1), [128, 8, 128]), bc(dA[:, c, hs].unsqueeze(2), [128, 8, 128]), ALU.mult, ["cf", "dA"], ["dATri"])
                    for hf in range(2):
                        mm(bk(2 + hf), U_, dATri[:, hf * 4:(hf + 1) * 4, :].rearrange("p a b -> p (a b)"), True, False, ["cf", "dATri"], ["bank%d" % (2 + hf)])
                        mm(bk(2 + hf), ident, NEG4, False, True, ["cbf"], ["bank%d" % (2 + hf)])
                        act(LmT[:, hf * 4:(hf + 1) * 4, :].rearrange("p a b -> p (a b)"), bk(2 + hf), AF.Exp, ["bank%d" % (2 + hf)], ["LmT"])
                    for hh in range(8):
                        stt("dve", MT[:, hh, :], LmT[:, hh, :], dtt[:, c, g * 8 + hh:g * 8 + hh + 1], CB, ALU.mult, ALU.mult, ["LmT", "dtt", "bank1"], ["MT"])
                    Yd = bk(4).rearrange("p (a b) -> p a b", b=64)
                    for hh in range(8):
                        H = g * 8 + hh
                        xs_h = xs_tok[:, c, H * 64:(H + 1) * 64]
                        mm(Yd[:, hh, :], MT[:, hh, :], xs_h, True, False, ["MT", "xs_tok"], ["bank4"])
                        mm(Yd[:, hh, :], diagh[:, H, :], xs_h, False, False, ["diagh", "xs_tok"], ["bank4"])
                        mm(Yd[:, hh, :], diagl[:, H, :], xs_h, False, True, ["diagl", "xs_tok"], ["bank4"])
                    if c > 0:
                        Yo = bk(5).rearrange("p (a b) -> p a b", b=64)
                        mm(bk(5), cmT[:, g, tok], prevT[g].rearrange("p a b -> p (a b)"), True, True, ["cmT", "prevT%d" % g], ["bank5"])
                        tt("dve", u1, Yo, bc(expcs[:, c, hs].unsqueeze(2), [128, 8, 64]), ALU.mult, ["bank5", "expcs"], ["u1"])
                        tt("dve", u2, Yd, u1, ALU.add, ["bank4", "u1"], ["u2"])
                    else:
                        cp("dve", u2, Yd, ["bank4"], ["u2"])
                    tt("dve", u3, u2.rearrange("p a b -> p (a b)"), sz, ALU.mult, ["u2", "sz"], ["u3"])
                    c0 = 16 + 3 * (2 * c + g)
                    act(junk[:, 0:512], u3, AF.Square, ["u3", "sscol"], ["junk", "sscol"], accum=smallf[:, c0:c0 + 1])
                    act(smallf[:, c0 + 1:c0 + 2], smallf[:, c0:c0 + 1], AF.Ln, ["sscol"], ["sscol"], bias=EPS, scale=1.0 / 512)
                    act(smallf[:, c0 + 2:c0 + 3], smallf[:, c0 + 1:c0 + 2], AF.Exp, ["sscol"], ["sscol"], scale=-0.5)
                    stt("dve", yb, u3, smallf[:, c0 + 2:c0 + 3], ssdnw[:, g * 512:(g + 1) * 512], ALU.mult, ALU.mult, ["u3", "sscol", "ssdnw"], ["yb"])
                    dma(yd[b * S + c * 128: b * S + (c + 1) * 128, 1024 + g * 512: 1024 + (g + 1) * 512], yb, "yb", reads=["yb"], writes=[("yd", b)])
                    if c < NT - 1:
                        tt("pool", xdtp, xs_tok[:, c, g * 512:(g + 1) * 512].rearrange("p (a b) -> p a b", b=64),
                           bc(wds[:, c, hs].unsqueeze(2), [128, 8, 64]), ALU.mult, ["xs_tok", "wds"], ["xdtp"])
                        mm(bk(6), bm_tok[:, c, g * 128:(g + 1) * 128], xdtp.rearrange("p a b -> p (a b)"), True, True, ["bm_tok", "xdtp"], ["bank6"])
                        STv = bk(6).rearrange("p (a b) -> p a b", b=64)
                        if c == 0:
                            cp("dve", hst[g], STv, ["bank6"], ["hst%d" % g])
                        else:
                            tt("dve", hst[g], hst[g], bc(cdec[:, c, hs].unsqueeze(2), [128, 8, 64]), ALU.mult, ["hst%d" % g, "cdec"], ["hst%d" % g])
                            tt("dve", hst[g], hst[g], STv, ALU.add, ["hst%d" % g, "bank6"], ["hst%d" % g])
                        cp("act", prevT[g], hst[g], ["hst%d" % g], ["prevT%d" % g])

        def phase_out(b, layer):
            AR.reset()
            KC = 16 if layer == 0 else 8
            wdram = w_out_a if layer == 0 else w_out_c
            ysrc = yd if layer == 0 else y2d
            xsrc = x if layer == 0 else x1d
            wout = AR.alloc([KC, D], BF16)
            for kh in range(KC // 8):
                for csl in range(8):
                    load_w(wout[:, kh * 8:(kh + 1) * 8, csl * 128:(csl + 1) * 128], "wout", wdram[kh * 1024:(kh + 1) * 1024, csl * 128:(csl + 1) * 128])
            yts = [AR.alloc([KC * 128], BF16) for _ in range(2)]
            yTts = [AR.alloc([KC, 128], BF16) for _ in range(2)]
            x1ts = [AR.alloc([D], F32) for _ in range(2)]
            ots = [AR.alloc([D], F32) for _ in range(2)]
            xts = [AR.alloc([D], F32) for _ in range(2)]
            xtR = Rot([(xts[i], "xt%d" % i) for i in range(2)])
            hbs = [AR.alloc([D], BF16) for _ in range(2)]
            dma(normw[:], (final_norm if layer == 1 else norm_c).partition_broadcast(128), "normw", writes=["normw"])
            sscol = smallf[:, 16:16 + 3 * NT]
            memset("pool", sscol, 0.0, ["sscol"])
            trR = Rot([0, 1, 2, 3])
            mmR = Rot([4, 5, 6, 7])
            for t in range(NT):
                s2 = t % 2
                rows = slice(b * S + t * 128, b * S + (t + 1) * 128)
                dma(yts[s2], ysrc[rows, :], "yt%d" % s2, reads=[("yd", b)], writes=["yt%d" % s2])
                xt, xres = xtR.next()
                dma(xt, xsrc[rows, :], xres, reads=[("x1d", b)] if layer == 1 else [], writes=[xres])
                for k0 in range(0, KC, 8):
                    bi = trR.next()
                    pv = bkbf(bi).rearrange("p (c n) -> p c n", n=128)
                    for k in range(8):
                        tr(pv[:, k, :], yts[s2][:, (k0 + k) * 128:(k0 + k + 1) * 128], ["yt%d" % s2], ["bank%d" % bi])
                    cp(evac_rr.next(), yTts[s2][:, k0:k0 + 8, :], pv, ["bank%d" % bi], ["yTt%d" % s2])
                for hf in range(2):
                    bi = mmR.next()
                    for k in range(KC):
                        mm(bk(bi), yTts[s2][:, k, :], wout[:, k, hf * 512:(hf + 1) * 512], k == 0, k == KC - 1, ["yTt%d" % s2, "wout"], ["bank%d" % bi])
                    tt("dve", x1ts[s2][:, hf * 512:(hf + 1) * 512], xt[:, hf * 512:(hf + 1) * 512], bk(bi), ALU.add, [xres, "bank%d" % bi], ["x1t%d" % s2])
                c0 = 16 + 3 * t
                act(junk[:], x1ts[s2], AF.Square, ["x1t%d" % s2, "sscol"], ["junk", "sscol"], accum=smallf[:, c0:c0 + 1])
                act(smallf[:, c0 + 1:c0 + 2], smallf[:, c0:c0 + 1], AF.Ln, ["sscol"], ["sscol"], bias=EPS, scale=1.0 / D)
                act(smallf[:, c0 + 2:c0 + 3], smallf[:, c0 + 1:c0 + 2], AF.Exp, ["sscol"], ["sscol"], scale=-0.5)
                if layer == 0:
                    dma(x1d[rows, :], x1ts[s2], "x1w%d" % s2, reads=["x1t%d" % s2], writes=[("x1d", b)])
                    hb = hbs[s2]
                    stt("dve", hb, x1ts[s2], smallf[:, c0 + 2:c0 + 3], normw[:], ALU.mult, ALU.mult, ["x1t%d" % s2, "sscol", "normw"], ["hb%d" % s2])
                    bi = trR.next()
                    pvn = bkbf(bi).rearrange("p (c n) -> p c n", n=128)
                    for c in range(8):
                        tr(pvn[:, c, :], hb[:, c * 128:(c + 1) * 128], ["hb%d" % s2], ["bank%d" % bi])
                    cp(evac_rr.next(), hT[:, :, t * 128:(t + 1) * 128], pvn, ["bank%d" % bi], [("hTw", t)])
                else:
                    stt("dve", ots[s2], x1ts[s2], smallf[:, c0 + 2:c0 + 3], normw[:], ALU.mult, ALU.mult, ["x1t%d" % s2, "sscol", "normw"], ["ot%d" % s2])
                    final.append(dma(out[rows, :], ots[s2], "ow%d" % s2, reads=["ot%d" % s2]))

        def phase_swa(b):
            AR.reset()
            swab = AR.alloc([16, 4, 128], BF16)
            dma(swab.rearrange("p a b c -> p (a b c)"), c_swa, "swab", writes=["swab"])
            wk = [AR.alloc([8, 128], BF16) for _ in range(2)]
            wqg = [AR.alloc([8, 128], BF16) for _ in range(4)]
            wv = AR.alloc([8, 128], BF16)
            kz = [[AR.alloc([S], BF16) for _ in range(2)] for _ in range(2)]
            for kv_ in range(2):
                memset("pool", kz[kv_][0], 0.0, ["kzero"])
                memset("pool", kz[kv_][1], 0.0, ["kzero"])
            vaug = AR.alloc([NT, 2, 65], BF16)
            qT = AR.alloc([S], BF16); sg = AR.alloc([NT, 128], BF16); gtmp = AR.alloc([4, 128], F32)
            PTs = [AR.alloc([2, 2, 128], BF16) for _ in range(2)]
            PTR = Rot([(PTs[i], "PT%d" % i) for i in range(2)])
            den = smallf[:, 400:404]
            of = AR.alloc([2, 64], F32); obf = [AR.alloc([128], BF16) for _ in range(2)]
            memset("pool", vaug[:, :, :, 64:65], 1.0, ["vaug"])
            bvg = AR.alloc([128 + D], F32)
            dma(bvg[:, 0:128], b_in_c[1152:1280].partition_broadcast(128), "bvg0", writes=["bvg0"])
            dma(bvg[:, 128:128 + D], b_in_c[1280:2304].partition_broadcast(128), "bvg1", writes=["bvg1"])
            pjR = Rot([6, 7])
            for kv in range(2):
                s_t, s_res = stgR.next()
                for hf in range(2):
                    dma(s_t[:, :, hf * 64:(hf + 1) * 64], w_in_c[:, 1024 + kv * 64:1024 + (kv + 1) * 64].rearrange("(c p) n -> p c n", p=128),
                        s_res, reads=([s_res] if hf == 1 else []), writes=[s_res])
                cp("pool", wk[kv], s_t[:], [s_res], ["wk%d" % kv])
                for n in range(S // 512):
                    bi = pjR.next()
                    for c in range(8):
                        mm(bk(bi), wk[kv][:, c, :], hT[:, c, n * 512:(n + 1) * 512], c == 0, c == 7, ["wk%d" % kv, "hT"], ["bank%d" % bi])
                    for i2 in range(2):
                        hs2 = slice(i2 * 64, (i2 + 1) * 64)
                        act(kz[kv][i2][hs2, n * 512:(n + 1) * 512], bk(bi)[hs2, :], AF.Identity, ["bank%d" % bi, "bcol", "kzero"], [("kT2_%d" % kv, n)],
                            bias=bcol[hs2, 8 + kv:9 + kv])
            load_w(wv, "wv", w_in_c[:, 1152:1280])
            for t0 in range(0, NT, 4):
                bi = pjR.next()
                pv = proj_tm4(t0, wv, "wv", bi)
                tt("dve", vaug[:, t0:t0 + 4, :, 0:64], pv.rearrange("p j (k e) -> p j k e", e=64),
                   bc(bvg[:, 0:128].rearrange("p (k e) -> p k e", e=64).unsqueeze(1), [128, 4, 2, 64]), ALU.add, ["bank%d" % bi, "bvg0"], ["vaug"])
            LR = Rot([0, 1, 2])
            OR = Rot([3, 4])
            trR = Rot([5])
            for j in range(8):
                kv = j // 4
                ws = j % 2
                load_w(wqg[ws * 2], "wqg%d" % (ws * 2), w_in_c[:, j * 128:(j + 1) * 128])
                load_w(wqg[ws * 2 + 1], "wqg%d" % (ws * 2 + 1), w_in_c[:, 1280 + j * 128:1280 + (j + 1) * 128])
                proj_fm(qT, "qT", wqg[ws * 2], "wqg%d" % (ws * 2), pjR, bias_col=bcol[:, j:j + 1])
                for t0 in range(0, NT, 4):
                    bi = pjR.next()
                    pv = proj_tm4(t0, wqg[ws * 2 + 1], "wqg%d" % (ws * 2 + 1), bi)
                    tt("dve", gtmp, pv, bc(bvg[:, 128 + j * 128:128 + (j + 1) * 128].unsqueeze(1), [128, 4, 128]), ALU.add, ["bank%d" % bi, "bvg1"], ["gtmp"])
                    act(sg[:, t0:t0 + 4, :], gtmp, AF.Silu, ["gtmp"], ["sg"])
                for n in range(NT):
                    li = LR.next(); Lres = "bank%d" % li
                    Lv = bk(li).rearrange("p (i k q) -> p i k q", k=2, q=128)
                    kts = (1,) if n == 0 else (0, 1)
                    for i in range(2):
                        hq = 2 * j + i
                        rows = slice(i * 64, (i + 1) * 64)
                        for kt in kts:
                            tk = n - 1 + kt
                            mm(Lv[:, i, kt, :], kz[kv][i][:, tk * 128:(tk + 1) * 128], qT[:, n * 128:(n + 1) * 128], True, False, [("kT2_%d" % kv, tk // 4), ("qT", n // 4)], [Lres])
                            mm(Lv[:, i, kt, :], ident, swab[:, hq, kt * 2 + 0, :], False, False, ["cbf", "swab"], [Lres])
                            mm(Lv[:, i, kt, :], ident, swab[:, hq, kt * 2 + 1, :], False, True, ["cbf", "swab"], [Lres])
                    pt, ptres = PTR.next()
                    if n == 0:
                        act(pt[:, :, 1, :], Lv[:, :, 1, :], AF.Exp, [Lres], [ptres], scale=0.125)
                    else:
                        act(pt, Lv, AF.Exp, [Lres], [ptres], scale=0.125)
                    oi = OR.next(); Ores = "bank%d" % oi
                    Ov = bk(oi)[:, 0:130].rearrange("p (i e) -> p i e", e=65)
                    for i in range(2):
                        for kt in kts:
                            tk = n - 1 + kt
                            mm(Ov[:, i, :], pt[:, i, kt, :], vaug[:, tk, kv, :], kt == kts[0], kt == kts[-1], [ptres, "vaug"], [Ores])
                    tt("dve", den[:, 0:2], Ov[:, :, 64:65].rearrange("p i e -> p (i e)"), esink_bc[:, 2 * j:2 * j + 2], ALU.add, [Ores, "sp16"], ["den"])
                    P.op("dve", lambda e: e.reciprocal(out=den[:, 2:4], in_=den[:, 0:2]), ["den"], ["den"])
                    tt("dve", of, Ov[:, :, 0:64], bc(den[:, 2:4].unsqueeze(2), [128, 2, 64]), ALU.mult, [Ores, "den"], ["of"])
                    ob2 = obf[n % 2]; obres = "obf%d" % (n % 2)
                    tt("pool", ob2, of.rearrange("p a b -> p (a b)"), sg[:, n, :], ALU.mult, ["of", "sg"], [obres])
                    dma(y2d[b * S + n * 128: b * S + (n + 1) * 128, j * 128:(j + 1) * 128], ob2, obres, reads=[obres], writes=[("yd", b)])

        import os
        KSTOP = os.environ.get("KSTOP", "all")
        order = ["norm0", "attn", "ssd", "out0", "norm1", "swa", "out1"]
        nphase = len(order) if KSTOP == "all" else (0 if KSTOP == "setup" else order.index(KSTOP) + 1)
        for b in range(NSEQ):
            for ph in order[:nphase]:
                if ph == "norm0":
                    phase_norm(b, x, norm_a, "a")
                elif ph == "attn":
                    phase_attn(b)
                elif ph == "ssd":
                    phase_ssd(b)
                elif ph == "out0":
                    phase_out(b, 0)
                elif ph == "norm1":
                    continue
                elif ph == "swa":
                    phase_swa(b)
                elif ph == "out1":
                    phase_out(b, 1)
                P.barrier()
        counts = P.emit(final)
        print("ops:", counts, "sbuf left", nc.sbuf_bytes_remaining)
    return nc


PARAMS = ["norm_a", "w_in_a", "conv_w_a", "conv_b_a", "dt_bias_a", "a_log_a", "d_skip_a", "ssd_norm_a",
          "lambda_q1_a", "lambda_k1_a", "lambda_q2_a", "lambda_k2_a", "subln_a", "w_out_a",
          "norm_c", "w_in_c", "b_in_c", "sinks_c", "w_out_c", "final_norm"]

_CACHE = {}


def run(inputs, n_cores=8):
    x = np.ascontiguousarray(np.asarray(inputs["x"], dtype=np.float32))
    B, S, _ = x.shape
    assert B % n_cores == 0
    NSEQ = B // n_cores
    key = (S, NSEQ)
    if key not in _CACHE:
        _CACHE[key] = build(S, NSEQ)
    nc = _CACHE[key]
    c_bf, c_f32, c_swa = make_consts()
    base = {"c_bf": c_bf, "c_f32": c_f32, "c_swa": c_swa}
    for k in PARAMS:
        a = np.asarray(inputs[k], dtype=np.float32)
        if k != "final_norm":
            a = a[0]
        base[k] = np.ascontiguousarray(a)
    in_maps = []
    for i in range(n_cores):
        m = dict(base)
        m["x"] = x[i * NSEQ:(i + 1) * NSEQ].reshape(NSEQ * S, D)
        in_maps.append(m)
    res = run_bass_kernel_spmd(nc, in_maps, core_ids=list(range(n_cores)))
    import os
    if os.environ.get("KDBG"):
        global DBG
        DBG = [{k: np.asarray(v) for k, v in r.items()} for r in res.results]
    outs = [np.asarray(r["out"]).reshape(NSEQ, S, D) for r in res.results]
    return np.concatenate(outs, axis=0).astype(np.float32)


def kernel(**inputs):
    return run(inputs, 8)
```

```python
import contextlib
import math
import numpy as np
import ml_dtypes
import concourse.bass as bass
import concourse.mybir as mybir
from concourse.bass_utils import run_bass_kernel_spmd

F32 = mybir.dt.float32
BF16 = mybir.dt.bfloat16
AF = mybir.ActivationFunctionType
ALU = mybir.AluOpType
AX = mybir.AxisListType

D = 1024
EVEN_IN = 6672
ODD_IN = 2304
EPS = 1e-5
NEGBIG = -30000.0
LAMBDA_INIT = 0.8 - 0.6 * math.exp(-0.3 * 0)
NM = 19


ENGS = ("pe", "act", "dve", "pool", "sp")


class Prog:
    def __init__(self, nc, same_engine_sync=True):
        self.nc = nc
        self.ops = {e: [] for e in ENGS}
        self.last_write = {}
        self.reads_since = {}
        self.known = {e: {} for e in ENGS}
        self.dma_count = {}
        self.same_engine_sync = same_engine_sync
        self.latest = {}

    def _add(self, eng, fn, reads, writes, dma_key=None, extra_deps=()):
        writes = tuple(writes) + tuple(r for r in reads if isinstance(r, str) and (r.startswith("bank") or (r[0] == "O" and r[1:].isdigit())))
        lst = self.ops[eng]
        idx = len(lst)
        deps = {}

        def need(tok):
            if tok is None:
                return
            k, i, clk = tok
            if deps.get(k, (-1, None))[0] < i:
                deps[k] = (i, clk)

        for r in reads:
            need(self.last_write.get(r))
        for w in writes:
            need(self.last_write.get(w))
            for tok in self.reads_since.get(w, {}).values():
                need(tok)
        for tok in extra_deps:
            need(tok)
        known = self.known[eng]
        waits = []
        for k, (i, clk) in deps.items():
            if k == eng and not (self.same_engine_sync and eng != "pe" and eng != "sp"):
                continue
            if known.get(k, -1) >= i:
                continue
            waits.append((k, i))
        for k, (i, clk) in deps.items():
            if k == eng and (k, i) not in waits:
                continue
            if known.get(k, -1) < i:
                known[k] = i
            for kk, vv in clk.items():
                if known.get(kk, -1) < vv:
                    known[kk] = vv
        op = dict(fn=fn, waits=waits, flag=False, dma_key=dma_key)
        lst.append(op)
        if dma_key is not None:
            n = self.dma_count.get(dma_key, 0) + 1
            self.dma_count[dma_key] = n
            tok = (("D", dma_key), n, dict(known))
        else:
            clk = dict(known)
            tok = (eng, idx, clk)
        self.latest[tok[0]] = tok
        for r in reads:
            self.reads_since.setdefault(r, {})[tok[0]] = tok
        for w in writes:
            self.last_write[w] = tok
            self.reads_since[w] = {}
        return tok


    def barrier(self):
        toks = list(self.latest.values())
        for e in ENGS:
            self._add(e, (lambda eng: eng.nop()), (), (), extra_deps=toks)

    def op(self, eng, fn, reads=(), writes=(), extra_deps=()):
        return self._add(eng, fn, tuple(reads), tuple(writes), extra_deps=extra_deps)

    def dma(self, eng, fn, key, reads=(), writes=(), extra_deps=()):
        return self._add(eng, fn, tuple(reads), tuple(writes), dma_key=key, extra_deps=extra_deps)

    def emit(self, final_tokens):
        nc = self.nc
        for e in ENGS:
            for op in self.ops[e]:
                for k, i in op["waits"]:
                    if not isinstance(k, tuple):
                        self.ops[k][i]["flag"] = True
        final_waits = []
        for k, i, _ in final_tokens:
            if isinstance(k, tuple):
                final_waits.append((k, i))
            else:
                self.ops[k][i]["flag"] = True
                final_waits.append((k, i))
        rank = {}
        for e in ENGS:
            c = 0
            r = {}
            for i, op in enumerate(self.ops[e]):
                if op["flag"]:
                    c += 1
                    r[i] = c
            rank[e] = r
        import contextlib
        with contextlib.ExitStack() as st:
            esem = {e: st.enter_context(nc.semaphore("s_" + e)) for e in ENGS}
            dsem = {}
            for key in self.dma_count:
                dsem[key] = st.enter_context(nc.semaphore("d_%s" % (str(key).replace(" ", ""))))
            block = st.enter_context(nc.Block())

            def run(e, engine):
                for i, op in enumerate(self.ops[e]):
                    for k, v in op["waits"]:
                        if isinstance(k, tuple):
                            engine.wait_ge(dsem[k[1]], 16 * v)
                        else:
                            engine.wait_ge(esem[k], rank[k][v])
                    ins = op["fn"](engine)
                    if op["dma_key"] is not None:
                        ins.then_inc(dsem[op["dma_key"]], 16)
                    elif op["flag"]:
                        ins.then_inc(esem[e], 1)
                if e == "sp":
                    for k, v in final_waits:
                        if isinstance(k, tuple):
                            engine.wait_ge(dsem[k[1]], 16 * v)
                        else:
                            engine.wait_ge(esem[k], rank[k][v])

            @block.tensor
            def _(eng):
                run("pe", eng)

            @block.scalar
            def _(eng):
                run("act", eng)

            @block.vector
            def _(eng):
                run("dve", eng)

            @block.gpsimd
            def _(eng):
                run("pool", eng)

            @block.sync
            def _(eng):
                run("sp", eng)
        return {e: len(self.ops[e]) for e in ENGS}


def make_consts():
    s = np.arange(128)
    ident = np.eye(128, dtype=np.float32)
    negT = np.where(s[None, :] < s[:, None], NEGBIG, 0.0).astype(np.float32)
    c_bf = np.concatenate([ident] + [negT] * 4, axis=1).astype(ml_dtypes.bfloat16)
    U = (s[:, None] > s[None, :]).astype(np.float32)
    Tri = (s[:, None] <= s[None, :]).astype(np.float32)
    Sel = np.zeros((128, 128), np.float32); Sel[127, :] = 1.0
    slopes = np.exp2(-8.0 * np.arange(1, 9, dtype=np.float32) / 8).astype(np.float32)
    ab = np.zeros((128, 8, NM), np.float32)
    for h in range(8):
        for mi in range(NM):
            m = mi - 17
            ab[:, h, mi] = slopes[h] * (128.0 * m + s)
    c_f32 = np.concatenate([U, Tri, Sel, ident, ab.reshape(128, 8 * NM)], axis=1).astype(np.float32)
    sl16 = np.exp2(-8.0 * np.arange(1, 17, dtype=np.float32) / 16).astype(np.float32)
    sw = np.zeros((128, 16, 2, 2, 128), np.float32)
    q = np.arange(128)
    for kt in range(2):
        srel = s - 128 if kt == 0 else s
        dist = q[None, :] - srel[:, None]
        valid = (dist >= 0) & (dist < 128)
        for h in range(16):
            val = np.where(valid, -sl16[h] * dist.astype(np.float32) * 8.0, NEGBIG * 8.0).astype(np.float32)
            hi = val.astype(ml_dtypes.bfloat16).astype(np.float32)
            lo = (val - hi).astype(ml_dtypes.bfloat16).astype(np.float32)
            sw[:, h, kt, 0, :] = hi
            sw[:, h, kt, 1, :] = lo
    c_swa = sw.reshape(128, -1).astype(ml_dtypes.bfloat16)
    return c_bf, c_f32, c_swa


class Rot:
    def __init__(self, items):
        self.items = list(items)
        self.i = 0

    def next(self):
        it = self.items[self.i % len(self.items)]
        self.i += 1
        return it


def build(S, NSEQ, dbg=()):
    NT = S // 128
    NQ = S // 512
    nc = bass.Bass("TRN2", target_bir_lowering=False)

    def din(name, shape, dt=F32):
        return nc.dram_tensor(name, list(shape), dt, kind="ExternalInput").ap()

    x = din("x", [NSEQ * S, D])
    norm_a = din("norm_a", [D]); w_in_a = din("w_in_a", [D, EVEN_IN])
    conv_w = din("conv_w_a", [4, 1536]); conv_b = din("conv_b_a", [1536])
    dt_bias = din("dt_bias_a", [16]); a_log = din("a_log_a", [16]); d_skip = din("d_skip_a", [16])
    ssd_norm = din("ssd_norm_a", [D])
    lq1 = din("lambda_q1_a", [64]); lk1 = din("lambda_k1_a", [64]); lq2 = din("lambda_q2_a", [64]); lk2 = din("lambda_k2_a", [64])
    subln = din("subln_a", [128]); w_out_a = din("w_out_a", [2048, D])
    norm_c = din("norm_c", [D]); w_in_c = din("w_in_c", [D, ODD_IN]); b_in_c = din("b_in_c", [ODD_IN])
    sinks = din("sinks_c", [16]); w_out_c = din("w_out_c", [D, D]); final_norm = din("final_norm", [D])
    c_bf = din("c_bf", [128, 640], BF16); c_f32 = din("c_f32", [128, 512 + 8 * NM], F32)
    c_swa = din("c_swa", [128, 16 * 2 * 2 * 128], BF16)
    out = nc.dram_tensor("out", [NSEQ * S, D], F32, kind="ExternalOutput").ap()
    import os
    _kw = {"kind": "ExternalOutput"} if os.environ.get("KDBG") else {}
    x1d = nc.dram_tensor("x1_scr", [NSEQ * S, D], F32, **_kw).ap()
    yd = nc.dram_tensor("y_scr", [NSEQ * S, 2048], BF16, **_kw).ap()
    y2d = nc.dram_tensor("y2_scr", [NSEQ * S, D], BF16, **_kw).ap()
    dbg_out = {}
    for name, shape in dbg:
        dbg_out[name] = nc.dram_tensor("dbg_" + name, list(shape), F32, kind="ExternalOutput").ap()

    st = contextlib.ExitStack()
    with st:
        def sb(name, shape, dt):
            return st.enter_context(nc.sbuf_tensor(name, list(shape), dt))

        P = Prog(nc, same_engine_sync=bool(int(os.environ.get("KSES", "1"))))
        final = []
        banks = [st.enter_context(nc.psum_tensor("bank%d" % i, [128, 512], F32)) for i in range(8)]

        def bk(i):
            return banks[i][:]

        def bkbf(i):
            return banks[i][:].bitcast(BF16)

        cbf = sb("cbf", [128, 640], BF16)
        cf = sb("cf", [128, 512 + 8 * NM], F32)
        ident = cbf[:, 0:128]
        negT = cbf[:, 128:256]
        NEG4 = cbf[:, 128:640]
        U_ = cf[:, 0:128]; Tri = cf[:, 128:256]; Sel = cf[:, 256:384]; identf = cf[:, 384:512]
        abias = cf[:, 512:512 + 8 * NM].rearrange("p (h m) -> p h m", m=NM)
        hT = sb("hT", [128, 8, S], BF16)
        normw = sb("normw", [128, D], F32)
        stg = [sb("stg%d" % i, [128, 8, 128], F32) for i in range(3)]
        stgR = Rot([(stg[i], "stg%d" % i) for i in range(3)])
        junk = sb("junk", [128, D], BF16)
        smallf = sb("smallf", [128, 512], F32)
        ARENA = 70000
        arena_t = sb("arena", [128, ARENA], BF16)

        class Arena:
            def __init__(self):
                self.off = 0

            def reset(self):
                self.off = 0

            def alloc(self, shape, dt):
                n = int(np.prod(shape))
                el = n * (2 if dt == F32 else 1)
                el = (el + 1) // 2 * 2
                assert self.off + el <= ARENA, ("arena overflow", self.off, el)
                v = arena_t[:, self.off:self.off + el]
                self.off += el
                if dt == F32:
                    v = v.bitcast(F32)
                if el != n * (2 if dt == F32 else 1):
                    v = v[:, 0:n]
                if len(shape) == 2:
                    v = v.rearrange("p (a b) -> p a b", b=shape[1])
                elif len(shape) == 3:
                    v = v.rearrange("p (a b c) -> p a b c", b=shape[1], c=shape[2])
                return v

        AR = Arena()

        def dma(out_ap, in_ap, key, reads=(), writes=(), eng="sp"):
            return P.dma(eng, lambda e: e.dma_start(out=out_ap, in_=in_ap), key, reads, writes)

        def mm(out_ap, lhsT, rhs, start, stop, reads, writes):
            return P.op("pe", lambda e: e.matmul(out_ap, lhsT=lhsT, rhs=rhs, start=start, stop=stop), reads, writes)

        def tr(out_ap, in_ap, reads, writes):
            return P.op("pe", lambda e: e.transpose(out=out_ap, in_=in_ap, identity=ident), tuple(reads) + ("cbf",), writes)

        def act(out_ap, in_ap, func, reads, writes, bias=None, scale=None, accum=None):
            kw = {}
            if bias is not None:
                kw["bias"] = bias
            if scale is not None:
                kw["scale"] = scale
            if accum is not None:
                kw["accum_out"] = accum
            return P.op("act", lambda e: e.activation(out=out_ap, in_=in_ap, func=func, **kw), reads, writes)

        def tt(eng, out_ap, in0, in1, op, reads, writes):
            return P.op(eng, lambda e: e.tensor_tensor(out=out_ap, in0=in0, in1=in1, op=op), reads, writes)

        def ts(eng, out_ap, in0, s1, s2, op0, op1, reads, writes):
            if op1 is None:
                return P.op(eng, lambda e: e.tensor_scalar(out=out_ap, in0=in0, scalar1=s1, scalar2=None, op0=op0), reads, writes)
            return P.op(eng, lambda e: e.tensor_scalar(out=out_ap, in0=in0, scalar1=s1, scalar2=s2, op0=op0, op1=op1), reads, writes)

        def stt(eng, out_ap, in0, scalar, in1, op0, op1, reads, writes):
            return P.op(eng, lambda e: e.scalar_tensor_tensor(out=out_ap, in0=in0, scalar=scalar, in1=in1, op0=op0, op1=op1), reads, writes)

        def cp(eng, out_ap, in_ap, reads, writes):
            if eng == "act":
                return act(out_ap, in_ap, AF.Copy, reads, writes)
            return P.op(eng, lambda e: e.tensor_copy(out=out_ap, in_=in_ap), reads, writes)

        def recip(out_ap, in_ap, reads, writes):
            return P.op("dve", lambda e: e.reciprocal(out=out_ap, in_=in_ap), reads, writes)

        def memset(eng, ap, val, writes):
            return P.op(eng, lambda e: e.memset(ap, val), (), writes)

        def bc(ap, shape):
            return ap.to_broadcast(list(shape))

        def load_w(dst_ap, dst_res, src_ap):
            s_t, s_res = stgR.next()
            dma(s_t[:], src_ap.rearrange("(c p) n -> p c n", p=128), s_res, writes=[s_res])
            cp("pool", dst_ap, s_t[:], [s_res], [dst_res])

        evac_rr = Rot(["act", "dve"])

        dma(cbf[:], c_bf, "setup", writes=["cbf"])
        dma(cf[:], c_f32, "setup", writes=["cf"])
        lamt = AR.alloc([4, 64], F32)
        for i, v in enumerate((lq1, lk1, lq2, lk2)):
            dma(lamt[:, i, :], v.partition_broadcast(128), "setup", writes=["lamt"])
        sublnw = sb("sublnw", [128, 128], F32)
        dma(sublnw[:], subln.partition_broadcast(128), "setup", writes=["sublnw"])
        sp16 = sb("sp16", [128, 4, 16], F32)
        dma(sp16[:, 0, :], dt_bias.partition_broadcast(128), "setup", writes=["sp16"])
        dma(sp16[:, 1, :], a_log.partition_broadcast(128), "setup", writes=["sp16"])
        dma(sp16[:, 2, :], d_skip.partition_broadcast(128), "setup", writes=["sp16"])
        dma(sp16[:, 3, :], sinks.partition_broadcast(128), "setup", writes=["sp16"])
        convw = sb("convw", [128, 12, 4], F32)
        convb = sb("convb", [128, 12], F32)
        for k4 in range(4):
            P.dma("sp", lambda e, k4=k4: e.dma_start(out=convw[:, :, k4], in_=conv_w[k4].rearrange("(c p) -> p c", p=128), allow_slow_non_contiguous=True), "setup", (), ["convw"])
        P.dma("sp", lambda e: e.dma_start(out=convb[:], in_=conv_b.rearrange("(c p) -> p c", p=128), allow_slow_non_contiguous=True), "setup", (), ["convb"])
        bcol = sb("bcol", [128, 12], F32)
        for j in range(8):
            P.dma("sp", lambda e, j=j: e.dma_start(out=bcol[:, j:j + 1], in_=b_in_c[j * 128:(j + 1) * 128].rearrange("(p o) -> p o", o=1), allow_slow_non_contiguous=True), "setup", (), ["bcol"])
        for kv in range(2):
            for hf in range(2):
                P.dma("sp", lambda e, kv=kv, hf=hf: e.dma_start(out=bcol[hf * 64:(hf + 1) * 64, 8 + kv:9 + kv],
                      in_=b_in_c[1024 + kv * 64:1024 + (kv + 1) * 64].rearrange("(p o) -> p o", o=1), allow_slow_non_contiguous=True),
                      "setup", (), ["bcol"])
        P.barrier()
        tt("dve", lamt[:, 0, :], lamt[:, 0, :], lamt[:, 1, :], ALU.mult, ["lamt"], ["lamt"])
        tt("dve", lamt[:, 2, :], lamt[:, 2, :], lamt[:, 3, :], ALU.mult, ["lamt"], ["lamt"])
        P.op("dve", lambda e: e.tensor_reduce(out=smallf[:, 1:2], in_=lamt[:, 0, :], axis=AX.X, op=ALU.add), ["lamt"], ["sm1"])
        P.op("dve", lambda e: e.tensor_reduce(out=smallf[:, 2:3], in_=lamt[:, 2, :], axis=AX.X, op=ALU.add), ["lamt"], ["sm2"])
        act(smallf[:, 1:3], smallf[:, 1:3], AF.Exp, ["sm1", "sm2"], ["sm1", "sm2"])
        stt("dve", smallf[:, 0:1], smallf[:, 2:3], -LAMBDA_INIT, smallf[:, 1:2], ALU.add, ALU.subtract, ["sm1", "sm2"], ["neglam"])
        neg_lam = smallf[:, 0:1]
        ts("dve", sublnw[:], sublnw[:], 1.0 - LAMBDA_INIT, None, ALU.mult, None, ["sublnw"], ["sublnw"])
        act(sp16[:, 1, :], sp16[:, 1, :], AF.Exp, ["sp16"], ["sp16"])
        ts("dve", sp16[:, 1, :], sp16[:, 1, :], -1.0, None, ALU.mult, None, ["sp16"], ["sp16"])
        act(sp16[:, 3, :], sp16[:, 3, :], AF.Exp, ["sp16"], ["sp16"])
        dtb_bc = sp16[:, 0, :]; negA_bc = sp16[:, 1, :]; dsk_bc = sp16[:, 2, :]; esink_bc = sp16[:, 3, :]
        diagf = AR.alloc([16, 128], F32)
        diagh = sb("diagh", [128, 16, 128], BF16)
        diagl = sb("diagl", [128, 16, 128], BF16)
        tt("dve", diagf, bc(identf.unsqueeze(1), [128, 16, 128]), bc(dsk_bc.unsqueeze(2), [128, 16, 128]), ALU.mult, ["cf", "sp16"], ["diagf"])
        cp("dve", diagh[:], diagf, ["diagf"], ["diagh"])
        tt("dve", diagf, diagf, diagh[:], ALU.subtract, ["diagf", "diagh"], ["diagf"])
        cp("dve", diagl[:], diagf, ["diagf"], ["diagl"])
        P.barrier()
        def phase_norm(b, src, nw_dram, tag):
            AR.reset()
            xts = [AR.alloc([D], F32) for _ in range(2)]
            xtR = Rot([(xts[i], "xt%d" % i) for i in range(2)])
            hbs = [AR.alloc([D], BF16) for _ in range(2)]
            hbR = Rot([(hbs[i], "hb%d" % i) for i in range(2)])
            dma(normw[:], nw_dram.partition_broadcast(128), "normw", writes=["normw"])
            sscol = smallf[:, 16:16 + 3 * NT]
            memset("pool", sscol, 0.0, ["sscol"])
            trR = Rot([0, 1])
            for t in range(NT):
                xt, xres = xtR.next()
                hb, hres = hbR.next()
                dma(xt, src[b * S + t * 128: b * S + (t + 1) * 128, :], xres, writes=[xres])
                c0 = 16 + 3 * t
                act(junk[:], xt, AF.Square, [xres, "sscol"], ["junk", "sscol"], accum=smallf[:, c0:c0 + 1])
                act(smallf[:, c0 + 1:c0 + 2], smallf[:, c0:c0 + 1], AF.Ln, ["sscol"], ["sscol"], bias=EPS, scale=1.0 / D)
                act(smallf[:, c0 + 2:c0 + 3], smallf[:, c0 + 1:c0 + 2], AF.Exp, ["sscol"], ["sscol"], scale=-0.5)
                stt("dve", hb, xt, smallf[:, c0 + 2:c0 + 3], normw[:], ALU.mult, ALU.mult, [xres, "sscol", "normw"], [hres])
                bi = trR.next()
                pv = bkbf(bi).rearrange("p (c n) -> p c n", n=128)
                for c in range(8):
                    tr(pv[:, c, :], hb[:, c * 128:(c + 1) * 128], [hres], ["bank%d" % bi])
                cp(evac_rr.next(), hT[:, :, t * 128:(t + 1) * 128], pv, ["bank%d" % bi], [("hTw", t)])

        def proj_fm(dst, dst_res, wt, wres, pjR, bias_col=None):
            for n in range(S // 512):
                bi = pjR.next()
                for c in range(8):
                    mm(bk(bi), wt[:, c, :], hT[:, c, n * 512:(n + 1) * 512], c == 0, c == 7, [wres, "hT"], ["bank%d" % bi])
                if bias_col is None:
                    cp(evac_rr.next(), dst[:, n * 512:(n + 1) * 512], bk(bi), ["bank%d" % bi], [(dst_res, n)])
                else:
                    act(dst[:, n * 512:(n + 1) * 512], bk(bi), AF.Identity, ["bank%d" % bi, "bcol"], [(dst_res, n)], bias=bias_col)

        def proj_tm4(t0, wt, wres, bi):
            pv = bk(bi).rearrange("p (j n) -> p j n", n=128)
            for j in range(4):
                for c in range(8):
                    mm(pv[:, j, :], hT[:, c, (t0 + j) * 128:(t0 + j + 1) * 128], wt[:, c, :], c == 0, c == 7, [wres, "hT"], ["bank%d" % bi])
            return pv

        def phase_attn(b):
            AR.reset()
            wq = [AR.alloc([8, 128], BF16) for _ in range(8)]
            qT = AR.alloc([S], BF16); kTz = [AR.alloc([S], BF16) for _ in range(2)]
            vaug = AR.alloc([NT, 129], BF16); sg = AR.alloc([NT, 128], BF16)
            memset("pool", kTz[0], 0.0, ["kTzero"])
            memset("pool", kTz[1], 0.0, ["kTzero"])
            PTs = [AR.alloc([512], BF16) for _ in range(3)]
            PTR = Rot([(PTs[i], "PT%d" % i) for i in range(3)])
            a0 = AR.alloc([4, 128], F32); t1 = AR.alloc([4, 128], F32); attn = AR.alloc([4, 128], F32)
            sq = AR.alloc([4, 128], F32)
            ya = AR.alloc([4, 128], BF16)
            rec = smallf[:, 400:408]; ssq = smallf[:, 408:420]
            memset("pool", vaug[:, :, 128:129], 1.0, ["vaug"])
            pjR = Rot([6, 7])
            LR = Rot([4, 5])
            OR = Rot([(0, 1), (2, 3)])
            col0 = [0, 1024, 2048, 3072]
            for h in range(8):
                ws = h % 2
                for i in range(4):
                    load_w(wq[ws * 4 + i], "wq%d_%d" % (ws, i), w_in_a[:, col0[i] + h * 128: col0[i] + (h + 1) * 128])
                proj_fm(qT, "qT", wq[ws * 4 + 0], "wq%d_0" % ws, pjR)
                for n in range(S // 512):
                    bi = pjR.next()
                    for c in range(8):
                        mm(bk(bi), wq[ws * 4 + 1][:, c, :], hT[:, c, n * 512:(n + 1) * 512], c == 0, c == 7, ["wq%d_1" % ws, "hT"], ["bank%d" % bi])
                    cp("dve", kTz[0][0:64, n * 512:(n + 1) * 512], bk(bi)[0:64, :], ["bank%d" % bi, "kTzero"], [("kT", n)])
                    cp("dve", kTz[1][64:128, n * 512:(n + 1) * 512], bk(bi)[64:128, :], ["bank%d" % bi, "kTzero"], [("kT", n)])
                for t0 in range(0, NT, 4):
                    bi = pjR.next()
                    pv = proj_tm4(t0, wq[ws * 4 + 2], "wq%d_2" % ws, bi)
                    cp("dve", vaug[:, t0:t0 + 4, 0:128], pv, ["bank%d" % bi], ["vaug"])
                    bi = pjR.next()
                    pv = proj_tm4(t0, wq[ws * 4 + 3], "wq%d_3" % ws, bi)
                    act(sg[:, t0:t0 + 4, :], pv, AF.Silu, ["bank%d" % bi], ["sg"])
                WA = 256 if h == 0 else 512
                blocks = [(qt, m, kt) for qt in range(NQ) for m in range(2) for kt in range(4 * qt + 4)]
                binfo = {}
                ostate = {}

                def emit_qk(i):
                    qt, m, kt = blocks[i]
                    rows = slice(0, 128)
                    kT = kTz[m]
                    j = kt - 4 * qt
                    c0 = 128 * j if j > 0 else 0
                    li = LR.next()
                    Lres = "bank%d" % li
                    kq = [("kT", kt // 4), ("qT", qt)]
                    if j < 0:
                        mm(bk(li)[:, 0:512], kT[rows, kt * 128:(kt + 1) * 128], qT[rows, qt * 512:(qt + 1) * 512], True, True, kq, [Lres])
                    else:
                        mm(bk(li)[:, c0:c0 + 128], kT[rows, kt * 128:(kt + 1) * 128], qT[rows, qt * 512 + c0:qt * 512 + c0 + 128], True, False, kq, [Lres])
                        mm(bk(li)[:, c0:c0 + 128], ident, negT, False, True, ["cbf"], [Lres])
                        if c0 + 128 < 512:
                            mm(bk(li)[:, c0 + 128:512], kT[rows, kt * 128:(kt + 1) * 128], qT[rows, qt * 512 + c0 + 128:(qt + 1) * 512], True, True, kq, [Lres])
                    pt, ptres = PTR.next()
                    for a in range(512 // WA):
                        lo = max(c0, a * WA); hi = (a + 1) * WA
                        if lo >= hi:
                            continue
                        mval = kt - 4 * qt - (WA // 128) * a - (WA // 256)
                        assert -17 <= mval <= 1
                        act(pt[:, lo:hi], bk(li)[:, lo:hi], AF.Exp, [Lres, "cf"], [ptres], bias=abias[:, h, mval + 17:mval + 18], scale=0.125)
                    binfo[i] = (pt, ptres)

                def emit_pv(i):
                    qt, m, kt = blocks[i]
                    j = kt - 4 * qt
                    if kt == 0:
                        ob = OR.next()
                        ostate[(qt, m)] = (ob, "O%d" % ob[0])
                    ob, Ores = ostate[(qt, m)]
                    pt, ptres = binfo.pop(i)
                    for jj in range(max(j, 0), 4):
                        osub = banks[ob[jj // 2]][:, (jj % 2) * 129:(jj % 2) * 129 + 129]
                        mm(osub, pt[:, jj * 128:(jj + 1) * 128], vaug[:, kt, :], (kt == 0 and jj % 2 == 0), kt == 4 * qt + jj, [ptres, "vaug"], [Ores])
                    if kt == 4 * qt + 3:
                        if m == 0:
                            evac_a(qt)
                        else:
                            evac(qt)

                def O4(ob, lo, hi):
                    return [banks[ob[k]][:, 0:258].rearrange("p (i e) -> p i e", e=129)[:, :, lo:hi] for k in range(2)]

                def evac_a(qt):
                    oa, ra = ostate[(qt, 0)]
                    for k in range(2):
                        recip(rec[:, 2 * k:2 * k + 2], O4(oa, 128, 129)[k].rearrange("p i e -> p (i e)"), [ra], ["recA"])
                    for k in range(2):
                        tt("dve", a0[:, 2 * k:2 * k + 2, :], O4(oa, 0, 128)[k], bc(rec[:, 2 * k:2 * k + 2].unsqueeze(2), [128, 2, 128]), ALU.mult, [ra, "recA"], ["a0"])

                def evac(qt):
                    ob_, rb = ostate[(qt, 1)]
                    for k in range(2):
                        recip(rec[:, 4 + 2 * k:4 + 2 * k + 2], O4(ob_, 128, 129)[k].rearrange("p i e -> p (i e)"), [rb], ["rec"])
                    ts("dve", rec[:, 4:8], rec[:, 4:8], neg_lam, None, ALU.mult, None, ["rec", "neglam"], ["rec"])
                    for k in range(2):
                        tt("dve", t1[:, 2 * k:2 * k + 2, :], O4(ob_, 0, 128)[k], bc(rec[:, 4 + 2 * k:4 + 2 * k + 2].unsqueeze(2), [128, 2, 128]), ALU.mult, [rb, "rec"], ["t1"])
                    tt("pool", attn, a0, t1, ALU.add, ["a0", "t1"], ["attn"])
                    tt("pool", sq, attn, attn, ALU.mult, ["attn"], ["sq"])
                    P.op("dve", lambda e: e.tensor_reduce(out=ssq[:, 0:4], in_=sq, axis=AX.X, op=ALU.add), ["sq"], ["ssq"])
                    act(ssq[:, 4:8], ssq[:, 0:4], AF.Ln, ["ssq"], ["ssq"], bias=EPS, scale=1.0 / 128)
                    act(ssq[:, 8:12], ssq[:, 4:8], AF.Exp, ["ssq"], ["ssq"], scale=-0.5)
                    tt("pool", attn, attn, bc(ssq[:, 8:12].unsqueeze(2), [128, 4, 128]), ALU.mult, ["attn", "ssq"], ["attn"])
                    tt("pool", attn, attn, bc(sublnw[:].unsqueeze(1), [128, 4, 128]), ALU.mult, ["attn", "sublnw"], ["attn"])
                    tt("pool", ya, attn, sg[:, 4 * qt:4 * qt + 4, :], ALU.mult, ["attn", "sg"], ["ya"])
                    dma(yd[b * S + qt * 512: b * S + (qt + 1) * 512, h * 128:(h + 1) * 128].rearrange("(j p) e -> p j e", p=128), ya,
                        "ya", reads=["ya"], writes=[("yd", b)])

                for i in range(len(blocks) + 1):
                    if i < len(blocks):
                        emit_qk(i)
                    if i >= 1:
                        emit_pv(i - 1)

        def phase_ssd(b):
            AR.reset()
            wx = [AR.alloc([8, 128], BF16) for _ in range(2)]
            wz = AR.alloc([8, 8, 128], BF16)
            wdt_f = AR.alloc([8, 16], F32); wdt = AR.alloc([8, 16], BF16)
            raw = AR.alloc([3 + S], F32); acc = AR.alloc([S], F32); xtmp = AR.alloc([S], BF16)
            xs_tok = AR.alloc([NT, 1024], BF16)
            bmT = AR.alloc([2, S], BF16); cmT = AR.alloc([2, S], BF16); bm_tok = AR.alloc([NT, 256], BF16)
            xpre = AR.alloc([NT, 16], F32); dtt = AR.alloc([NT, 16], F32); dA = AR.alloc([NT, 16], F32)
            cs = AR.alloc([NT, 16], F32); expcs = AR.alloc([NT, 16], F32); cdec = AR.alloc([NT, 16], F32)
            wds = AR.alloc([NT, 16], F32); tmp16 = AR.alloc([NT, 16], F32)
            dATri = AR.alloc([8, 128], F32); LmT = AR.alloc([8, 128], F32); MT = AR.alloc([8, 128], BF16)
            u1 = AR.alloc([8, 64], F32); u2 = AR.alloc([8, 64], F32); u3 = AR.alloc([512], F32)
            sz = AR.alloc([512], F32); xdtp = AR.alloc([8, 64], BF16); yb = AR.alloc([512], BF16)
            hst = [AR.alloc([8, 64], F32) for _ in range(2)]
            prevT = [AR.alloc([8, 64], BF16) for _ in range(2)]
            ssdnw = AR.alloc([D], F32)
            dma(ssdnw, ssd_norm.partition_broadcast(128), "ssdnw", writes=["ssdnw"])
            sscol = smallf[:, 16:16 + 3 * 2 * NT]
            memset("pool", sscol, 0.0, ["sscol"])
            memset("pool", raw[:, 0:3], 0.0, ["raw"])
            pjR = Rot([6, 7])
            import os
            KSSD = int(os.environ.get("KSSD", "9"))
            if KSSD < -3:
                return
            P.dma("sp", lambda e: e.dma_start(out=wdt_f, in_=w_in_a[:, 6656:6672].rearrange("(c p) n -> p c n", p=128)), "wdt", (), ["wdt_f"])
            cp("pool", wdt, wdt_f, ["wdt_f"], ["wdt"])
            dtp = bk(0)[:, 0:NT * 16].rearrange("p (t n) -> p t n", n=16)
            for t in range(NT):
                for c in range(8):
                    mm(dtp[:, t, :], hT[:, c, t * 128:(t + 1) * 128], wdt[:, c, :], c == 0, c == 7, ["wdt", "hT"], ["bank0"])
            tt("dve", xpre, dtp, bc(dtb_bc.unsqueeze(1), [128, NT, 16]), ALU.add, ["bank0", "sp16"], ["xpre"])
            act(tmp16, xpre, AF.Abs, ["xpre"], ["tmp16"])
            act(tmp16, tmp16, AF.Exp, ["tmp16"], ["tmp16"], scale=-1.0)
            act(tmp16, tmp16, AF.Ln, ["tmp16"], ["tmp16"], bias=1.0)
            stt("dve", dtt, xpre, 0.0, tmp16, ALU.max, ALU.add, ["xpre", "tmp16"], ["dtt"])
            tt("dve", dA, dtt, bc(negA_bc.unsqueeze(1), [128, NT, 16]), ALU.mult, ["dtt", "sp16"], ["dA"])
            if KSSD < -2:
                return
            csp = bk(1)[:, 0:NT * 16].rearrange("p (t n) -> p t n", n=16)
            for t in range(NT):
                mm(csp[:, t, :], Tri, dA[:, t, :], True, True, ["cf", "dA"], ["bank1"])
            cp("dve", cs, csp, ["bank1"], ["cs"])
            act(expcs, cs, AF.Exp, ["cs"], ["expcs"])
            if KSSD < -1:
                return
            clp = bk(2)[:, 0:NT * 16].rearrange("p (t n) -> p t n", n=16)
            KVAR = os.environ.get("KVAR", "")
            if KVAR == "V3":
                mm(bk(2)[:, 0:NT * 16], Tri, cs.rearrange("p t n -> p (t n)"), True, True, ["cf", "cs"], ["bank2"])
            elif KVAR == "V4":
                clp = bk(1)[:, 256:256 + NT * 16].rearrange("p (t n) -> p t n", n=16)
                mm(bk(1)[:, 256:256 + NT * 16], Sel, cs.rearrange("p t n -> p (t n)"), True, True, ["cf", "cs"], ["bank2"])
            else:
                mm(bk(2)[:, 0:NT * 16], Sel, cs.rearrange("p t n -> p (t n)"), True, True, ["cf", "cs"], ["bank2"])
            act(cdec, clp, AF.Exp, ["bank2"], ["cdec"])
            if KVAR in ("V2", "V3", "V4"):
                return
            tt("dve", tmp16, clp, cs, ALU.subtract, ["bank2", "cs"], ["tmp16"])
            act(tmp16, tmp16, AF.Exp, ["tmp16"], ["tmp16"])
            tt("dve", wds, tmp16, dtt, ALU.mult, ["tmp16", "dtt"], ["wds"])
            if KSSD < 1:
                return
            for i in range(8):
                load_w(wz[:, i, :, :], "wz", w_in_a[:, 4096 + i * 128: 4096 + (i + 1) * 128])
            trR = Rot([3, 4])
            for i in range(12):
                wt = wx[i % 2]; wres = "wx%d" % (i % 2)
                load_w(wt, wres, w_in_a[:, 5120 + i * 128: 5120 + (i + 1) * 128])
                for n in range(S // 512):
                    bi = pjR.next()
                    for c in range(8):
                        mm(bk(bi), wt[:, c, :], hT[:, c, n * 512:(n + 1) * 512], c == 0, c == 7, [wres, "hT"], ["bank%d" % bi])
                    cp(evac_rr.next(), raw[:, 3 + n * 512:3 + (n + 1) * 512], bk(bi), ["bank%d" % bi], ["raw"])
                ts("dve", acc, raw[:, 3:3 + S], convw[:, i, 3:4], convb[:, i:i + 1], ALU.mult, ALU.add, ["raw", "convw", "convb"], ["acc0"])
                for k in (2, 1, 0):
                    stt("dve", acc, raw[:, k:k + S], convw[:, i, k:k + 1], acc, ALU.mult, ALU.add, ["raw", "convw", "acc0"], ["acc0"])
                if i < 8:
                    dst, dres = xtmp, "xtmp"
                elif i < 10:
                    dst, dres = bmT[:, i - 8, :], "bmT"
                else:
                    dst, dres = cmT[:, i - 10, :], "cmT"
                act(dst, acc, AF.Silu, ["acc0"], [dres])
                if i < 10:
                    for t0 in range(0, NT, 4):
                        bi = trR.next()
                        pv = bkbf(bi)[:, 0:512].rearrange("p (j n) -> p j n", n=128)
                        for j in range(4):
                            tr(pv[:, j, :], dst[:, (t0 + j) * 128:(t0 + j + 1) * 128], [dres], ["bank%d" % bi])
                        if i < 8:
                            cp("dve", xs_tok[:, t0:t0 + 4, i * 128:(i + 1) * 128], pv, ["bank%d" % bi], ["xs_tok"])
                        else:
                            cp("dve", bm_tok[:, t0:t0 + 4, (i - 8) * 128:(i - 7) * 128], pv, ["bank%d" % bi], ["bm_tok"])
            if KSSD < 2:
                return
            zR = Rot([0, 7])
            for c in range(NT if KSSD >= 4 else 1):
                tok = slice(c * 128, (c + 1) * 128)
                for g in range(2):
                    hs = slice(g * 8, (g + 1) * 8)
                    zi = zR.next()
                    zp = bk(zi).rearrange("p (j n) -> p j n", n=128)
                    for j in range(4):
                        for kc in range(8):
                            mm(zp[:, j, :], hT[:, kc, tok], wz[:, g * 4 + j, kc, :], kc == 0, kc == 7, ["wz", "hT"], ["bank%d" % zi])
                    act(sz, bk(zi), AF.Silu, ["bank%d" % zi], ["sz"])
                    CB = bk(1)[:, 256:384]
                    mm(CB, bmT[:, g, tok], cmT[:, g, tok], True, True, ["bmT", "cmT"], ["bank1"])
                    tt("dve", dATri, bc(Tri.unsqueeze(1), [128, 8, 128]), bc(dA[:, c, hs].unsqueeze(2), [128, 8, 128]), ALU.mult, ["cf", "dA"], ["dATri"])
                    for hf in range(2):
                        mm(bk(2 + hf), U_, dATri[:, hf * 4:(hf + 1) * 4, :].rearrange("p a b -> p (a b)"), True, False, ["cf", "dATri"], ["bank%d" % (2 + hf)])
                        mm(bk(2 + hf), ident, NEG4, False, True, ["cbf"], ["bank%d" % (2 + hf)])
                        act(LmT[:, hf * 4:(hf + 1) * 4, :].rearrange("p a b -> p (a b)"), bk(2 + hf), AF.Exp, ["bank%d" % (2 + hf)], ["LmT"])
                    for hh in range(8):
                        stt("dve", MT[:, hh, :], LmT[:, hh, :], dtt[:, c, g * 8 + hh:g * 8 + hh + 1], CB, ALU.mult, ALU.mult, ["LmT", "dtt", "bank1"], ["MT"])
                    Yd = bk(4).rearrange("p (a b) -> p a b", b=64)
                    for hh in range(8):
                        H = g * 8 + hh
                        xs_h = xs_tok[:, c, H * 64:(H + 1) * 64]
                        mm(Yd[:, hh, :], MT[:, hh, :], xs_h, True, False, ["MT", "xs_tok"], ["bank4"])
                        mm(Yd[:, hh, :], diagh[:, H, :], xs_h, False, False, ["diagh", "xs_tok"], ["bank4"])
                        mm(Yd[:, hh, :], diagl[:, H, :], xs_h, False, True, ["diagl", "xs_tok"], ["bank4"])
                    if c > 0:
                        Yo = bk(5).rearrange("p (a b) -> p a b", b=64)
                        mm(bk(5), cmT[:, g, tok], prevT[g].rearrange("p a b -> p (a b)"), True, True, ["cmT", "prevT%d" % g], ["bank5"])
                        tt("dve", u1, Yo, bc(expcs[:, c, hs].unsqueeze(2), [128, 8, 64]), ALU.mult, ["bank5", "expcs"], ["u1"])
                        tt("dve", u2, Yd, u1, ALU.add, ["bank4", "u1"], ["u2"])
                    else:
                        cp("dve", u2, Yd, ["bank4"], ["u2"])
                    tt("dve", u3, u2.rearrange("p a b -> p (a b)"), sz, ALU.mult, ["u2", "sz"], ["u3"])
                    c0 = 16 + 3 * (2 * c + g)
                    act(junk[:, 0:512], u3, AF.Square, ["u3", "sscol"], ["junk", "sscol"], accum=smallf[:, c0:c0 + 1])
                    act(smallf[:, c0 + 1:c0 + 2], smallf[:, c0:c0 + 1], AF.Ln, ["sscol"], ["sscol"], bias=EPS, scale=1.0 / 512)
                    act(smallf[:, c0 + 2:c0 + 3], smallf[:, c0 + 1:c0 + 2], AF.Exp, ["sscol"], ["sscol"], scale=-0.5)
                    stt("dve", yb, u3, smallf[:, c0 + 2:c0 + 3], ssdnw[:, g * 512:(g + 1) * 512], ALU.mult, ALU.mult, ["u3", "sscol", "ssdnw"], ["yb"])
                    dma(yd[b * S + c * 128: b * S + (c + 1) * 128, 1024 + g * 512: 1024 + (g + 1) * 512], yb, "yb", reads=["yb"], writes=[("yd", b)])
                    if c < NT - 1:
                        tt("pool", xdtp, xs_tok[:, c, g * 512:(g + 1) * 512].rearrange("p (a b) -> p a b", b=64),
                           bc(wds[:, c, hs].unsqueeze(2), [128, 8, 64]), ALU.mult, ["xs_tok", "wds"], ["xdtp"])
                        mm(bk(6), bm_tok[:, c, g * 128:(g + 1) * 128], xdtp.rearrange("p a b -> p (a b)"), True, True, ["bm_tok", "xdtp"], ["bank6"])
                        STv = bk(6).rearrange("p (a b) -> p a b", b=64)
                        if c == 0:
                            cp("dve", hst[g], STv, ["bank6"], ["hst%d" % g])
                        else:
                            tt("dve", hst[g], hst[g], bc(cdec[:, c, hs].unsqueeze(2), [128, 8, 64]), ALU.mult, ["hst%d" % g, "cdec"], ["hst%d" % g])
                            tt("dve", hst[g], hst[g], STv, ALU.add, ["hst%d" % g, "bank6"], ["hst%d" % g])
                        cp("act", prevT[g], hst[g], ["hst%d" % g], ["prevT%d" % g])

        def phase_out(b, layer):
            AR.reset()
            KC = 16 if layer == 0 else 8
            wdram = w_out_a if layer == 0 else w_out_c
            ysrc = yd if layer == 0 else y2d
            xsrc = x if layer == 0 else x1d
            wout = AR.alloc([KC, D], BF16)
            for kh in range(KC // 8):
                for csl in range(8):
                    load_w(wout[:, kh * 8:(kh + 1) * 8, csl * 128:(csl + 1) * 128], "wout", wdram[kh * 1024:(kh + 1) * 1024, csl * 128:(csl + 1) * 128])
            yts = [AR.alloc([KC * 128], BF16) for _ in range(2)]
            yTts = [AR.alloc([KC, 128], BF16) for _ in range(2)]
            x1ts = [AR.alloc([D], F32) for _ in range(2)]
            ots = [AR.alloc([D], F32) for _ in range(2)]
            xts = [AR.alloc([D], F32) for _ in range(2)]
            xtR = Rot([(xts[i], "xt%d" % i) for i in range(2)])
            hbs = [AR.alloc([D], BF16) for _ in range(2)]
            dma(normw[:], (final_norm if layer == 1 else norm_c).partition_broadcast(128), "normw", writes=["normw"])
            sscol = smallf[:, 16:16 + 3 * NT]
            memset("pool", sscol, 0.0, ["sscol"])
            trR = Rot([0, 1, 2, 3])
            mmR = Rot([4, 5, 6, 7])
            for t in range(NT):
                s2 = t % 2
                rows = slice(b * S + t * 128, b * S + (t + 1) * 128)
                dma(yts[s2], ysrc[rows, :], "yt%d" % s2, reads=[("yd", b)], writes=["yt%d" % s2])
                xt, xres = xtR.next()
                dma(xt, xsrc[rows, :], xres, reads=[("x1d", b)] if layer == 1 else [], writes=[xres])
                for k0 in range(0, KC, 8):
                    bi = trR.next()
                    pv = bkbf(bi).rearrange("p (c n) -> p c n", n=128)
                    for k in range(8):
                        tr(pv[:, k, :], yts[s2][:, (k0 + k) * 128:(k0 + k + 1) * 128], ["yt%d" % s2], ["bank%d" % bi])
                    cp(evac_rr.next(), yTts[s2][:, k0:k0 + 8, :], pv, ["bank%d" % bi], ["yTt%d" % s2])
                for hf in range(2):
                    bi = mmR.next()
                    for k in range(KC):
                        mm(bk(bi), yTts[s2][:, k, :], wout[:, k, hf * 512:(hf + 1) * 512], k == 0, k == KC - 1, ["yTt%d" % s2, "wout"], ["bank%d" % bi])
                    tt("dve", x1ts[s2][:, hf * 512:(hf + 1) * 512], xt[:, hf * 512:(hf + 1) * 512], bk(bi), ALU.add, [xres, "bank%d" % bi], ["x1t%d" % s2])
                c0 = 16 + 3 * t
                act(junk[:], x1ts[s2], AF.Square, ["x1t%d" % s2, "sscol"], ["junk", "sscol"], accum=smallf[:, c0:c0 + 1])
                act(smallf[:, c0 + 1:c0 + 2], smallf[:, c0:c0 + 1], AF.Ln, ["sscol"], ["sscol"], bias=EPS, scale=1.0 / D)
                act(smallf[:, c0 + 2:c0 + 3], smallf[:, c0 + 1:c0 + 2], AF.Exp, ["sscol"], ["sscol"], scale=-0.5)
                if layer == 0:
                    dma(x1d[rows, :], x1ts[s2], "x1w%d" % s2, reads=["x1t%d" % s2], writes=[("x1d", b)])
                    hb = hbs[s2]
                    stt("dve", hb, x1ts[s2], smallf[:, c0 + 2:c0 + 3], normw[:], ALU.mult, ALU.mult, ["x1t%d" % s2, "sscol", "normw"], ["hb%d" % s2])
                    bi = trR.next()
                    pvn = bkbf(bi).rearrange("p (c n) -> p c n", n=128)
                    for c in range(8):
                        tr(pvn[:, c, :], hb[:, c * 128:(c + 1) * 128], ["hb%d" % s2], ["bank%d" % bi])
                    cp(evac_rr.next(), hT[:, :, t * 128:(t + 1) * 128], pvn, ["bank%d" % bi], [("hTw", t)])
                else:
                    stt("dve", ots[s2], x1ts[s2], smallf[:, c0 + 2:c0 + 3], normw[:], ALU.mult, ALU.mult, ["x1t%d" % s2, "sscol", "normw"], ["ot%d" % s2])
                    final.append(dma(out[rows, :], ots[s2], "ow%d" % s2, reads=["ot%d" % s2]))

        def phase_swa(b):
            AR.reset()
            swab = AR.alloc([16, 4, 128], BF16)
            dma(swab.rearrange("p a b c -> p (a b c)"), c_swa, "swab", writes=["swab"])
            wk = [AR.alloc([8, 128], BF16) for _ in range(2)]
            wqg = [AR.alloc([8, 128], BF16) for _ in range(4)]
            wv = AR.alloc([8, 128], BF16)
            kz = [[AR.alloc([S], BF16) for _ in range(2)] for _ in range(2)]
            for kv_ in range(2):
                memset("pool", kz[kv_][0], 0.0, ["kzero"])
                memset("pool", kz[kv_][1], 0.0, ["kzero"])
            vaug = AR.alloc([NT, 2, 65], BF16)
            qT = AR.alloc([S], BF16); sg = AR.alloc([NT, 128], BF16); gtmp = AR.alloc([4, 128], F32)
            PTs = [AR.alloc([2, 2, 128], BF16) for _ in range(2)]
            PTR = Rot([(PTs[i], "PT%d" % i) for i in range(2)])
            den = smallf[:, 400:404]
            of = AR.alloc([2, 64], F32); obf = [AR.alloc([128], BF16) for _ in range(2)]
            memset("pool", vaug[:, :, :, 64:65], 1.0, ["vaug"])
            bvg = AR.alloc([128 + D], F32)
            dma(bvg[:, 0:128], b_in_c[1152:1280].partition_broadcast(128), "bvg0", writes=["bvg0"])
            dma(bvg[:, 128:128 + D], b_in_c[1280:2304].partition_broadcast(128), "bvg1", writes=["bvg1"])
            pjR = Rot([6, 7])
            for kv in range(2):
                s_t, s_res = stgR.next()
                for hf in range(2):
                    dma(s_t[:, :, hf * 64:(hf + 1) * 64], w_in_c[:, 1024 + kv * 64:1024 + (kv + 1) * 64].rearrange("(c p) n -> p c n", p=128),
                        s_res, reads=([s_res] if hf == 1 else []), writes=[s_res])
                cp("pool", wk[kv], s_t[:], [s_res], ["wk%d" % kv])
                for n in range(S // 512):
                    bi = pjR.next()
                    for c in range(8):
                        mm(bk(bi), wk[kv][:, c, :], hT[:, c, n * 512:(n + 1) * 512], c == 0, c == 7, ["wk%d" % kv, "hT"], ["bank%d" % bi])
                    for i2 in range(2):
                        hs2 = slice(i2 * 64, (i2 + 1) * 64)
                        act(kz[kv][i2][hs2, n * 512:(n + 1) * 512], bk(bi)[hs2, :], AF.Identity, ["bank%d" % bi, "bcol", "kzero"], [("kT2_%d" % kv, n)],
                            bias=bcol[hs2, 8 + kv:9 + kv])
            load_w(wv, "wv", w_in_c[:, 1152:1280])
            for t0 in range(0, NT, 4):
                bi = pjR.next()
                pv = proj_tm4(t0, wv, "wv", bi)
                tt("dve", vaug[:, t0:t0 + 4, :, 0:64], pv.rearrange("p j (k e) -> p j k e", e=64),
                   bc(bvg[:, 0:128].rearrange("p (k e) -> p k e", e=64).unsqueeze(1), [128, 4, 2, 64]), ALU.add, ["bank%d" % bi, "bvg0"], ["vaug"])
            LR = Rot([0, 1, 2])
            OR = Rot([3, 4])
            trR = Rot([5])
            for j in range(8):
                kv = j // 4
                ws = j % 2
                load_w(wqg[ws * 2], "wqg%d" % (ws * 2), w_in_c[:, j * 128:(j + 1) * 128])
                load_w(wqg[ws * 2 + 1], "wqg%d" % (ws * 2 + 1), w_in_c[:, 1280 + j * 128:1280 + (j + 1) * 128])
                proj_fm(qT, "qT", wqg[ws * 2], "wqg%d" % (ws * 2), pjR, bias_col=bcol[:, j:j + 1])
                for t0 in range(0, NT, 4):
                    bi = pjR.next()
                    pv = proj_tm4(t0, wqg[ws * 2 + 1], "wqg%d" % (ws * 2 + 1), bi)
                    tt("dve", gtmp, pv, bc(bvg[:, 128 + j * 128:128 + (j + 1) * 128].unsqueeze(1), [128, 4, 128]), ALU.add, ["bank%d" % bi, "bvg1"], ["gtmp"])
                    act(sg[:, t0:t0 + 4, :], gtmp, AF.Silu, ["gtmp"], ["sg"])
                for n in range(NT):
                    li = LR.next(); Lres = "bank%d" % li
                    Lv = bk(li).rearrange("p (i k q) -> p i k q", k=2, q=128)
                    kts = (1,) if n == 0 else (0, 1)
                    for i in range(2):
                        hq = 2 * j + i
                        rows = slice(i * 64, (i + 1) * 64)
                        for kt in kts:
                            tk = n - 1 + kt
                            mm(Lv[:, i, kt, :], kz[kv][i][:, tk * 128:(tk + 1) * 128], qT[:, n * 128:(n + 1) * 128], True, False, [("kT2_%d" % kv, tk // 4), ("qT", n // 4)], [Lres])
                            mm(Lv[:, i, kt, :], ident, swab[:, hq, kt * 2 + 0, :], False, False, ["cbf", "swab"], [Lres])
                            mm(Lv[:, i, kt, :], ident, swab[:, hq, kt * 2 + 1, :], False, True, ["cbf", "swab"], [Lres])
                    pt, ptres = PTR.next()
                    if n == 0:
                        act(pt[:, :, 1, :], Lv[:, :, 1, :], AF.Exp, [Lres], [ptres], scale=0.125)
                    else:
                        act(pt, Lv, AF.Exp, [Lres], [ptres], scale=0.125)
                    oi = OR.next(); Ores = "bank%d" % oi
                    Ov = bk(oi)[:, 0:130].rearrange("p (i e) -> p i e", e=65)
                    for i in range(2):
                        for kt in kts:
                            tk = n - 1 + kt
                            mm(Ov[:, i, :], pt[:, i, kt, :], vaug[:, tk, kv, :], kt == kts[0], kt == kts[-1], [ptres, "vaug"], [Ores])
                    tt("dve", den[:, 0:2], Ov[:, :, 64:65].rearrange("p i e -> p (i e)"), esink_bc[:, 2 * j:2 * j + 2], ALU.add, [Ores, "sp16"], ["den"])
                    P.op("dve", lambda e: e.reciprocal(out=den[:, 2:4], in_=den[:, 0:2]), ["den"], ["den"])
                    tt("dve", of, Ov[:, :, 0:64], bc(den[:, 2:4].unsqueeze(2), [128, 2, 64]), ALU.mult, [Ores, "den"], ["of"])
                    ob2 = obf[n % 2]; obres = "obf%d" % (n % 2)
                    tt("pool", ob2, of.rearrange("p a b -> p (a b)"), sg[:, n, :], ALU.mult, ["of", "sg"], [obres])
                    dma(y2d[b * S + n * 128: b * S + (n + 1) * 128, j * 128:(j + 1) * 128], ob2, obres, reads=[obres], writes=[("yd", b)])

        import os
        KSTOP = os.environ.get("KSTOP", "all")
        order = ["norm0", "attn", "ssd", "out0", "norm1", "swa", "out1"]
        nphase = len(order) if KSTOP == "all" else (0 if KSTOP == "setup" else order.index(KSTOP) + 1)
        for b in range(NSEQ):
            for ph in order[:nphase]:
                if ph == "norm0":
                    phase_norm(b, x, norm_a, "a")
                elif ph == "attn":
                    phase_attn(b)
                elif ph == "ssd":
                    phase_ssd(b)
                elif ph == "out0":
                    phase_out(b, 0)
                elif ph == "norm1":
                    continue
                elif ph == "swa":
                    phase_swa(b)
                elif ph == "out1":
                    phase_out(b, 1)
                P.barrier()
        counts = P.emit(final)
        print("ops:", counts, "sbuf left", nc.sbuf_bytes_remaining)
    return nc


PARAMS = ["norm_a", "w_in_a", "conv_w_a", "conv_b_a", "dt_bias_a", "a_log_a", "d_skip_a", "ssd_norm_a",
          "lambda_q1_a", "lambda_k1_a", "lambda_q2_a", "lambda_k2_a", "subln_a", "w_out_a",
          "norm_c", "w_in_c", "b_in_c", "sinks_c", "w_out_c", "final_norm"]

_CACHE = {}


def run(inputs, n_cores=8):
    x = np.ascontiguousarray(np.asarray(inputs["x"], dtype=np.float32))
    B, S, _ = x.shape
    assert B % n_cores == 0
    NSEQ = B // n_cores
    key = (S, NSEQ)
    if key not in _CACHE:
        _CACHE[key] = build(S, NSEQ)
    nc = _CACHE[key]
    c_bf, c_f32, c_swa = make_consts()
    base = {"c_bf": c_bf, "c_f32": c_f32, "c_swa": c_swa}
    for k in PARAMS:
        a = np.asarray(inputs[k], dtype=np.float32)
        if k != "final_norm":
            a = a[0]
        base[k] = np.ascontiguousarray(a)
    in_maps = []
    for i in range(n_cores):
        m = dict(base)
        m["x"] = x[i * NSEQ:(i + 1) * NSEQ].reshape(NSEQ * S, D)
        in_maps.append(m)
    res = run_bass_kernel_spmd(nc, in_maps, core_ids=list(range(n_cores)))
    import os
    if os.environ.get("KDBG"):
        global DBG
        DBG = [{k: np.asarray(v) for k, v in r.items()} for r in res.results]
    outs = [np.asarray(r["out"]).reshape(NSEQ, S, D) for r in res.results]
    return np.concatenate(outs, axis=0).astype(np.float32)


def kernel(**inputs):
    return run(inputs, 8)
```

```python
import contextlib
import math
import numpy as np
import ml_dtypes
import concourse.bass as bass
import concourse.mybir as mybir
from concourse.bass_utils import run_bass_kernel_spmd

F32 = mybir.dt.float32
BF16 = mybir.dt.bfloat16
AF = mybir.ActivationFunctionType
ALU = mybir.AluOpType
AX = mybir.AxisListType

D = 1024
EVEN_IN = 6672
ODD_IN = 2304
EPS = 1e-5
NEGBIG = -30000.0
LAMBDA_INIT = 0.8 - 0.6 * math.exp(-0.3 * 0)
NM = 19


ENGS = ("pe", "act", "dve", "pool", "sp")


class Prog:
    def __init__(self, nc, same_engine_sync=True):
        self.nc = nc
        self.ops = {e: [] for e in ENGS}
        self.last_write = {}
        self.reads_since = {}
        self.known = {e: {} for e in ENGS}
        self.dma_count = {}
        self.same_engine_sync = same_engine_sync
        self.latest = {}

    def _add(self, eng, fn, reads, writes, dma_key=None, extra_deps=()):
        writes = tuple(writes) + tuple(r for r in reads if isinstance(r, str) and (r.startswith("bank") or (r[0] == "O" and r[1:].isdigit())))
        lst = self.ops[eng]
        idx = len(lst)
        deps = {}

        def need(tok):
            if tok is None:
                return
            k, i, clk = tok
            if deps.get(k, (-1, None))[0] < i:
                deps[k] = (i, clk)

        for r in reads:
            need(self.last_write.get(r))
        for w in writes:
            need(self.last_write.get(w))
            for tok in self.reads_since.get(w, {}).values():
                need(tok)
        for tok in extra_deps:
            need(tok)
        known = self.known[eng]
        waits = []
        for k, (i, clk) in deps.items():
            if k == eng and not (self.same_engine_sync and eng != "pe" and eng != "sp"):
                continue
            if known.get(k, -1) >= i:
                continue
            waits.append((k, i))
        for k, (i, clk) in deps.items():
            if k == eng and (k, i) not in waits:
                continue
            if known.get(k, -1) < i:
                known[k] = i
            for kk, vv in clk.items():
                if known.get(kk, -1) < vv:
                    known[kk] = vv
        op = dict(fn=fn, waits=waits, flag=False, dma_key=dma_key)
        lst.append(op)
        if dma_key is not None:
            n = self.dma_count.get(dma_key, 0) + 1
            self.dma_count[dma_key] = n
            tok = (("D", dma_key), n, dict(known))
        else:
            clk = dict(known)
            tok = (eng, idx, clk)
        self.latest[tok[0]] = tok
        for r in reads:
            self.reads_since.setdefault(r, {})[tok[0]] = tok
        for w in writes:
            self.last_write[w] = tok
            self.reads_since[w] = {}
        return tok


    def barrier(self):
        toks = list(self.latest.values())
        for e in ENGS:
            self._add(e, (lambda eng: eng.nop()), (), (), extra_deps=toks)

    def op(self, eng, fn, reads=(), writes=(), extra_deps=()):
        return self._add(eng, fn, tuple(reads), tuple(writes), extra_deps=extra_deps)

    def dma(self, eng, fn, key, reads=(), writes=(), extra_deps=()):
        return self._add(eng, fn, tuple(reads), tuple(writes), dma_key=key, extra_deps=extra_deps)

    def emit(self, final_tokens):
        nc = self.nc
        for e in ENGS:
            for op in self.ops[e]:
                for k, i in op["waits"]:
                    if not isinstance(k, tuple):
                        self.ops[k][i]["flag"] = True
        final_waits = []
        for k, i, _ in final_tokens:
            if isinstance(k, tuple):
                final_waits.append((k, i))
            else:
                self.ops[k][i]["flag"] = True
                final_waits.append((k, i))
        rank = {}
        for e in ENGS:
            c = 0
            r = {}
            for i, op in enumerate(self.ops[e]):
                if op["flag"]:
                    c += 1
                    r[i] = c
            rank[e] = r
        import contextlib
        with contextlib.ExitStack() as st:
            esem = {e: st.enter_context(nc.semaphore("s_" + e)) for e in ENGS}
            dsem = {}
            for key in self.dma_count:
                dsem[key] = st.enter_context(nc.semaphore("d_%s" % (str(key).replace(" ", ""))))
            block = st.enter_context(nc.Block())

            def run(e, engine):
                for i, op in enumerate(self.ops[e]):
                    for k, v in op["waits"]:
                        if isinstance(k, tuple):
                            engine.wait_ge(dsem[k[1]], 16 * v)
                        else:
                            engine.wait_ge(esem[k], rank[k][v])
                    ins = op["fn"](engine)
                    if op["dma_key"] is not None:
                        ins.then_inc(dsem[op["dma_key"]], 16)
                    elif op["flag"]:
                        ins.then_inc(esem[e], 1)
                if e == "sp":
                    for k, v in final_waits:
                        if isinstance(k, tuple):
                            engine.wait_ge(dsem[k[1]], 16 * v)
                        else:
                            engine.wait_ge(esem[k], rank[k][v])

            @block.tensor
            def _(eng):
                run("pe", eng)

            @block.scalar
            def _(eng):
                run("act", eng)

            @block.vector
            def _(eng):
                run("dve", eng)

            @block.gpsimd
            def _(eng):
                run("pool", eng)

            @block.sync
            def _(eng):
                run("sp", eng)
        return {e: len(self.ops[e]) for e in ENGS}


def make_consts():
    s = np.arange(128)
    ident = np.eye(128, dtype=np.float32)
    negT = np.where(s[None, :] < s[:, None], NEGBIG, 0.0).astype(np.float32)
    c_bf = np.concatenate([ident] + [negT] * 4, axis=1).astype(ml_dtypes.bfloat16)
    U = (s[:, None] > s[None, :]).astype(np.float32)
    Tri = (s[:, None] <= s[None, :]).astype(np.float32)
    Sel = np.zeros((128, 128), np.float32); Sel[127, :] = 1.0
    slopes = np.exp2(-8.0 * np.arange(1, 9, dtype=np.float32) / 8).astype(np.float32)
    ab = np.zeros((128, 8, NM), np.float32)
    for h in range(8):
        for mi in range(NM):
            m = mi - 17
            ab[:, h, mi] = slopes[h] * (128.0 * m + s)
    c_f32 = np.concatenate([U, Tri, Sel, ident, ab.reshape(128, 8 * NM)], axis=1).astype(np.float32)
    sl16 = np.exp2(-8.0 * np.arange(1, 17, dtype=np.float32) / 16).astype(np.float32)
    sw = np.zeros((128, 16, 2, 2, 128), np.float32)
    q = np.arange(128)
    for kt in range(2):
        srel = s - 128 if kt == 0 else s
        dist = q[None, :] - srel[:, None]
        valid = (dist >= 0) & (dist < 128)
        for h in range(16):
            val = np.where(valid, -sl16[h] * dist.astype(np.float32) * 8.0, NEGBIG * 8.0).astype(np.float32)
            hi = val.astype(ml_dtypes.bfloat16).astype(np.float32)
            lo = (val - hi).astype(ml_dtypes.bfloat16).astype(np.float32)
            sw[:, h, kt, 0, :] = hi
            sw[:, h, kt, 1, :] = lo
    c_swa = sw.reshape(128, -1).astype(ml_dtypes.bfloat16)
    return c_bf, c_f32, c_swa


class Rot:
    def __init__(self, items):
        self.items = list(items)
        self.i = 0

    def next(self):
        it = self.items[self.i % len(self.items)]
        self.i += 1
        return it


def build(S, NSEQ, dbg=()):
    NT = S // 128
    NQ = S // 512
    nc = bass.Bass("TRN2", target_bir_lowering=False)

    def din(name, shape, dt=F32):
        return nc.dram_tensor(name, list(shape), dt, kind="ExternalInput").ap()

    x = din("x", [NSEQ * S, D])
    norm_a = din("norm_a", [D]); w_in_a = din("w_in_a", [D, EVEN_IN])
    conv_w = din("conv_w_a", [4, 1536]); conv_b = din("conv_b_a", [1536])
    dt_bias = din("dt_bias_a", [16]); a_log = din("a_log_a", [16]); d_skip = din("d_skip_a", [16])
    ssd_norm = din("ssd_norm_a", [D])
    lq1 = din("lambda_q1_a", [64]); lk1 = din("lambda_k1_a", [64]); lq2 = din("lambda_q2_a", [64]); lk2 = din("lambda_k2_a", [64])
    subln = din("subln_a", [128]); w_out_a = din("w_out_a", [2048, D])
    norm_c = din("norm_c", [D]); w_in_c = din("w_in_c", [D, ODD_IN]); b_in_c = din("b_in_c", [ODD_IN])
    sinks = din("sinks_c", [16]); w_out_c = din("w_out_c", [D, D]); final_norm = din("final_norm", [D])
    c_bf = din("c_bf", [128, 640], BF16); c_f32 = din("c_f32", [128, 512 + 8 * NM], F32)
    c_swa = din("c_swa", [128, 16 * 2 * 2 * 128], BF16)
    out = nc.dram_tensor("out", [NSEQ * S, D], F32, kind="ExternalOutput").ap()
    import os
    _kw = {"kind": "ExternalOutput"} if os.environ.get("KDBG") else {}
    x1d = nc.dram_tensor("x1_scr", [NSEQ * S, D], F32, **_kw).ap()
    yd = nc.dram_tensor("y_scr", [NSEQ * S, 2048], BF16, **_kw).ap()
    y2d = nc.dram_tensor("y2_scr", [NSEQ * S, D], BF16, **_kw).ap()
    dbg_out = {}
    for name, shape in dbg:
        dbg_out[name] = nc.dram_tensor("dbg_" + name, list(shape), F32, kind="ExternalOutput").ap()

    st = contextlib.ExitStack()
    with st:
        def sb(name, shape, dt):
            return st.enter_context(nc.sbuf_tensor(name, list(shape), dt))

        P = Prog(nc, same_engine_sync=bool(int(os.environ.get("KSES", "1"))))
        final = []
        banks = [st.enter_context(nc.psum_tensor("bank%d" % i, [128, 512], F32)) for i in range(8)]

        def bk(i):
            return banks[i][:]

        def bkbf(i):
            return banks[i][:].bitcast(BF16)

        cbf = sb("cbf", [128, 640], BF16)
        cf = sb("cf", [128, 512 + 8 * NM], F32)
        ident = cbf[:, 0:128]
        negT = cbf[:, 128:256]
        NEG4 = cbf[:, 128:640]
        U_ = cf[:, 0:128]; Tri = cf[:, 128:256]; Sel = cf[:, 256:384]; identf = cf[:, 384:512]
        abias = cf[:, 512:512 + 8 * NM].rearrange("p (h m) -> p h m", m=NM)
        hT = sb("hT", [128, 8, S], BF16)
        normw = sb("normw", [128, D], F32)
        stg = [sb("stg%d" % i, [128, 8, 128], F32) for i in range(3)]
        stgR = Rot([(stg[i], "stg%d" % i) for i in range(3)])
        junk = sb("junk", [128, D], BF16)
        smallf = sb("smallf", [128, 512], F32)
        ARENA = 70000
        arena_t = sb("arena", [128, ARENA], BF16)

        class Arena:
            def __init__(self):
                self.off = 0

            def reset(self):
                self.off = 0

            def alloc(self, shape, dt):
                n = int(np.prod(shape))
                el = n * (2 if dt == F32 else 1)
                el = (el + 1) // 2 * 2
                assert self.off + el <= ARENA, ("arena overflow", self.off, el)
                v = arena_t[:, self.off:self.off + el]
                self.off += el
                if dt == F32:
                    v = v.bitcast(F32)
                if el != n * (2 if dt == F32 else 1):
                    v = v[:, 0:n]
                if len(shape) == 2:
                    v = v.rearrange("p (a b) -> p a b", b=shape[1])
                elif len(shape) == 3:
                    v = v.rearrange("p (a b c) -> p a b c", b=shape[1], c=shape[2])
                return v

        AR = Arena()

        def dma(out_ap, in_ap, key, reads=(), writes=(), eng="sp"):
            return P.dma(eng, lambda e: e.dma_start(out=out_ap, in_=in_ap), key, reads, writes)

        def mm(out_ap, lhsT, rhs, start, stop, reads, writes):
            return P.op("pe", lambda e: e.matmul(out_ap, lhsT=lhsT, rhs=rhs, start=start, stop=stop), reads, writes)

        def tr(out_ap, in_ap, reads, writes):
            return P.op("pe", lambda e: e.transpose(out=out_ap, in_=in_ap, identity=ident), tuple(reads) + ("cbf",), writes)

        def act(out_ap, in_ap, func, reads, writes, bias=None, scale=None, accum=None):
            kw = {}
            if bias is not None:
                kw["bias"] = bias
            if scale is not None:
                kw["scale"] = scale
            if accum is not None:
                kw["accum_out"] = accum
            return P.op("act", lambda e: e.activation(out=out_ap, in_=in_ap, func=func, **kw), reads, writes)

        def tt(eng, out_ap, in0, in1, op, reads, writes):
            return P.op(eng, lambda e: e.tensor_tensor(out=out_ap, in0=in0, in1=in1, op=op), reads, writes)

        def ts(eng, out_ap, in0, s1, s2, op0, op1, reads, writes):
            if op1 is None:
                return P.op(eng, lambda e: e.tensor_scalar(out=out_ap, in0=in0, scalar1=s1, scalar2=None, op0=op0), reads, writes)
            return P.op(eng, lambda e: e.tensor_scalar(out=out_ap, in0=in0, scalar1=s1, scalar2=s2, op0=op0, op1=op1), reads, writes)

        def stt(eng, out_ap, in0, scalar, in1, op0, op1, reads, writes):
            return P.op(eng, lambda e: e.scalar_tensor_tensor(out=out_ap, in0=in0, scalar=scalar, in1=in1, op0=op0, op1=op1), reads, writes)

        def cp(eng, out_ap, in_ap, reads, writes):
            if eng == "act":
                return act(out_ap, in_ap, AF.Copy, reads, writes)
            return P.op(eng, lambda e: e.tensor_copy(out=out_ap, in_=in_ap), reads, writes)

        def recip(out_ap, in_ap, reads, writes):
            return P.op("dve", lambda e: e.reciprocal(out=out_ap, in_=in_ap), reads, writes)

        def memset(eng, ap, val, writes):
            return P.op(eng, lambda e: e.memset(ap, val), (), writes)

        def bc(ap, shape):
            return ap.to_broadcast(list(shape))

        def load_w(dst_ap, dst_res, src_ap):
            s_t, s_res = stgR.next()
            dma(s_t[:], src_ap.rearrange("(c p) n -> p c n", p=128), s_res, writes=[s_res])
            cp("pool", dst_ap, s_t[:], [s_res], [dst_res])

        evac_rr = Rot(["act", "dve"])

        dma(cbf[:], c_bf, "setup", writes=["cbf"])
        dma(cf[:], c_f32, "setup", writes=["cf"])
        lamt = AR.alloc([4, 64], F32)
        for i, v in enumerate((lq1, lk1, lq2, lk2)):
            dma(lamt[:, i, :], v.partition_broadcast(128), "setup", writes=["lamt"])
        sublnw = sb("sublnw", [128, 128], F32)
        dma(sublnw[:], subln.partition_broadcast(128), "setup", writes=["sublnw"])
        sp16 = sb("sp16", [128, 4, 16], F32)
        dma(sp16[:, 0, :], dt_bias.partition_broadcast(128), "setup", writes=["sp16"])
        dma(sp16[:, 1, :], a_log.partition_broadcast(128), "setup", writes=["sp16"])
        dma(sp16[:, 2, :], d_skip.partition_broadcast(128), "setup", writes=["sp16"])
        dma(sp16[:, 3, :], sinks.partition_broadcast(128), "setup", writes=["sp16"])
        convw = sb("convw", [128, 12, 4], F32)
        convb = sb("convb", [128, 12], F32)
        for k4 in range(4):
            P.dma("sp", lambda e, k4=k4: e.dma_start(out=convw[:, :, k4], in_=conv_w[k4].rearrange("(c p) -> p c", p=128), allow_slow_non_contiguous=True), "setup", (), ["convw"])
        P.dma("sp", lambda e: e.dma_start(out=convb[:], in_=conv_b.rearrange("(c p) -> p c", p=128), allow_slow_non_contiguous=True), "setup", (), ["convb"])
        bcol = sb("bcol", [128, 12], F32)
        for j in range(8):
            P.dma("sp", lambda e, j=j: e.dma_start(out=bcol[:, j:j + 1], in_=b_in_c[j * 128:(j + 1) * 128].rearrange("(p o) -> p o", o=1), allow_slow_non_contiguous=True), "setup", (), ["bcol"])
        for kv in range(2):
            for hf in range(2):
                P.dma("sp", lambda e, kv=kv, hf=hf: e.dma_start(out=bcol[hf * 64:(hf + 1) * 64, 8 + kv:9 + kv],
                      in_=b_in_c[1024 + kv * 64:1024 + (kv + 1) * 64].rearrange("(p o) -> p o", o=1), allow_slow_non_contiguous=True),
                      "setup", (), ["bcol"])
        P.barrier()
        tt("dve", lamt[:, 0, :], lamt[:, 0, :], lamt[:, 1, :], ALU.mult, ["lamt"], ["lamt"])
        tt("dve", lamt[:, 2, :], lamt[:, 2, :], lamt[:, 3, :], ALU.mult, ["lamt"], ["lamt"])
        P.op("dve", lambda e: e.tensor_reduce(out=smallf[:, 1:2], in_=lamt[:, 0, :], axis=AX.X, op=ALU.add), ["lamt"], ["sm1"])
        P.op("dve", lambda e: e.tensor_reduce(out=smallf[:, 2:3], in_=lamt[:, 2, :], axis=AX.X, op=ALU.add), ["lamt"], ["sm2"])
        act(smallf[:, 1:3], smallf[:, 1:3], AF.Exp, ["sm1", "sm2"], ["sm1", "sm2"])
        stt("dve", smallf[:, 0:1], smallf[:, 2:3], -LAMBDA_INIT, smallf[:, 1:2], ALU.add, ALU.subtract, ["sm1", "sm2"], ["neglam"])
        neg_lam = smallf[:, 0:1]
        ts("dve", sublnw[:], sublnw[:], 1.0 - LAMBDA_INIT, None, ALU.mult, None, ["sublnw"], ["sublnw"])
        act(sp16[:, 1, :], sp16[:, 1, :], AF.Exp, ["sp16"], ["sp16"])
        ts("dve", sp16[:, 1, :], sp16[:, 1, :], -1.0, None, ALU.mult, None, ["sp16"], ["sp16"])
        act(sp16[:, 3, :], sp16[:, 3, :], AF.Exp, ["sp16"], ["sp16"])
        dtb_bc = sp16[:, 0, :]; negA_bc = sp16[:, 1, :]; dsk_bc = sp16[:, 2, :]; esink_bc = sp16[:, 3, :]
        diagf = AR.alloc([16, 128], F32)
        diagh = sb("diagh", [128, 16, 128], BF16)
        diagl = sb("diagl", [128, 16, 128], BF16)
        tt("dve", diagf, bc(identf.unsqueeze(1), [128, 16, 128]), bc(dsk_bc.unsqueeze(2), [128, 16, 128]), ALU.mult, ["cf", "sp16"], ["diagf"])
        cp("dve", diagh[:], diagf, ["diagf"], ["diagh"])
        tt("dve", diagf, diagf, diagh[:], ALU.subtract, ["diagf", "diagh"], ["diagf"])
        cp("dve", diagl[:], diagf, ["diagf"], ["diagl"])
        P.barrier()
        def phase_norm(b, src, nw_dram, tag):
            AR.reset()
            xts = [AR.alloc([D], F32) for _ in range(2)]
            xtR = Rot([(xts[i], "xt%d" % i) for i in range(2)])
            hbs = [AR.alloc([D], BF16) for _ in range(2)]
            hbR = Rot([(hbs[i], "hb%d" % i) for i in range(2)])
            dma(normw[:], nw_dram.partition_broadcast(128), "normw", writes=["normw"])
            sscol = smallf[:, 16:16 + 3 * NT]
            memset("pool", sscol, 0.0, ["sscol"])
            trR = Rot([0, 1])
            for t in range(NT):
                xt, xres = xtR.next()
                hb, hres = hbR.next()
                dma(xt, src[b * S + t * 128: b * S + (t + 1) * 128, :], xres, writes=[xres])
                c0 = 16 + 3 * t
                act(junk[:], xt, AF.Square, [xres, "sscol"], ["junk", "sscol"], accum=smallf[:, c0:c0 + 1])
                act(smallf[:, c0 + 1:c0 + 2], smallf[:, c0:c0 + 1], AF.Ln, ["sscol"], ["sscol"], bias=EPS, scale=1.0 / D)
                act(smallf[:, c0 + 2:c0 + 3], smallf[:, c0 + 1:c0 + 2], AF.Exp, ["sscol"], ["sscol"], scale=-0.5)
                stt("dve", hb, xt, smallf[:, c0 + 2:c0 + 3], normw[:], ALU.mult, ALU.mult, [xres, "sscol", "normw"], [hres])
                bi = trR.next()
                pv = bkbf(bi).rearrange("p (c n) -> p c n", n=128)
                for c in range(8):
                    tr(pv[:, c, :], hb[:, c * 128:(c + 1) * 128], [hres], ["bank%d" % bi])
                cp(evac_rr.next(), hT[:, :, t * 128:(t + 1) * 128], pv, ["bank%d" % bi], [("hTw", t)])

        def proj_fm(dst, dst_res, wt, wres, pjR, bias_col=None):
            for n in range(S // 512):
                bi = pjR.next()
                for c in range(8):
                    mm(bk(bi), wt[:, c, :], hT[:, c, n * 512:(n + 1) * 512], c == 0, c == 7, [wres, "hT"], ["bank%d" % bi])
                if bias_col is None:
                    cp(evac_rr.next(), dst[:, n * 512:(n + 1) * 512], bk(bi), ["bank%d" % bi], [(dst_res, n)])
                else:
                    act(dst[:, n * 512:(n + 1) * 512], bk(bi), AF.Identity, ["bank%d" % bi, "bcol"], [(dst_res, n)], bias=bias_col)

        def proj_tm4(t0, wt, wres, bi):
            pv = bk(bi).rearrange("p (j n) -> p j n", n=128)
            for j in range(4):
                for c in range(8):
                    mm(pv[:, j, :], hT[:, c, (t0 + j) * 128:(t0 + j + 1) * 128], wt[:, c, :], c == 0, c == 7, [wres, "hT"], ["bank%d" % bi])
            return pv

        def phase_attn(b):
            AR.reset()
            wq = [AR.alloc([8, 128], BF16) for _ in range(8)]
            qT = AR.alloc([S], BF16); kTz = [AR.alloc([S], BF16) for _ in range(2)]
            vaug = AR.alloc([NT, 129], BF16); sg = AR.alloc([NT, 128], BF16)
            memset("pool", kTz[0], 0.0, ["kTzero"])
            memset("pool", kTz[1], 0.0, ["kTzero"])
            PTs = [AR.alloc([512], BF16) for _ in range(3)]
            PTR = Rot([(PTs[i], "PT%d" % i) for i in range(3)])
            a0 = AR.alloc([4, 128], F32); t1 = AR.alloc([4, 128], F32); attn = AR.alloc([4, 128], F32)
            sq = AR.alloc([4, 128], F32)
            ya = AR.alloc([4, 128], BF16)
            rec = smallf[:, 400:408]; ssq = smallf[:, 408:420]
            memset("pool", vaug[:, :, 128:129], 1.0, ["vaug"])
            pjR = Rot([6, 7])
            LR = Rot([4, 5])
            OR = Rot([(0, 1), (2, 3)])
            col0 = [0, 1024, 2048, 3072]
            def load_head_w(h):
                ws_ = h % 2
                for i in range(4):
                    load_w(wq[ws_ * 4 + i], "wq%d_%d" % (ws_, i), w_in_a[:, col0[i] + h * 128: col0[i] + (h + 1) * 128])

            load_head_w(0)
            for h in range(8):
                ws = h % 2
                proj_fm(qT, "qT", wq[ws * 4 + 0], "wq%d_0" % ws, pjR)
                for n in range(S // 512):
                    bi = pjR.next()
                    for c in range(8):
                        mm(bk(bi), wq[ws * 4 + 1][:, c, :], hT[:, c, n * 512:(n + 1) * 512], c == 0, c == 7, ["wq%d_1" % ws, "hT"], ["bank%d" % bi])
                    cp("dve", kTz[0][0:64, n * 512:(n + 1) * 512], bk(bi)[0:64, :], ["bank%d" % bi, "kTzero"], [("kT", n)])
                    cp("dve", kTz[1][64:128, n * 512:(n + 1) * 512], bk(bi)[64:128, :], ["bank%d" % bi, "kTzero"], [("kT", n)])
                for t0 in range(0, NT, 4):
                    bi = pjR.next()
                    pv = proj_tm4(t0, wq[ws * 4 + 2], "wq%d_2" % ws, bi)
                    cp("dve", vaug[:, t0:t0 + 4, 0:128], pv, ["bank%d" % bi], ["vaug"])
                    bi = pjR.next()
                    pv = proj_tm4(t0, wq[ws * 4 + 3], "wq%d_3" % ws, bi)
                    act(sg[:, t0:t0 + 4, :], pv, AF.Silu, ["bank%d" % bi], ["sg"])
                if h < 7:
                    load_head_w(h + 1)
                WA = 256 if h == 0 else 512
                blocks = [(qt, m, kt) for qt in range(NQ) for m in range(2) for kt in range(4 * qt + 4)]
                binfo = {}
                ostate = {}

                def emit_qk(i):
                    qt, m, kt = blocks[i]
                    rows = slice(0, 128)
                    kT = kTz[m]
                    j = kt - 4 * qt
                    c0 = 128 * j if j > 0 else 0
                    li = LR.next()
                    Lres = "bank%d" % li
                    kq = [("kT", kt // 4), ("qT", qt)]
                    if j < 0:
                        mm(bk(li)[:, 0:512], kT[rows, kt * 128:(kt + 1) * 128], qT[rows, qt * 512:(qt + 1) * 512], True, True, kq, [Lres])
                    else:
                        mm(bk(li)[:, c0:c0 + 128], kT[rows, kt * 128:(kt + 1) * 128], qT[rows, qt * 512 + c0:qt * 512 + c0 + 128], True, False, kq, [Lres])
                        mm(bk(li)[:, c0:c0 + 128], ident, negT, False, True, ["cbf"], [Lres])
                        if c0 + 128 < 512:
                            mm(bk(li)[:, c0 + 128:512], kT[rows, kt * 128:(kt + 1) * 128], qT[rows, qt * 512 + c0 + 128:(qt + 1) * 512], True, True, kq, [Lres])
                    pt, ptres = PTR.next()
                    for a in range(512 // WA):
                        lo = max(c0, a * WA); hi = (a + 1) * WA
                        if lo >= hi:
                            continue
                        mval = kt - 4 * qt - (WA // 128) * a - (WA // 256)
                        assert -17 <= mval <= 1
                        act(pt[:, lo:hi], bk(li)[:, lo:hi], AF.Exp, [Lres, "cf"], [ptres], bias=abias[:, h, mval + 17:mval + 18], scale=0.125)
                    binfo[i] = (pt, ptres)

                def emit_pv(i):
                    qt, m, kt = blocks[i]
                    j = kt - 4 * qt
                    if kt == 0:
                        ob = OR.next()
                        ostate[(qt, m)] = (ob, "O%d" % ob[0])
                    ob, Ores = ostate[(qt, m)]
                    pt, ptres = binfo.pop(i)
                    for jj in range(max(j, 0), 4):
                        osub = banks[ob[jj // 2]][:, (jj % 2) * 129:(jj % 2) * 129 + 129]
                        mm(osub, pt[:, jj * 128:(jj + 1) * 128], vaug[:, kt, :], (kt == 0 and jj % 2 == 0), (kt == 4 * qt + jj and jj % 2 == 1), [ptres, "vaug"], [Ores])
                    if kt == 4 * qt + 3:
                        if m == 0:
                            evac_a(qt)
                        else:
                            evac(qt)

                def O4(ob, lo, hi):
                    return [banks[ob[k]][:, 0:258].rearrange("p (i e) -> p i e", e=129)[:, :, lo:hi] for k in range(2)]

                def evac_a(qt):
                    oa, ra = ostate[(qt, 0)]
                    for k in range(2):
                        recip(rec[:, 2 * k:2 * k + 2], O4(oa, 128, 129)[k].rearrange("p i e -> p (i e)"), [ra], ["recA"])
                    for k in range(2):
                        tt("dve", a0[:, 2 * k:2 * k + 2, :], O4(oa, 0, 128)[k], bc(rec[:, 2 * k:2 * k + 2].unsqueeze(2), [128, 2, 128]), ALU.mult, [ra, "recA"], ["a0"])

                def evac(qt):
                    ob_, rb = ostate[(qt, 1)]
                    for k in range(2):
                        recip(rec[:, 4 + 2 * k:4 + 2 * k + 2], O4(ob_, 128, 129)[k].rearrange("p i e -> p (i e)"), [rb], ["rec"])
                    ts("dve", rec[:, 4:8], rec[:, 4:8], neg_lam, None, ALU.mult, None, ["rec", "neglam"], ["rec"])
                    for k in range(2):
                        tt("dve", t1[:, 2 * k:2 * k + 2, :], O4(ob_, 0, 128)[k], bc(rec[:, 4 + 2 * k:4 + 2 * k + 2].unsqueeze(2), [128, 2, 128]), ALU.mult, [rb, "rec"], ["t1"])
                    tt("pool", attn, a0, t1, ALU.add, ["a0", "t1"], ["attn"])
                    tt("pool", sq, attn, attn, ALU.mult, ["attn"], ["sq"])
                    P.op("dve", lambda e: e.tensor_reduce(out=ssq[:, 0:4], in_=sq, axis=AX.X, op=ALU.add), ["sq"], ["ssq"])
                    act(ssq[:, 4:8], ssq[:, 0:4], AF.Ln, ["ssq"], ["ssq"], bias=EPS, scale=1.0 / 128)
                    act(ssq[:, 8:12], ssq[:, 4:8], AF.Exp, ["ssq"], ["ssq"], scale=-0.5)
                    tt("pool", attn, attn, bc(ssq[:, 8:12].unsqueeze(2), [128, 4, 128]), ALU.mult, ["attn", "ssq"], ["attn"])
                    tt("pool", attn, attn, bc(sublnw[:].unsqueeze(1), [128, 4, 128]), ALU.mult, ["attn", "sublnw"], ["attn"])
                    tt("pool", ya, attn, sg[:, 4 * qt:4 * qt + 4, :], ALU.mult, ["attn", "sg"], ["ya"])
                    dma(yd[b * S + qt * 512: b * S + (qt + 1) * 512, h * 128:(h + 1) * 128].rearrange("(j p) e -> p j e", p=128), ya,
                        "ya", reads=["ya"], writes=[("yd", b)])

                for i in range(len(blocks) + 1):
                    if i < len(blocks):
                        emit_qk(i)
                    if i >= 1:
                        emit_pv(i - 1)

        def phase_ssd(b):
            AR.reset()
            wx = [AR.alloc([8, 128], BF16) for _ in range(2)]
            wz = AR.alloc([8, 8, 128], BF16)
            wdt_f = AR.alloc([8, 16], F32); wdt = AR.alloc([8, 16], BF16)
            raw = AR.alloc([3 + S], F32); acc = AR.alloc([S], F32); xtmp = AR.alloc([S], BF16)
            xs_tok = AR.alloc([NT, 1024], BF16)
            bmT = AR.alloc([2, S], BF16); cmT = AR.alloc([2, S], BF16); bm_tok = AR.alloc([NT, 256], BF16)
            xpre = AR.alloc([NT, 16], F32); dtt = AR.alloc([NT, 16], F32); dA = AR.alloc([NT, 16], F32)
            cs = AR.alloc([NT, 16], F32); expcs = AR.alloc([NT, 16], F32); cdec = AR.alloc([NT, 16], F32)
            wds = AR.alloc([NT, 16], F32); tmp16 = AR.alloc([NT, 16], F32)
            dATri = AR.alloc([8, 128], F32); LmT = AR.alloc([8, 128], F32); MT = AR.alloc([8, 128], BF16)
            u1 = AR.alloc([8, 64], F32); u2 = AR.alloc([8, 64], F32); u3 = AR.alloc([512], F32)
            sz = AR.alloc([512], F32); xdtp = AR.alloc([8, 64], BF16); yb = AR.alloc([512], BF16)
            hst = [AR.alloc([8, 64], F32) for _ in range(2)]
            prevT = [AR.alloc([8, 64], BF16) for _ in range(2)]
            ssdnw = AR.alloc([D], F32)
            dma(ssdnw, ssd_norm.partition_broadcast(128), "ssdnw", writes=["ssdnw"])
            sscol = smallf[:, 16:16 + 3 * 2 * NT]
            memset("pool", sscol, 0.0, ["sscol"])
            memset("pool", raw[:, 0:3], 0.0, ["raw"])
            pjR = Rot([6, 7])
            import os
            KSSD = int(os.environ.get("KSSD", "9"))
            if KSSD < -3:
                return
            P.dma("sp", lambda e: e.dma_start(out=wdt_f, in_=w_in_a[:, 6656:6672].rearrange("(c p) n -> p c n", p=128)), "wdt", (), ["wdt_f"])
            cp("pool", wdt, wdt_f, ["wdt_f"], ["wdt"])
            dtp = bk(0)[:, 0:NT * 16].rearrange("p (t n) -> p t n", n=16)
            for t in range(NT):
                for c in range(8):
                    mm(dtp[:, t, :], hT[:, c, t * 128:(t + 1) * 128], wdt[:, c, :], c == 0, c == 7, ["wdt", "hT"], ["bank0"])
            tt("dve", xpre, dtp, bc(dtb_bc.unsqueeze(1), [128, NT, 16]), ALU.add, ["bank0", "sp16"], ["xpre"])
            act(tmp16, xpre, AF.Abs, ["xpre"], ["tmp16"])
            act(tmp16, tmp16, AF.Exp, ["tmp16"], ["tmp16"], scale=-1.0)
            act(tmp16, tmp16, AF.Ln, ["tmp16"], ["tmp16"], bias=1.0)
            stt("dve", dtt, xpre, 0.0, tmp16, ALU.max, ALU.add, ["xpre", "tmp16"], ["dtt"])
            tt("dve", dA, dtt, bc(negA_bc.unsqueeze(1), [128, NT, 16]), ALU.mult, ["dtt", "sp16"], ["dA"])
            if KSSD < -2:
                return
            csp = bk(1)[:, 0:NT * 16].rearrange("p (t n) -> p t n", n=16)
            for t in range(NT):
                mm(csp[:, t, :], Tri, dA[:, t, :], True, True, ["cf", "dA"], ["bank1"])
            cp("dve", cs, csp, ["bank1"], ["cs"])
            act(expcs, cs, AF.Exp, ["cs"], ["expcs"])
            if KSSD < -1:
                return
            clp = bk(2)[:, 0:NT * 16].rearrange("p (t n) -> p t n", n=16)
            KVAR = os.environ.get("KVAR", "")
            if KVAR == "V3":
                mm(bk(2)[:, 0:NT * 16], Tri, cs.rearrange("p t n -> p (t n)"), True, True, ["cf", "cs"], ["bank2"])
            elif KVAR == "V4":
                clp = bk(1)[:, 256:256 + NT * 16].rearrange("p (t n) -> p t n", n=16)
                mm(bk(1)[:, 256:256 + NT * 16], Sel, cs.rearrange("p t n -> p (t n)"), True, True, ["cf", "cs"], ["bank2"])
            else:
                mm(bk(2)[:, 0:NT * 16], Sel, cs.rearrange("p t n -> p (t n)"), True, True, ["cf", "cs"], ["bank2"])
            act(cdec, clp, AF.Exp, ["bank2"], ["cdec"])
            if KVAR in ("V2", "V3", "V4"):
                return
            tt("dve", tmp16, clp, cs, ALU.subtract, ["bank2", "cs"], ["tmp16"])
            act(tmp16, tmp16, AF.Exp, ["tmp16"], ["tmp16"])
            tt("dve", wds, tmp16, dtt, ALU.mult, ["tmp16", "dtt"], ["wds"])
            if KSSD < 1:
                return
            for i in range(8):
                load_w(wz[:, i, :, :], "wz", w_in_a[:, 4096 + i * 128: 4096 + (i + 1) * 128])
            trR = Rot([3, 4])
            load_w(wx[0], "wx0", w_in_a[:, 5120: 5120 + 128])
            for i in range(12):
                wt = wx[i % 2]; wres = "wx%d" % (i % 2)
                if i < 11:
                    load_w(wx[(i + 1) % 2], "wx%d" % ((i + 1) % 2), w_in_a[:, 5120 + (i + 1) * 128: 5120 + (i + 2) * 128])
                for n in range(S // 512):
                    bi = pjR.next()
                    for c in range(8):
                        mm(bk(bi), wt[:, c, :], hT[:, c, n * 512:(n + 1) * 512], c == 0, c == 7, [wres, "hT"], ["bank%d" % bi])
                    cp(evac_rr.next(), raw[:, 3 + n * 512:3 + (n + 1) * 512], bk(bi), ["bank%d" % bi], ["raw"])
                ts("dve", acc, raw[:, 3:3 + S], convw[:, i, 3:4], convb[:, i:i + 1], ALU.mult, ALU.add, ["raw", "convw", "convb"], ["acc0"])
                for k in (2, 1, 0):
                    stt("dve", acc, raw[:, k:k + S], convw[:, i, k:k + 1], acc, ALU.mult, ALU.add, ["raw", "convw", "acc0"], ["acc0"])
                if i < 8:
                    dst, dres = xtmp, "xtmp"
                elif i < 10:
                    dst, dres = bmT[:, i - 8, :], "bmT"
                else:
                    dst, dres = cmT[:, i - 10, :], "cmT"
                act(dst, acc, AF.Silu, ["acc0"], [dres])
                if i < 10:
                    for t0 in range(0, NT, 4):
                        bi = trR.next()
                        pv = bkbf(bi)[:, 0:512].rearrange("p (j n) -> p j n", n=128)
                        for j in range(4):
                            tr(pv[:, j, :], dst[:, (t0 + j) * 128:(t0 + j + 1) * 128], [dres], ["bank%d" % bi])
                        if i < 8:
                            cp("dve", xs_tok[:, t0:t0 + 4, i * 128:(i + 1) * 128], pv, ["bank%d" % bi], ["xs_tok"])
                        else:
                            cp("dve", bm_tok[:, t0:t0 + 4, (i - 8) * 128:(i - 7) * 128], pv, ["bank%d" % bi], ["bm_tok"])
            if KSSD < 2:
                return
            zR = Rot([0, 7])
            for c in range(NT if KSSD >= 4 else 1):
                tok = slice(c * 128, (c + 1) * 128)
                for g in range(2):
                    hs = slice(g * 8, (g + 1) * 8)
                    zi = zR.next()
                    zp = bk(zi).rearrange("p (j n) -> p j n", n=128)
                    for j in range(4):
                        for kc in range(8):
                            mm(zp[:, j, :], hT[:, kc, tok], wz[:, g * 4 + j, kc, :], kc == 0, kc == 7, ["wz", "hT"], ["bank%d" % zi])
                    act(sz, bk(zi), AF.Silu, ["bank%d" % zi], ["sz"])
                    CB = bk(1)[:, 256:384]
                    mm(CB, bmT[:, g, tok], cmT[:, g, tok], True, True, ["bmT", "cmT"], ["bank1"])
                    tt("dve", dATri, bc(Tri.unsqueeze(1), [128, 8, 128]), bc(dA[:, c, hs].unsqueeze(2), [128, 8, 128]), ALU.mult, ["cf", "dA"], ["dATri"])
                    for hf in range(2):
                        mm(bk(2 + hf), U_, dATri[:, hf * 4:(hf + 1) * 4, :].rearrange("p a b -> p (a b)"), True, False, ["cf", "dATri"], ["bank%d" % (2 + hf)])
                        mm(bk(2 + hf), ident, NEG4, False, True, ["cbf"], ["bank%d" % (2 + hf)])
                        act(LmT[:, hf * 4:(hf + 1) * 4, :].rearrange("p a b -> p (a b)"), bk(2 + hf), AF.Exp, ["bank%d" % (2 + hf)], ["LmT"])
                    for hh in range(8):
                        stt("dve", MT[:, hh, :], LmT[:, hh, :], dtt[:, c, g * 8 + hh:g * 8 + hh + 1], CB, ALU.mult, ALU.mult, ["LmT", "dtt", "bank1"], ["MT"])
                    Yd = bk(4).rearrange("p (a b) -> p a b", b=64)
                    for hh in range(8):
                        H = g * 8 + hh
                        xs_h = xs_tok[:, c, H * 64:(H + 1) * 64]
                        mm(Yd[:, hh, :], MT[:, hh, :], xs_h, True, False, ["MT", "xs_tok"], ["bank4"])
                        mm(Yd[:, hh, :], diagh[:, H, :], xs_h, False, False, ["diagh", "xs_tok"], ["bank4"])
                        mm(Yd[:, hh, :], diagl[:, H, :], xs_h, False, True, ["diagl", "xs_tok"], ["bank4"])
                    if c > 0:
                        Yo = bk(5).rearrange("p (a b) -> p a b", b=64)
                        mm(bk(5), cmT[:, g, tok], prevT[g].rearrange("p a b -> p (a b)"), True, True, ["cmT", "prevT%d" % g], ["bank5"])
                        tt("dve", u1, Yo, bc(expcs[:, c, hs].unsqueeze(2), [128, 8, 64]), ALU.mult, ["bank5", "expcs"], ["u1"])
                        tt("dve", u2, Yd, u1, ALU.add, ["bank4", "u1"], ["u2"])
                    else:
                        cp("dve", u2, Yd, ["bank4"], ["u2"])
                    tt("dve", u3, u2.rearrange("p a b -> p (a b)"), sz, ALU.mult, ["u2", "sz"], ["u3"])
                    c0 = 16 + 3 * (2 * c + g)
                    act(junk[:, 0:512], u3, AF.Square, ["u3", "sscol"], ["junk", "sscol"], accum=smallf[:, c0:c0 + 1])
                    act(smallf[:, c0 + 1:c0 + 2], smallf[:, c0:c0 + 1], AF.Ln, ["sscol"], ["sscol"], bias=EPS, scale=1.0 / 512)
                    act(smallf[:, c0 + 2:c0 + 3], smallf[:, c0 + 1:c0 + 2], AF.Exp, ["sscol"], ["sscol"], scale=-0.5)
                    stt("dve", yb, u3, smallf[:, c0 + 2:c0 + 3], ssdnw[:, g * 512:(g + 1) * 512], ALU.mult, ALU.mult, ["u3", "sscol", "ssdnw"], ["yb"])
                    dma(yd[b * S + c * 128: b * S + (c + 1) * 128, 1024 + g * 512: 1024 + (g + 1) * 512], yb, "yb", reads=["yb"], writes=[("yd", b)])
                    if c < NT - 1:
                        tt("pool", xdtp, xs_tok[:, c, g * 512:(g + 1) * 512].rearrange("p (a b) -> p a b", b=64),
                           bc(wds[:, c, hs].unsqueeze(2), [128, 8, 64]), ALU.mult, ["xs_tok", "wds"], ["xdtp"])
                        mm(bk(6), bm_tok[:, c, g * 128:(g + 1) * 128], xdtp.rearrange("p a b -> p (a b)"), True, True, ["bm_tok", "xdtp"], ["bank6"])
                        STv = bk(6).rearrange("p (a b) -> p a b", b=64)
                        if c == 0:
                            cp("dve", hst[g], STv, ["bank6"], ["hst%d" % g])
                        else:
                            tt("dve", hst[g], hst[g], bc(cdec[:, c, hs].unsqueeze(2), [128, 8, 64]), ALU.mult, ["hst%d" % g, "cdec"], ["hst%d" % g])
                            tt("dve", hst[g], hst[g], STv, ALU.add, ["hst%d" % g, "bank6"], ["hst%d" % g])
                        cp("act", prevT[g], hst[g], ["hst%d" % g], ["prevT%d" % g])

        def phase_out(b, layer):
            AR.reset()
            KC = 16 if layer == 0 else 8
            wdram = w_out_a if layer == 0 else w_out_c
            ysrc = yd if layer == 0 else y2d
            xsrc = x if layer == 0 else x1d
            wout = AR.alloc([KC, D], BF16)
            for csl in range(8):
                for kh in range(KC // 8):
                    load_w(wout[:, kh * 8:(kh + 1) * 8, csl * 128:(csl + 1) * 128], "wout", wdram[kh * 1024:(kh + 1) * 1024, csl * 128:(csl + 1) * 128])
            yts = [AR.alloc([KC * 128], BF16) for _ in range(2)]
            yTts = [AR.alloc([KC, 128], BF16) for _ in range(2)]
            x1ts = [AR.alloc([D], F32) for _ in range(2)]
            ots = [AR.alloc([D], F32) for _ in range(2)]
            xts = [AR.alloc([D], F32) for _ in range(2)]
            xtR = Rot([(xts[i], "xt%d" % i) for i in range(2)])
            hbs = [AR.alloc([D], BF16) for _ in range(2)]
            dma(normw[:], (final_norm if layer == 1 else norm_c).partition_broadcast(128), "normw", writes=["normw"])
            sscol = smallf[:, 16:16 + 3 * NT]
            memset("pool", sscol, 0.0, ["sscol"])
            trR = Rot([0, 1, 2, 3])
            mmR = Rot([4, 5, 6, 7])
            for t in range(NT):
                s2 = t % 2
                rows = slice(b * S + t * 128, b * S + (t + 1) * 128)
                dma(yts[s2], ysrc[rows, :], "yt%d" % s2, reads=[("yd", b)], writes=["yt%d" % s2])
                xt, xres = xtR.next()
                dma(xt, xsrc[rows, :], xres, reads=[("x1d", b)] if layer == 1 else [], writes=[xres])
                for k0 in range(0, KC, 8):
                    bi = trR.next()
                    pv = bkbf(bi).rearrange("p (c n) -> p c n", n=128)
                    for k in range(8):
                        tr(pv[:, k, :], yts[s2][:, (k0 + k) * 128:(k0 + k + 1) * 128], ["yt%d" % s2], ["bank%d" % bi])
                    cp(evac_rr.next(), yTts[s2][:, k0:k0 + 8, :], pv, ["bank%d" % bi], ["yTt%d" % s2])
                for hf in range(2):
                    bi = mmR.next()
                    for k in range(KC):
                        mm(bk(bi), yTts[s2][:, k, :], wout[:, k, hf * 512:(hf + 1) * 512], k == 0, k == KC - 1, ["yTt%d" % s2, "wout"], ["bank%d" % bi])
                    tt("dve", x1ts[s2][:, hf * 512:(hf + 1) * 512], xt[:, hf * 512:(hf + 1) * 512], bk(bi), ALU.add, [xres, "bank%d" % bi], ["x1t%d" % s2])
                c0 = 16 + 3 * t
                act(junk[:], x1ts[s2], AF.Square, ["x1t%d" % s2, "sscol"], ["junk", "sscol"], accum=smallf[:, c0:c0 + 1])
                act(smallf[:, c0 + 1:c0 + 2], smallf[:, c0:c0 + 1], AF.Ln, ["sscol"], ["sscol"], bias=EPS, scale=1.0 / D)
                act(smallf[:, c0 + 2:c0 + 3], smallf[:, c0 + 1:c0 + 2], AF.Exp, ["sscol"], ["sscol"], scale=-0.5)
                if layer == 0:
                    dma(x1d[rows, :], x1ts[s2], "x1w%d" % s2, reads=["x1t%d" % s2], writes=[("x1d", b)])
                    hb = hbs[s2]
                    stt("dve", hb, x1ts[s2], smallf[:, c0 + 2:c0 + 3], normw[:], ALU.mult, ALU.mult, ["x1t%d" % s2, "sscol", "normw"], ["hb%d" % s2])
                    bi = trR.next()
                    pvn = bkbf(bi).rearrange("p (c n) -> p c n", n=128)
                    for c in range(8):
                        tr(pvn[:, c, :], hb[:, c * 128:(c + 1) * 128], ["hb%d" % s2], ["bank%d" % bi])
                    cp(evac_rr.next(), hT[:, :, t * 128:(t + 1) * 128], pvn, ["bank%d" % bi], [("hTw", t)])
                else:
                    stt("dve", ots[s2], x1ts[s2], smallf[:, c0 + 2:c0 + 3], normw[:], ALU.mult, ALU.mult, ["x1t%d" % s2, "sscol", "normw"], ["ot%d" % s2])
                    final.append(dma(out[rows, :], ots[s2], "ow%d" % s2, reads=["ot%d" % s2]))

        def phase_swa(b):
            AR.reset()
            swab = AR.alloc([16, 4, 128], BF16)
            dma(swab.rearrange("p a b c -> p (a b c)"), c_swa, "swab", writes=["swab"])
            wk = [AR.alloc([8, 128], BF16) for _ in range(2)]
            wqg = [AR.alloc([8, 128], BF16) for _ in range(4)]
            wv = AR.alloc([8, 128], BF16)
            kz = [[AR.alloc([S], BF16) for _ in range(2)] for _ in range(2)]
            for kv_ in range(2):
                memset("pool", kz[kv_][0], 0.0, ["kzero"])
                memset("pool", kz[kv_][1], 0.0, ["kzero"])
            vaug = AR.alloc([NT, 2, 65], BF16)
            qT = AR.alloc([S], BF16); sg = AR.alloc([NT, 128], BF16); gtmp = AR.alloc([4, 128], F32)
            PTs = [AR.alloc([2, 2, 128], BF16) for _ in range(2)]
            PTR = Rot([(PTs[i], "PT%d" % i) for i in range(2)])
            den = smallf[:, 400:404]
            of = AR.alloc([2, 64], F32); obf = [AR.alloc([128], BF16) for _ in range(2)]
            memset("pool", vaug[:, :, :, 64:65], 1.0, ["vaug"])
            bvg = AR.alloc([128 + D], F32)
            dma(bvg[:, 0:128], b_in_c[1152:1280].partition_broadcast(128), "bvg0", writes=["bvg0"])
            dma(bvg[:, 128:128 + D], b_in_c[1280:2304].partition_broadcast(128), "bvg1", writes=["bvg1"])
            pjR = Rot([6, 7])
            for kv in range(2):
                s_t, s_res = stgR.next()
                for hf in range(2):
                    dma(s_t[:, :, hf * 64:(hf + 1) * 64], w_in_c[:, 1024 + kv * 64:1024 + (kv + 1) * 64].rearrange("(c p) n -> p c n", p=128),
                        s_res, reads=([s_res] if hf == 1 else []), writes=[s_res])
                cp("pool", wk[kv], s_t[:], [s_res], ["wk%d" % kv])
                for n in range(S // 512):
                    bi = pjR.next()
                    for c in range(8):
                        mm(bk(bi), wk[kv][:, c, :], hT[:, c, n * 512:(n + 1) * 512], c == 0, c == 7, ["wk%d" % kv, "hT"], ["bank%d" % bi])
                    for i2 in range(2):
                        hs2 = slice(i2 * 64, (i2 + 1) * 64)
                        act(kz[kv][i2][hs2, n * 512:(n + 1) * 512], bk(bi)[hs2, :], AF.Identity, ["bank%d" % bi, "bcol", "kzero"], [("kT2_%d" % kv, n)],
                            bias=bcol[hs2, 8 + kv:9 + kv])
            load_w(wv, "wv", w_in_c[:, 1152:1280])
            for t0 in range(0, NT, 4):
                bi = pjR.next()
                pv = proj_tm4(t0, wv, "wv", bi)
                tt("dve", vaug[:, t0:t0 + 4, :, 0:64], pv.rearrange("p j (k e) -> p j k e", e=64),
                   bc(bvg[:, 0:128].rearrange("p (k e) -> p k e", e=64).unsqueeze(1), [128, 4, 2, 64]), ALU.add, ["bank%d" % bi, "bvg0"], ["vaug"])
            LR = Rot([0, 1, 2])
            OR = Rot([3, 4])
            trR = Rot([5])
            def load_pair_w(j):
                ws_ = j % 2
                load_w(wqg[ws_ * 2], "wqg%d" % (ws_ * 2), w_in_c[:, j * 128:(j + 1) * 128])
                load_w(wqg[ws_ * 2 + 1], "wqg%d" % (ws_ * 2 + 1), w_in_c[:, 1280 + j * 128:1280 + (j + 1) * 128])

            load_pair_w(0)
            for j in range(8):
                kv = j // 4
                ws = j % 2
                proj_fm(qT, "qT", wqg[ws * 2], "wqg%d" % (ws * 2), pjR, bias_col=bcol[:, j:j + 1])
                for t0 in range(0, NT, 4):
                    bi = pjR.next()
                    pv = proj_tm4(t0, wqg[ws * 2 + 1], "wqg%d" % (ws * 2 + 1), bi)
                    tt("dve", gtmp, pv, bc(bvg[:, 128 + j * 128:128 + (j + 1) * 128].unsqueeze(1), [128, 4, 128]), ALU.add, ["bank%d" % bi, "bvg1"], ["gtmp"])
                    act(sg[:, t0:t0 + 4, :], gtmp, AF.Silu, ["gtmp"], ["sg"])
                if j < 7:
                    load_pair_w(j + 1)
                for n in range(NT):
                    li = LR.next(); Lres = "bank%d" % li
                    Lv = bk(li).rearrange("p (i k q) -> p i k q", k=2, q=128)
                    kts = (1,) if n == 0 else (0, 1)
                    for i in range(2):
                        hq = 2 * j + i
                        rows = slice(i * 64, (i + 1) * 64)
                        for kt in kts:
                            tk = n - 1 + kt
                            mm(Lv[:, i, kt, :], kz[kv][i][:, tk * 128:(tk + 1) * 128], qT[:, n * 128:(n + 1) * 128], True, False, [("kT2_%d" % kv, tk // 4), ("qT", n // 4)], [Lres])
                            mm(Lv[:, i, kt, :], ident, swab[:, hq, kt * 2 + 0, :], False, False, ["cbf", "swab"], [Lres])
                            mm(Lv[:, i, kt, :], ident, swab[:, hq, kt * 2 + 1, :], False, True, ["cbf", "swab"], [Lres])
                    pt, ptres = PTR.next()
                    if n == 0:
                        act(pt[:, :, 1, :], Lv[:, :, 1, :], AF.Exp, [Lres], [ptres], scale=0.125)
                    else:
                        act(pt, Lv, AF.Exp, [Lres], [ptres], scale=0.125)
                    oi = OR.next(); Ores = "bank%d" % oi
                    Ov = bk(oi)[:, 0:130].rearrange("p (i e) -> p i e", e=65)
                    for i in range(2):
                        for kt in kts:
                            tk = n - 1 + kt
                            mm(Ov[:, i, :], pt[:, i, kt, :], vaug[:, tk, kv, :], kt == kts[0], kt == kts[-1], [ptres, "vaug"], [Ores])
                    tt("dve", den[:, 0:2], Ov[:, :, 64:65].rearrange("p i e -> p (i e)"), esink_bc[:, 2 * j:2 * j + 2], ALU.add, [Ores, "sp16"], ["den"])
                    P.op("dve", lambda e: e.reciprocal(out=den[:, 2:4], in_=den[:, 0:2]), ["den"], ["den"])
                    tt("dve", of, Ov[:, :, 0:64], bc(den[:, 2:4].unsqueeze(2), [128, 2, 64]), ALU.mult, [Ores, "den"], ["of"])
                    ob2 = obf[n % 2]; obres = "obf%d" % (n % 2)
                    tt("pool", ob2, of.rearrange("p a b -> p (a b)"), sg[:, n, :], ALU.mult, ["of", "sg"], [obres])
                    dma(y2d[b * S + n * 128: b * S + (n + 1) * 128, j * 128:(j + 1) * 128], ob2, obres, reads=[obres], writes=[("yd", b)])

        import os
        KSTOP = os.environ.get("KSTOP", "all")
        order = ["norm0", "attn", "ssd", "out0", "norm1", "swa", "out1"]
        nphase = len(order) if KSTOP == "all" else (0 if KSTOP == "setup" else order.index(KSTOP) + 1)
        for b in range(NSEQ):
            for ph in order[:nphase]:
                if ph == "norm0":
                    phase_norm(b, x, norm_a, "a")
                elif ph == "attn":
                    phase_attn(b)
                elif ph == "ssd":
                    phase_ssd(b)
                elif ph == "out0":
                    phase_out(b, 0)
                elif ph == "norm1":
                    continue
                elif ph == "swa":
                    phase_swa(b)
                elif ph == "out1":
                    phase_out(b, 1)
                P.barrier()
        counts = P.emit(final)
        print("ops:", counts, "sbuf left", nc.sbuf_bytes_remaining)
    return nc


PARAMS = ["norm_a", "w_in_a", "conv_w_a", "conv_b_a", "dt_bias_a", "a_log_a", "d_skip_a", "ssd_norm_a",
          "lambda_q1_a", "lambda_k1_a", "lambda_q2_a", "lambda_k2_a", "subln_a", "w_out_a",
          "norm_c", "w_in_c", "b_in_c", "sinks_c", "w_out_c", "final_norm"]

_CACHE = {}


def run(inputs, n_cores=8):
    x = np.ascontiguousarray(np.asarray(inputs["x"], dtype=np.float32))
    B, S, _ = x.shape
    assert B % n_cores == 0
    NSEQ = B // n_cores
    key = (S, NSEQ)
    if key not in _CACHE:
        _CACHE[key] = build(S, NSEQ)
    nc = _CACHE[key]
    c_bf, c_f32, c_swa = make_consts()
    base = {"c_bf": c_bf, "c_f32": c_f32, "c_swa": c_swa}
    for k in PARAMS:
        a = np.asarray(inputs[k], dtype=np.float32)
        if k != "final_norm":
            a = a[0]
        base[k] = np.ascontiguousarray(a)
    in_maps = []
    for i in range(n_cores):
        m = dict(base)
        m["x"] = x[i * NSEQ:(i + 1) * NSEQ].reshape(NSEQ * S, D)
        in_maps.append(m)
    res = run_bass_kernel_spmd(nc, in_maps, core_ids=list(range(n_cores)))
    import os
    if os.environ.get("KDBG"):
        global DBG
        DBG = [{k: np.asarray(v) for k, v in r.items()} for r in res.results]
    outs = [np.asarray(r["out"]).reshape(NSEQ, S, D) for r in res.results]
    return np.concatenate(outs, axis=0).astype(np.float32)


def kernel(**inputs):
    return run(inputs, 8)
```
